# Optimizing a Trainium2 kernel written in Bass

```python
import jax, jax.numpy as jnp
from jax import lax
import numpy as np

D_MODEL = 1024
BATCH = 2
SEQ = 16384
DEPTH = 1
DEC_BATCH = 8
DEC_SEQ = 64
PAST_LEN = 1024

CHUNK = 64
Q_BLOCK = 128
N_HEADS = 8
QK_NOPE = 64
QK_ROPE = 32
QK_HEAD = QK_NOPE + QK_ROPE
V_HEAD = 64
Q_LORA = 384
KV_LORA = 256
ROPE_THETA = 10000.0
D_RNN = D_MODEL
RNN_BLOCKS = 8
RNN_BLOCK = D_RNN // RNN_BLOCKS
CONV_W = 4
LRU_C = 8.0
D_FF = 2816
EPS = 1e-6
OFF_Q = 0
OFF_KV = OFF_Q + Q_LORA
OFF_KR = OFF_KV + KV_LORA
OFF_RX = OFF_KR + QK_ROPE
OFF_RG = OFF_RX + D_RNN
OFF_GA = OFF_RG + D_RNN
OFF_GB = OFF_GA + D_MODEL
D_IN = OFF_GB + D_MODEL

kernel_name = "streaming_mla_rglru_macaron_step"


def rmsnorm(x, g):
    xf = x.astype(jnp.float32)
    y = xf * lax.rsqrt(jnp.mean(xf * xf, axis=-1, keepdims=True) + EPS)
    return (y * g.astype(jnp.float32)).astype(x.dtype)


def swiglu_half(x, g, w1, w3, w2):
    h = rmsnorm(x, g)
    return x + 0.5 * ((jax.nn.silu(h @ w1) * (h @ w3)) @ w2)


def rope_tables(pos):
    inv = 1.0 / (ROPE_THETA ** (jnp.arange(0, QK_ROPE, 2, dtype=jnp.float32) / QK_ROPE))
    ang = pos.astype(jnp.float32)[:, None] * inv[None, :]
    return jnp.cos(ang), jnp.sin(ang)


def apply_rope(x, cos, sin):
    x1, x2 = jnp.split(x.astype(jnp.float32), 2, axis=-1)
    return jnp.concatenate([x1 * cos - x2 * sin, x2 * cos + x1 * sin], axis=-1).astype(x.dtype)


def mla_qkr(z, pos, p):
    B, S = z.shape[:2]
    cos, sin = rope_tables(pos)
    q = (rmsnorm(z[..., OFF_Q:OFF_KV], p["norm_q"]) @ p["w_uq"]).reshape(B, S, N_HEADS, QK_HEAD)
    q = jnp.concatenate([q[..., :QK_NOPE],
                         apply_rope(q[..., QK_NOPE:], cos[None, :, None], sin[None, :, None])], axis=-1)
    c_kv = rmsnorm(z[..., OFF_KV:OFF_KR], p["norm_kv"])
    k_r = apply_rope(z[..., OFF_KR:OFF_RX], cos[None], sin[None])
    return q, c_kv, k_r


def mla_keys(c_kv, k_r, p):
    B, T = c_kv.shape[:2]
    kv = (c_kv @ p["w_ukv"]).reshape(B, T, N_HEADS, QK_NOPE + V_HEAD)
    k = jnp.concatenate([kv[..., :QK_NOPE],
                         jnp.broadcast_to(k_r[:, :, None, :], (B, T, N_HEADS, QK_ROPE))], axis=-1)
    return k, kv[..., QK_NOPE:]


def attend(q, k, v, mask):
    s = jnp.einsum("bqhd,bkhd->bhqk", q, k).astype(jnp.float32) * (QK_HEAD ** -0.5)
    if mask is not None:
        s = jnp.where(mask, s, -jnp.inf)
    pr = jax.nn.softmax(s, axis=-1).astype(v.dtype)
    return jnp.einsum("bhqk,bkhd->bqhd", pr, v)


def chunk_causal_attention(q, k, v):
    B, S = q.shape[:2]
    k_chunk = jnp.arange(S) // CHUNK

    def one_block(i):
        start = i * Q_BLOCK
        qb = lax.dynamic_slice_in_dim(q, start, Q_BLOCK, axis=1)
        q_chunk = (start + jnp.arange(Q_BLOCK)) // CHUNK
        mask = k_chunk[None, :] <= q_chunk[:, None]
        return attend(qb, k, v, mask[None, None])

    o = lax.map(one_block, jnp.arange(S // Q_BLOCK))
    return o.transpose(1, 0, 2, 3, 4).reshape(B, S, N_HEADS, V_HEAD)


def rglru_branch(xr, conv_buf, h0, reset, p):
    B, S = xr.shape[:2]
    xp = jnp.concatenate([conv_buf.astype(xr.dtype), xr], axis=1)
    xc = p["conv_b"] + sum(xp[:, j:j + S] * p["conv_w"][j] for j in range(CONV_W))
    new_buf = xp[:, S:]
    xb = xc.reshape(B, S, RNN_BLOCKS, RNN_BLOCK)
    r = jax.nn.sigmoid(jnp.einsum("bsni,nij->bsnj", xb, p["w_rgate"]).reshape(B, S, D_RNN) + p["b_rgate"])
    ig = jax.nn.sigmoid(jnp.einsum("bsni,nij->bsnj", xb, p["w_igate"]).reshape(B, S, D_RNN) + p["b_igate"])
    log_a = -LRU_C * r.astype(jnp.float32) * jax.nn.softplus(-p["lru_lambda"].astype(jnp.float32))
    rs = reset[None, :, None]
    a = jnp.where(rs, 0.0, jnp.exp(log_a))
    mult = jnp.where(rs, 1.0, jnp.sqrt(-jnp.expm1(2.0 * log_a)))
    bterm = mult * (ig * xc).astype(jnp.float32)
    bterm = bterm.at[:, 0].add(a[:, 0] * h0.astype(jnp.float32))

    def combine(lhs, rhs):
        a1, b1 = lhs
        a2, b2 = rhs
        return a1 * a2, a2 * b1 + b2

    _, h = lax.associative_scan(combine, (a, bterm), axis=1)
    return h.astype(xr.dtype), new_buf, h[:, -1].astype(xr.dtype)


def layer(x, pos, reset, cache_c, cache_kr, conv_buf, h0, p):
    B, S = x.shape[:2]
    x = swiglu_half(x, p["norm_ffn1"], p["w1_ffn1"], p["w3_ffn1"], p["w2_ffn1"])
    u = rmsnorm(x, p["norm_mix"])
    z = u @ p["w_in"]
    q, c_new, kr_new = mla_qkr(z, pos, p)
    if cache_c is None:
        k, v = mla_keys(c_new, kr_new, p)
        o = chunk_causal_attention(q, k, v)
    else:
        k, v = mla_keys(jnp.concatenate([cache_c.astype(c_new.dtype), c_new], axis=1),
                        jnp.concatenate([cache_kr.astype(kr_new.dtype), kr_new], axis=1), p)
        o = attend(q, k, v, None)
    y_attn = o.reshape(B, S, N_HEADS * V_HEAD) @ p["w_o_attn"]
    h, new_buf, h_last = rglru_branch(z[..., OFF_RX:OFF_RG], conv_buf, h0, reset, p)
    y_rnn = (h * jax.nn.gelu(z[..., OFF_RG:OFF_GA], approximate=True)) @ p["w_o_rnn"]
    m = jax.nn.sigmoid(z[..., OFF_GA:OFF_GB]) * y_attn + jax.nn.sigmoid(z[..., OFF_GB:D_IN]) * y_rnn
    x = x + m @ p["w_out"]
    x = swiglu_half(x, p["norm_ffn2"], p["w1_ffn2"], p["w3_ffn2"], p["w2_ffn2"])
    return x, c_new, kr_new, new_buf, h_last


def setup_inputs(seed: int = 0) -> dict:
    key = jax.random.key(seed)
    ks = iter(jax.random.split(key, 48))
    f32 = jnp.float32
    L = DEPTH

    def w(shape, fan_in):
        return jax.random.normal(next(ks), shape, f32) * fan_in ** -0.5

    def gain(shape):
        return 1.0 + 0.01 * jax.random.normal(next(ks), shape, f32)

    def bias(shape):
        return 0.01 * jax.random.normal(next(ks), shape, f32)

    x_prompt = jax.random.normal(next(ks), (BATCH, SEQ, D_MODEL), f32)
    x_sample = jax.random.normal(next(ks), (DEC_BATCH, DEC_SEQ, D_MODEL), f32)
    cache_kv_latent = jax.random.normal(next(ks), (L, DEC_BATCH, PAST_LEN, KV_LORA), f32)
    cache_k_rope = jax.random.normal(next(ks), (L, DEC_BATCH, PAST_LEN, QK_ROPE), f32)
    state_conv = jax.random.normal(next(ks), (L, DEC_BATCH, CONV_W - 1, D_RNN), f32)
    state_rglru = 0.5 * jax.random.normal(next(ks), (L, DEC_BATCH, D_RNN), f32)
    a_target = jax.random.uniform(next(ks), (L, D_RNN), f32, 0.9, 0.999)
    a_base = a_target ** (1.0 / LRU_C)
    lru_lambda = jnp.log(a_base) - jnp.log1p(-a_base)
    return {
        "x_prompt": x_prompt, "x_sample": x_sample,
        "cache_kv_latent": cache_kv_latent, "cache_k_rope": cache_k_rope,
        "state_conv": state_conv, "state_rglru": state_rglru,
        "norm_ffn1": gain((L, D_MODEL)),
        "w1_ffn1": w((L, D_MODEL, D_FF), D_MODEL), "w3_ffn1": w((L, D_MODEL, D_FF), D_MODEL),
        "w2_ffn1": w((L, D_FF, D_MODEL), D_FF),
        "norm_mix": gain((L, D_MODEL)), "w_in": w((L, D_MODEL, D_IN), D_MODEL),
        "norm_q": gain((L, Q_LORA)), "w_uq": w((L, Q_LORA, N_HEADS * QK_HEAD), Q_LORA),
        "norm_kv": gain((L, KV_LORA)), "w_ukv": w((L, KV_LORA, N_HEADS * (QK_NOPE + V_HEAD)), KV_LORA),
        "w_o_attn": w((L, N_HEADS * V_HEAD, D_MODEL), N_HEADS * V_HEAD),
        "conv_w": w((L, CONV_W, D_RNN), CONV_W), "conv_b": bias((L, D_RNN)),
        "w_rgate": w((L, RNN_BLOCKS, RNN_BLOCK, RNN_BLOCK), RNN_BLOCK), "b_rgate": bias((L, D_RNN)),
        "w_igate": w((L, RNN_BLOCKS, RNN_BLOCK, RNN_BLOCK), RNN_BLOCK), "b_igate": bias((L, D_RNN)),
        "lru_lambda": lru_lambda, "w_o_rnn": w((L, D_RNN, D_MODEL), D_RNN),
        "w_out": w((L, D_MODEL, D_MODEL), D_MODEL),
        "norm_ffn2": gain((L, D_MODEL)),
        "w1_ffn2": w((L, D_MODEL, D_FF), D_MODEL), "w3_ffn2": w((L, D_MODEL, D_FF), D_MODEL),
        "w2_ffn2": w((L, D_FF, D_MODEL), D_FF),
        "norm_final": gain((D_MODEL,)),
    }


def reference(x_prompt, x_sample, cache_kv_latent, cache_k_rope, state_conv, state_rglru,
              norm_ffn1, w1_ffn1, w3_ffn1, w2_ffn1, norm_mix, w_in,
              norm_q, w_uq, norm_kv, w_ukv, w_o_attn,
              conv_w, conv_b, w_rgate, b_rgate, w_igate, b_igate, lru_lambda, w_o_rnn,
              w_out, norm_ffn2, w1_ffn2, w3_ffn2, w2_ffn2, norm_final):
    Bp, Sp = x_prompt.shape[:2]
    Bs, Ss = x_sample.shape[:2]
    past = cache_kv_latent.shape[2]
    pos_p = jnp.arange(Sp)
    pos_s = past + jnp.arange(Ss)
    reset_p = pos_p == 0
    reset_s = jnp.zeros((Ss,), dtype=bool)
    hp, hs = x_prompt, x_sample
    cp, krp, cbp, hlp = [], [], [], []
    cs, krs, cbs, hls = [], [], [], []
    for l in range(DEPTH):
        p = dict(norm_ffn1=norm_ffn1[l], w1_ffn1=w1_ffn1[l], w3_ffn1=w3_ffn1[l], w2_ffn1=w2_ffn1[l],
                 norm_mix=norm_mix[l], w_in=w_in[l], norm_q=norm_q[l], w_uq=w_uq[l],
                 norm_kv=norm_kv[l], w_ukv=w_ukv[l], w_o_attn=w_o_attn[l],
                 conv_w=conv_w[l], conv_b=conv_b[l], w_rgate=w_rgate[l], b_rgate=b_rgate[l],
                 w_igate=w_igate[l], b_igate=b_igate[l], lru_lambda=lru_lambda[l], w_o_rnn=w_o_rnn[l],
                 w_out=w_out[l], norm_ffn2=norm_ffn2[l], w1_ffn2=w1_ffn2[l], w3_ffn2=w3_ffn2[l],
                 w2_ffn2=w2_ffn2[l])
        zero_buf = jnp.zeros((Bp, CONV_W - 1, D_RNN), x_prompt.dtype)
        zero_h = jnp.zeros((Bp, D_RNN), x_prompt.dtype)
        hp, c1, k1, b1, h1 = layer(hp, pos_p, reset_p, None, None, zero_buf, zero_h, p)
        hs, c2, k2, b2, h2 = layer(hs, pos_s, reset_s, cache_kv_latent[l], cache_k_rope[l],
                                   state_conv[l], state_rglru[l], p)
        cp.append(c1); krp.append(k1); cbp.append(b1); hlp.append(h1)
        cs.append(c2); krs.append(k2); cbs.append(b2); hls.append(h2)
    y_prompt = rmsnorm(hp, norm_final)
    y_sample = rmsnorm(hs, norm_final)
    return (y_prompt, y_sample,
            jnp.stack(cp), jnp.stack(krp), jnp.stack(cbp), jnp.stack(hlp),
            jnp.stack(cs), jnp.stack(krs), jnp.stack(cbs), jnp.stack(hls))
```

```python
import os
import bisect
import numpy as np
import concourse.bass as bass
import concourse.mybir as mybir
from concourse.bass_utils import run_bass_kernel_spmd

F32 = mybir.dt.float32
BF16 = mybir.dt.bfloat16
AF = mybir.ActivationFunctionType
ALU = mybir.AluOpType

D = 1024
DFF = 2816
NF = 22
EPS = 1e-6
TT = 512
PAST = 1024
DEC = 64
QSCALE = 96 ** -0.5
SAFE_DIST = 3


class StopBuild(Exception):
    pass


class Ctr:
    def __init__(self, fw, name):
        self.sem = fw.nc.alloc_semaphore(name)
        self.count = 0
        self.hist_t = []
        self.hist_k = []

    def snap(self, t, known):
        if self.hist_k and self.hist_k[-1] == known:
            return
        self.hist_t.append(t)
        self.hist_k.append(dict(known))

    def known_at(self, t):
        i = bisect.bisect_right(self.hist_t, t) - 1
        return self.hist_k[i] if i >= 0 else None


class Eng:
    def __init__(self, fw, name, eng):
        self.name = name
        self.eng = eng
        self.ctr = Ctr(fw, "c_" + name)
        self.known = {}
        self.n_issued = 0
        self.ticket_pos = {}


class Buf:
    __slots__ = ("name", "last_w", "reads")

    def __init__(self, name):
        self.name = name
        self.last_w = None
        self.reads = []


def _compress(reads):
    d = {}
    for c, t in reads:
        if d.get(c, 0) < t:
            d[c] = t
    return list(d.items())


class FW:
    def __init__(self, nc):
        self.nc = nc
        self.pe = Eng(self, "pe", nc.tensor)
        self.act = Eng(self, "act", nc.scalar)
        self.dve = Eng(self, "dve", nc.vector)
        self.pool = Eng(self, "pool", nc.gpsimd)
        self.sp = Eng(self, "sp", nc.sync)
        self.engs = [self.pe, self.act, self.dve, self.pool, self.sp]
        self.dma_ctrs = []
        self.dma_set = set()
        self.n_wait = 0
        self.n_inst = 0

    def dma_ctr(self, name):
        c = Ctr(self, name)
        self.dma_ctrs.append(c)
        self.dma_set.add(c)
        return c

    def _deps(self, reads, writes):
        deps = {}
        for b in reads:
            if b.last_w is not None:
                c, t = b.last_w
                if deps.get(c, 0) < t:
                    deps[c] = t
        for b in writes:
            if b.last_w is not None:
                c, t = b.last_w
                if deps.get(c, 0) < t:
                    deps[c] = t
            for c, t in b.reads:
                if deps.get(c, 0) < t:
                    deps[c] = t
        return deps

    def _wait(self, E, deps):
        for c, t in deps.items():
            if c in self.dma_set:
                t = c.count
            if c is E.ctr:
                if E is self.pe:
                    continue
            if E.known.get(c, 0) >= t:
                continue
            E.eng.wait_ge(c.sem, t)
            E.known[c] = t
            self.n_wait += 1
            k2 = c.known_at(t)
            if k2:
                for c2, t2 in k2.items():
                    if E.known.get(c2, 0) < t2:
                        E.known[c2] = t2

    def _record(self, ctr, t, reads, writes):
        for b in reads:
            b.reads.append((ctr, t))
            if len(b.reads) > 16:
                b.reads = _compress(b.reads)
        for b in writes:
            b.last_w = (ctr, t)
            b.reads = []

    def op(self, E, fn, reads=(), writes=()):
        self._wait(E, self._deps(reads, writes))
        inst = fn()
        E.ctr.count += 1
        t = E.ctr.count
        E.ctr.snap(t, E.known)
        inst.then_inc(E.ctr.sem, 1)
        E.ticket_pos[t] = E.n_issued
        E.n_issued += 1
        self.n_inst += 1
        if len(E.ticket_pos) > 64:
            for k in sorted(E.ticket_pos)[:32]:
                del E.ticket_pos[k]
        self._record(E.ctr, t, reads, writes)
        return inst

    def dma(self, Q, ctr, out, in_, reads=(), writes=(), **kw):
        self._wait(Q, self._deps(reads, writes))
        inst = Q.eng.dma_start(out=out, in_=in_, **kw)
        ctr.count += 16
        ctr.snap(ctr.count, Q.known)
        inst.then_inc(ctr.sem, 16)
        Q.n_issued += 1
        self.n_inst += 1
        self._record(ctr, ctr.count, reads, writes)
        return inst

    def finish(self, E):
        for F in self.engs:
            if F is not E and F.ctr.count > 0:
                E.eng.wait_ge(F.ctr.sem, F.ctr.count)
        for c in self.dma_ctrs:
            if c.count > 0:
                E.eng.wait_ge(c.sem, c.count)


NPAR = 101
PG1, PGM, PGQ, PGKV, PG2, PGF, PCW, PCB, PBR, PBI, PLAM = 0, 8, 16, 19, 21, 29, 37, 69, 77, 85, 93


def _pc(v, nchunk):
    return np.ascontiguousarray(np.asarray(v, np.float32).reshape(nchunk, 128).T)


def host_params(w):
    cols = [_pc(w["norm_ffn1"][0], 8), _pc(w["norm_mix"][0], 8), _pc(w["norm_q"][0], 3),
            _pc(w["norm_kv"][0], 2), _pc(w["norm_ffn2"][0], 8), _pc(w["norm_final"], 8)]
    cw = np.asarray(w["conv_w"][0], np.float32).reshape(4, 8, 128).transpose(2, 1, 0).reshape(128, 32)
    cols += [cw, _pc(w["conv_b"][0], 8), _pc(w["b_rgate"][0], 8), _pc(w["b_igate"][0], 8),
             _pc(w["lru_lambda"][0], 8)]
    p = np.concatenate(cols, axis=1)
    assert p.shape == (128, NPAR)
    return np.ascontiguousarray(p, dtype=np.float32)


def host_w13(w1, w3):
    a = np.asarray(w1, np.float32).reshape(8, 128, 11, 256)
    b = np.asarray(w3, np.float32).reshape(8, 128, 11, 256)
    s = np.stack([a, b], 0)
    return np.ascontiguousarray(s.transpose(3, 2, 0, 1, 4)).reshape(11 * 128, 4096)


def host_w2(w2):
    a = np.asarray(w2, np.float32).reshape(22, 128, 8, 128)
    return np.ascontiguousarray(a.transpose(2, 1, 0, 3)).reshape(8 * 128, 2816)


def _colblk(cols):
    n = cols.shape[1]
    out = np.zeros((128, 8, 512), np.float32)
    out[:, :, :n] = cols.reshape(8, 128, n).transpose(1, 0, 2)
    return out.reshape(128, 4096)


def host_win(w_in):
    w = np.asarray(w_in, np.float32)
    q = w[:, 0:384]
    kv = w[:, 384:640]
    kr = w[:, 640:672]
    krs = np.concatenate([kr[:, 16:32], kr[:, 0:16]], axis=1)
    rx = w[:, 672:1696]
    rg = w[:, 1696:2720]
    ga = w[:, 2720:3744]
    gb = w[:, 3744:4768]
    blks = [_colblk(np.concatenate([q, kr, krs], 1)), _colblk(kv)]
    for c2 in range(4):
        s = slice(c2 * 256, c2 * 256 + 256)
        blks.append(_colblk(np.concatenate([rx[:, s], rg[:, s]], 1)))
    for c2 in range(4):
        s = slice(c2 * 256, c2 * 256 + 256)
        blks.append(_colblk(np.concatenate([ga[:, s], gb[:, s]], 1)))
    return np.ascontiguousarray(np.stack(blks, 0)).reshape(10 * 128, 4096)


NRES = 3072 + 2048 + 2048


def host_wres(w_uq, w_ukv, w_rg, w_ig):
    uq = np.asarray(w_uq, np.float32).reshape(3, 128, 8, 96)
    nope = uq[..., 0:64]
    rope = uq[..., 64:96]
    sw = np.concatenate([rope[..., 16:32], rope[..., 0:16]], -1)
    a = np.concatenate([rope, nope, sw], -1).transpose(1, 0, 2, 3).reshape(128, 3072)
    ukv = np.asarray(w_ukv, np.float32).reshape(2, 128, 8, 128)
    kpart = ukv[..., 0:64].reshape(2, 128, 512)
    vpart = ukv[..., 64:128].reshape(2, 128, 512)
    b = np.stack([kpart, vpart], 0).transpose(2, 0, 1, 3).reshape(128, 2048)
    g = np.stack([np.asarray(w_rg, np.float32), np.asarray(w_ig, np.float32)], 0)
    c = g.transpose(2, 1, 0, 3).reshape(128, 2048)
    return np.ascontiguousarray(np.concatenate([a, b, c], 1))


def host_kc(wm, ncols_blk):
    wm = np.asarray(wm, np.float32)
    K, N = wm.shape
    a = wm.reshape(K // 128, 128, N // ncols_blk, ncols_blk)
    return np.ascontiguousarray(a.transpose(2, 1, 0, 3)).reshape((N // ncols_blk) * 128, (K // 128) * ncols_blk)


def rope_tables(pos):
    inv = (1.0 / (10000.0 ** (np.arange(0, 32, 2, dtype=np.float32) / np.float32(32)))).astype(np.float32)
    ang = pos.astype(np.float32)[:, None] * inv[None, :]
    c = np.cos(ang).astype(np.float32).T
    s = np.sin(ang).astype(np.float32).T
    cos2 = np.concatenate([c, c], 0)
    sins = np.concatenate([-s, s], 0)
    return np.ascontiguousarray(np.stack([cos2, sins], 1))


def build_program(S_P):
    NT = S_P // TT
    SK_S = PAST + DEC
    nc = bass.Bass("TRN2", target_bir_lowering=False)
    fw = FW(nc)
    PE, ACT, DVE, POOL, SP = fw.pe, fw.act, fw.dve, fw.pool, fw.sp

    def din(name, shape, dt=F32):
        return nc.dram_tensor(name, list(shape), dt, kind="ExternalInput").ap()

    def dout(name, shape):
        return nc.dram_tensor(name, list(shape), F32, kind="ExternalOutput").ap()

    def dscr(name, shape, dt=BF16):
        return nc.dram_tensor(name, list(shape), dt, kind="Internal").ap()

    xp = din("xp", [D, S_P])
    xs = din("xs", [D, DEC])
    ckv_c = din("ckv_c", [256, PAST])
    ckr_c = din("ckr_c", [32, PAST])
    sconv = din("sconv", [128, 24])
    srg = din("srg", [128, 8])
    rope_p = din("rope_p", [32, 2, S_P])
    rope_s = din("rope_s", [32, 2, DEC])
    params_d = din("params", [128, NPAR])
    w13a_f = din("w13a", [11 * 128, 4096])
    w2a_f = din("w2a", [8 * 128, 2816])
    w13b_f = din("w13b", [11 * 128, 4096])
    w2b_f = din("w2b", [8 * 128, 2816])
    win_f = din("win", [10 * 128, 4096])
    wres_f = din("wres", [128, NRES])
    woa_f = din("woa", [128, 4096])
    wor_f = din("wor", [2 * 128, 4096])
    wout_f = din("wout", [2 * 128, 4096])

    NF_ = NT // 4
    S_O = NF_ * TT
    flags_d = din("flags", [128, 12])
    y_p = dout("y_p", [D, S_O])
    kvl_p = dout("kvl_p", [256, S_O])
    kr_p = dout("kr_p", [32, S_O])
    conv_p = dout("conv_p", [128, 24])
    h_p = dout("h_p", [128, 8])
    y_s = dout("y_s", [D, DEC])
    kvl_s = dout("kvl_s", [256, DEC])
    kr_s = dout("kr_s", [32, DEC])
    conv_s = dout("conv_s", [128, 24])
    h_s = dout("h_s", [128, 8])

    w13a = dscr("w13a_b", [11 * 128, 4096])
    w2a = dscr("w2a_b", [8 * 128, 2816])
    w13b = dscr("w13b_b", [11 * 128, 4096])
    w2b = dscr("w2b_b", [8 * 128, 2816])
    win = dscr("win_b", [10 * 128, 4096])
    woa = dscr("woa_b", [128, 4096])
    wor = dscr("wor_b", [2 * 128, 4096])
    wout = dscr("wout_b", [2 * 128, 4096])
    KN_p = dscr("KN_p", [4, 128, S_P])
    KR_p = dscr("KR_p", [32, S_P])
    V_p = dscr("V_p", [S_P, 1024])
    KN_s = dscr("KN_s", [4, 128, SK_S])
    KR_s = dscr("KR_s", [32, SK_S])
    V_s = dscr("V_s", [SK_S, 1024])

    def sb(name, shape, dt=F32):
        return nc.alloc_sbuf_tensor("sb_" + name, list(shape), dt).ap()

    xT = sb("xT", [128, 8, TT])
    xT_b = [Buf("xT%d" % c) for c in range(8)]
    uT = sb("uT", [128, 8, TT], BF16)
    uT_b = [Buf("uT%d" % c) for c in range(8)]
    gT = sb("gT", [128, NF, TT], BF16)
    gT_b = [Buf("gT%d" % c) for c in range(NF)]
    rstd = sb("rstd", [128, TT])
    rstd_b = Buf("rstd")
    sil = [sb("sil%d" % i, [128, TT]) for i in range(2)]
    sil_b = [Buf("sil%d" % i) for i in range(2)]
    zq = sb("zq", [128, 3, TT])
    zq_b = [Buf("zq%d" % c) for c in range(3)]
    qn = sb("qn", [128, 3, TT], BF16)
    qn_b = [Buf("qn%d" % c) for c in range(3)]
    kvraw = sb("kvraw", [128, 2, TT])
    kvraw_b = [Buf("kvraw%d" % c) for c in range(2)]
    ckv = sb("ckv", [128, 2, TT])
    ckv_b = [Buf("ckv%d" % c) for c in range(2)]
    ckvb = sb("ckvb", [128, 2, TT], BF16)
    ckvb_b = [Buf("ckvb%d" % c) for c in range(2)]
    krraw = sb("krraw", [32, 2, TT])
    krraw_b = [Buf("krraw0"), Buf("krraw1")]
    krf = sb("krf", [32, TT])
    krf_b = Buf("krf")
    krb = sb("krb", [32, TT], BF16)
    krb_b = Buf("krb")
    cs = sb("cs", [32, 2, TT])
    cs_b = Buf("cs")
    rt = [sb("rt%d" % i, [32, TT]) for i in range(2)]
    rt_b = [Buf("rt0"), Buf("rt1")]
    Xb = [sb("Xb%d" % i, [128, TT + 3]) for i in range(2)]
    Xb_b = [Buf("Xb0"), Buf("Xb1")]
    xc = sb("xc", [128, TT])
    xc_b = Buf("xc")
    xcb = sb("xcb", [128, TT], BF16)
    xcb_b = Buf("xcb")
    rbuf = sb("rbuf", [128, TT])
    rbuf_b = Buf("rbuf")
    a2buf = sb("a2buf", [128, TT])
    a2buf_b = Buf("a2buf")
    igbuf = sb("igbuf", [128, TT])
    igbuf_b = Buf("igbuf")
    hbuf = sb("hbuf", [128, TT])
    hbuf_b = Buf("hbuf")
    gg = [sb("gg%d" % i, [128, TT], BF16) for i in range(2)]
    gg_b = [Buf("gg0"), Buf("gg1")]
    freg = sb("freg", [128, 12288], BF16)
    hg = freg[:, 0:4096].rearrange("p (c t) -> p c t", c=8)
    hg_b = [Buf("hg%d" % c) for c in range(8)]
    QT = freg[:, 4096:8192].rearrange("p (c t) -> p c t", c=8)
    QT_b = [Buf("QT%d" % c) for c in range(8)]
    oT = freg[:, 8192:10240].rearrange("p (c t) -> p c t", c=4)
    oT_b = [Buf("oT%d" % c) for c in range(4)]
    ma = freg[:, 10240:11264].bitcast(F32)
    ma_b = Buf("ma")
    rec = freg[:, 11264:12288].bitcast(F32)
    rec_b = Buf("rec")
    xr_all = freg[:, 0:8240].bitcast(F32).rearrange("p (c t) -> p c t", c=8)
    xr_b = [Buf("xr%d" % c) for c in range(8)]
    xcL = [freg[:, 8256:9280].bitcast(F32), freg[:, 9280:10304].bitcast(F32)]
    xcL_b = [Buf("xcL0"), Buf("xcL1")]
    alias_b = hg_b + QT_b + oT_b + [ma_b, rec_b] + xr_b + xcL_b
    fdummy = sb("fdummy", [128, 8])
    knT = sb("knT", [128, 4, TT], BF16)
    knT_b = [Buf("knT%d" % c) for c in range(4)]
    vst = sb("vst", [128, 4, 8, 128], BF16)
    vst_b = Buf("vst")
    NKS = 2
    kslot = [sb("kslot%d" % i, [96, 4, TT], BF16) for i in range(NKS)]
    kslot_b = [Buf("kslot%d" % i) for i in range(NKS)]
    vslot = [sb("vslot%d" % i, [128, 4, 4, 128], BF16) for i in range(NKS)]
    vslot_b = [Buf("vslot%d" % i) for i in range(NKS)]
    NPT = 4
    PTs = [sb("PT%d" % i, [128, TT], BF16) for i in range(NPT)]
    PT_b = [Buf("PT%d" % i) for i in range(NPT)]
    NWS = 4
    wslot = [sb("wslot%d" % i, [128, 4096], BF16) for i in range(NWS)]
    wslot_b = [Buf("wslot%d" % i) for i in range(NWS)]
    wslot_c = [fw.dma_ctr("wsl%d" % i) for i in range(NWS)]
    wres = sb("wres", [128, NRES], BF16)
    wres_b = Buf("wres")
    ones = sb("ones", [128, 128], BF16)
    ones_b = Buf("ones")
    par = sb("par", [128, NPAR])
    par_b = Buf("par")
    flg = sb("flg", [128, 12])
    flg_b = Buf("flg")
    onesv = sb("onesv", [128, 4, 64], BF16)
    onesv_b = Buf("onesv")
    nsp = sb("nsp", [128, 16])
    nsp_b = Buf("nsp")
    halo = sb("halo", [128, 8, 3])
    halo_b = [Buf("halo%d" % c) for c in range(8)]
    hst = sb("hst", [128, 8])
    hst_b = [Buf("hst%d" % c) for c in range(8)]

    wuq = wres[:, 0:3072].rearrange("p (k h j) -> p k h j", k=3, h=8)
    wukv = wres[:, 3072:5120].rearrange("p (e k j) -> p e k j", e=2, k=2)
    wgate = wres[:, 5120:7168].rearrange("p (n e j) -> p n e j", n=8, e=2)

    psum = [nc.alloc_psum_tensor("ps%d" % i, [128, TT], F32).ap() for i in range(8)]
    psum_b = [Buf("ps%d" % i) for i in range(8)]
    prr = [0]

    reserved = set()

    def pget(lo=0, hi=8):
        while True:
            i = lo + prr[0] % (hi - lo)
            prr[0] += 1
            if i not in reserved:
                return psum[i], psum_b[i]

    def pget_reserve():
        ps, pb = pget()
        reserved.add(psum_b.index(pb))
        return ps, pb

    c_x = fw.dma_ctr("c_x")
    c_misc = fw.dma_ctr("c_misc")
    c_msp = fw.dma_ctr("c_msp")
    c_cs = fw.dma_ctr("c_cs")
    c_out = fw.dma_ctr("c_out")
    c_kv = fw.dma_ctr("c_kv")
    c_kr = fw.dma_ctr("c_kr")
    c_kn = fw.dma_ctr("c_kn")
    c_krs = fw.dma_ctr("c_krs")
    c_vst = fw.dma_ctr("c_vst")
    c_ks = [fw.dma_ctr("c_ks%d" % i) for i in range(NKS)]
    c_vs = [fw.dma_ctr("c_vs%d" % i) for i in range(NKS)]
    c_cast = fw.dma_ctr("c_cast")
    c_st = fw.dma_ctr("c_st")

    wdram_b = Buf("wdram")
    KV_b = {}

    def kvbuf(seqname, tile):
        k = (seqname, tile)
        if k not in KV_b:
            KV_b[k] = Buf("kv_%s_%d" % k)
        return KV_b[k]

    wdram2_b = Buf("wdram2")
    c_cast2 = fw.dma_ctr("c_cast2")
    cast_ctx = {"ctr": c_cast, "buf": wdram_b}

    def cast_copy(dst, src, rows, cols):
        step = max(1, (1 << 20) // cols)
        r = 0
        cc, bb = cast_ctx["ctr"], cast_ctx["buf"]
        while r < rows:
            rr = min(step, rows - r)
            if cc.count >= 32:
                POOL.eng.wait_ge(cc.sem, cc.count - 16)
            fw.dma(POOL, cc, dst[r:r + rr, :], src[r:r + rr, :], writes=[bb])
            r += rr

    cast_copy(w13a, w13a_f, 11 * 128, 4096)
    cast_copy(w2a, w2a_f, 8 * 128, 2816)
    cast_copy(win, win_f, 10 * 128, 4096)

    def late_casts():
        cast_ctx["ctr"], cast_ctx["buf"] = c_cast2, wdram2_b
        cast_copy(woa, woa_f, 128, 4096)
        cast_copy(wor, wor_f, 256, 4096)
        cast_copy(wout, wout_f, 256, 4096)
        cast_copy(w13b, w13b_f, 11 * 128, 4096)
        cast_copy(w2b, w2b_f, 8 * 128, 2816)

    fw.dma(POOL, c_misc, wres, wres_f, writes=[wres_b])
    fw.dma(SP, c_msp, par, params_d, writes=[par_b])
    fw.dma(SP, c_msp, flg, flags_d, writes=[flg_b])
    fw.op(DVE, lambda: nc.vector.memset(onesv, 1.0), writes=[onesv_b])
    fw.op(DVE, lambda: nc.vector.memset(ones, 1.0), writes=[ones_b])
    fw.op(DVE, lambda: nc.vector.memset(vst, 1.0), writes=[vst_b])
    fw.op(ACT, lambda: nc.scalar.activation(out=nsp[:, 0:8], in_=par[:, PLAM:PLAM + 8], func=AF.Exp, scale=-1.0),
          reads=[par_b], writes=[nsp_b])
    fw.op(ACT, lambda: nc.scalar.activation(out=nsp[:, 0:8], in_=nsp[:, 0:8], func=AF.Ln, bias=1.0, scale=1.0),
          reads=[nsp_b], writes=[nsp_b])
    fw.op(DVE, lambda: nc.vector.tensor_scalar(out=nsp[:, 8:16], in0=nsp[:, 0:8], scalar1=-16.0, scalar2=None,
                                               op0=ALU.mult), reads=[nsp_b], writes=[nsp_b])
    fw.op(DVE, lambda: nc.vector.tensor_scalar(out=nsp[:, 0:8], in0=nsp[:, 0:8], scalar1=-8.0, scalar2=None,
                                               op0=ALU.mult), reads=[nsp_b], writes=[nsp_b])

    blocks_L = [(w13a, g, 4096) for g in range(11)] + [(w2a, d, 2816) for d in range(8)] + \
               [(win, i, 4096) for i in range(6)]
    blocks_F = [(w13a, g, 4096) for g in range(11)] + [(w2a, d, 2816) for d in range(8)] + \
               [(win, i, 4096) for i in range(10)] + [(woa, 0, 4096), (wor, 0, 4096), (wor, 1, 4096),
                                                       (wout, 0, 4096), (wout, 1, 4096)] + \
               [(w13b, g, 4096) for g in range(11)] + [(w2b, d, 2816) for d in range(8)]
    tile_blocks = []
    for k in range(NF_):
        tile_blocks += blocks_L * 3 + blocks_F
    tile_blocks += blocks_F
    total_blocks = len(tile_blocks)
    NBLK = total_blocks
    ws = {"issued": 0, "used": 0}

    def ws_issue():
        i = ws["issued"]
        if i >= total_blocks:
            return
        t, idx, E = tile_blocks[i % NBLK]
        s = i % NWS
        src_b = wdram_b if (t is w13a or t is w2a or t is win) else wdram2_b
        fw.dma(SP, wslot_c[s], wslot[s][:, 0:E], t[idx * 128:(idx + 1) * 128, 0:E], reads=[src_b],
               writes=[wslot_b[s]])
        ws["issued"] += 1

    def ws_next(expect, hold=0):
        i = ws["used"]
        assert tile_blocks[i % NBLK][0] is expect, "weight stream order mismatch"
        while ws["issued"] < min(total_blocks, i + NWS - hold):
            ws_issue()
        ws["used"] += 1
        s = i % NWS
        return wslot[s], wslot_b[s]

    def norm_sums(src, src_b, nch, n, sqbuf, sqbuf_b, reserve=False):
        ps, pb = pget_reserve() if reserve else pget()
        for c in range(nch):
            if c % 2 == 0:
                fw.op(POOL, lambda c=c: nc.gpsimd.tensor_tensor(out=sqbuf[:, c, 0:n], in0=src[:, c, 0:n],
                                                                in1=src[:, c, 0:n], op=ALU.mult),
                      reads=[src_b[c]], writes=[sqbuf_b[c]])
            else:
                fw.op(DVE, lambda c=c: nc.vector.tensor_tensor(out=sqbuf[:, c, 0:n], in0=src[:, c, 0:n],
                                                               in1=src[:, c, 0:n], op=ALU.mult),
                      reads=[src_b[c]], writes=[sqbuf_b[c]])
            fw.op(PE, lambda c=c: nc.tensor.matmul(ps[:, 0:n], lhsT=ones, rhs=sqbuf[:, c, 0:n], start=(c == 0),
                                                   stop=(c == nch - 1)),
                  reads=[sqbuf_b[c], ones_b], writes=[pb])
        return ps, pb

    def rmsnorm(src, src_b, nch, gcol, dim, out, out_b, n, sqbuf, sqbuf_b, pre=None):
        if pre is not None:
            ps, pb = pre
            reserved.discard(psum_b.index(pb))
        else:
            ps, pb = norm_sums(src, src_b, nch, n, sqbuf, sqbuf_b)
        fw.op(ACT, lambda: nc.scalar.activation(out=rstd[:, 0:n], in_=ps[:, 0:n], func=AF.Sqrt, scale=1.0 / dim,
                                                bias=EPS), reads=[pb], writes=[rstd_b])
        fw.op(DVE, lambda: nc.vector.reciprocal(out=rstd[:, 0:n], in_=rstd[:, 0:n]), reads=[rstd_b],
              writes=[rstd_b])
        for c in range(nch):
            fw.op(DVE, lambda c=c: nc.vector.scalar_tensor_tensor(out=out[:, c, 0:n], in0=src[:, c, 0:n],
                                                                  scalar=par[:, gcol + c:gcol + c + 1],
                                                                  in1=rstd[:, 0:n], op0=ALU.mult, op1=ALU.mult),
                  reads=[src_b[c], rstd_b, par_b], writes=[out_b[c]])

    def ffn_gen(w13t, w2t, n, sumsq=None):
        si = 0
        for g in range(11):
            wsl, wb = ws_next(w13t)
            wv = wsl.rearrange("p (e k j) -> p e k j", e=2, k=8)
            for e in range(2):
                f = 2 * g + e
                pa, pab = pget()
                pb_, pbb = pget()
                for k in range(8):
                    fw.op(PE, lambda k=k: nc.tensor.matmul(pa[:, 0:n], lhsT=wv[:, 0, k, e * 128:(e + 1) * 128],
                                                           rhs=uT[:, k, 0:n], start=(k == 0), stop=(k == 7)),
                          reads=[wb, uT_b[k]], writes=[pab])
                for k in range(8):
                    fw.op(PE, lambda k=k: nc.tensor.matmul(pb_[:, 0:n], lhsT=wv[:, 1, k, e * 128:(e + 1) * 128],
                                                           rhs=uT[:, k, 0:n], start=(k == 0), stop=(k == 7)),
                          reads=[wb, uT_b[k]], writes=[pbb])
                s = sil[si % 2]
                s_b = sil_b[si % 2]
                si += 1
                fw.op(ACT, lambda: nc.scalar.activation(out=s[:, 0:n], in_=pa[:, 0:n], func=AF.Silu),
                      reads=[pab], writes=[s_b])
                fw.op(DVE, lambda: nc.vector.tensor_tensor(out=gT[:, f, 0:n], in0=pb_[:, 0:n], in1=s[:, 0:n],
                                                           op=ALU.mult), reads=[pbb, s_b], writes=[gT_b[f]])
            yield
        for d in range(8):
            wsl, wb = ws_next(w2t)
            wv = wsl[:, 0:2816].rearrange("p (f j) -> p f j", f=NF)
            pd, pdb = pget()
            for f in range(NF):
                fw.op(PE, lambda f=f: nc.tensor.matmul(pd[:, 0:n], lhsT=wv[:, f, :], rhs=gT[:, f, 0:n],
                                                       start=(f == 0), stop=(f == NF - 1)),
                      reads=[wb, gT_b[f]], writes=[pdb])
            fw.op(DVE, lambda: nc.vector.scalar_tensor_tensor(out=xT[:, d, 0:n], in0=pd[:, 0:n], scalar=0.5,
                                                              in1=xT[:, d, 0:n], op0=ALU.mult, op1=ALU.add),
                  reads=[pdb, xT_b[d]], writes=[xT_b[d]])
            if sumsq is not None:
                sps, spb = sumsq
                if d > 0:
                    fw.op(PE, lambda: nc.tensor.matmul(sps[:, 0:n], lhsT=ones, rhs=sqv[(d - 1) % 2][:, 0:n],
                                                       start=(d == 1), stop=False),
                          reads=[sil_b[(d - 1) % 2], ones_b], writes=[spb])
                fw.op(ACT, lambda: nc.scalar.activation(out=sqv[d % 2][:, 0:n], in_=xT[:, d, 0:n], func=AF.Square),
                      reads=[xT_b[d]], writes=[sil_b[d % 2]])
                if d == 7:
                    fw.op(PE, lambda: nc.tensor.matmul(sps[:, 0:n], lhsT=ones, rhs=sqv[1][:, 0:n], start=False,
                                                       stop=True), reads=[sil_b[1], ones_b], writes=[spb])
            yield

    sqv = [sil[0].bitcast(BF16), sil[1].bitcast(BF16)]

    def ffn(w13t, w2t, n, side=None, sumsq=None):
        for _ in ffn_gen(w13t, w2t, n, sumsq):
            if side is not None:
                try:
                    next(side)
                except StopIteration:
                    side = None
        if side is not None:
            for _ in side:
                pass

    def fence(bufs):
        fw.op(POOL, lambda: nc.gpsimd.memset(fdummy[:, 0:1], 0.0), writes=bufs)

    def produce_kv(n, KTd, Vd, key0, seqname, tile_id, vflag=None):
        kb_ = kvbuf(seqname, tile_id)
        for p in range(4):
            ps, pb = pget()
            for kc in range(2):
                fw.op(PE, lambda kc=kc: nc.tensor.matmul(ps[:, 0:n], lhsT=wukv[:, 0, kc, p * 128:(p + 1) * 128],
                                                         rhs=ckvb[:, kc, 0:n], start=(kc == 0), stop=(kc == 1)),
                      reads=[wres_b, ckvb_b[kc]], writes=[pb])
            fw.op(ACT, lambda: nc.scalar.copy(out=knT[:, p, 0:n], in_=ps[:, 0:n]), reads=[pb], writes=[knT_b[p]])
            fw.dma(ACT, c_kn, KTd[0][p, :, key0:key0 + n], knT[:, p, 0:n], reads=[knT_b[p]], writes=[kb_])
        fw.dma(SP, c_krs, KTd[1][:, key0:key0 + n], krb[:, 0:n], reads=[krb_b], writes=[kb_])
        ntb = (n + 127) // 128
        for tb in range(ntb):
            rows = min(128, n - tb * 128)
            ps, pb = pget()
            for kc in range(2):
                fw.op(PE, lambda kc=kc: nc.tensor.matmul(ps[0:rows, 0:512],
                                                         lhsT=ckvb[:, kc, tb * 128:tb * 128 + rows],
                                                         rhs=wukv[:, 1, kc, :], start=(kc == 0), stop=(kc == 1)),
                      reads=[wres_b, ckvb_b[kc]], writes=[pb])
            psv = ps[0:rows, 0:512].rearrange("p (h e j) -> p h e j", h=4, e=2)
            vv = vst[0:rows, tb].rearrange("p (h e) j -> p h e j", e=2)
            fw.op(DVE, lambda: nc.vector.tensor_copy(out=vv[:, :, 0, 0:64], in_=psv[:, :, 0, :]), reads=[pb],
                  writes=[vst_b])
            fw.op(ACT, lambda: nc.scalar.copy(out=vv[:, :, 1, 64:128], in_=psv[:, :, 1, :]), reads=[pb],
                  writes=[vst_b])
            if vflag is not None:
                fw.op(DVE, lambda: nc.vector.tensor_scalar_mul(out=vv[:, :, 0, 64:128], in0=onesv[0:rows],
                                                               scalar1=flg[0:rows, vflag:vflag + 1]),
                      reads=[onesv_b, flg_b], writes=[vst_b])
                fw.op(DVE, lambda: nc.vector.tensor_scalar_mul(out=vv[:, :, 1, 0:64], in0=onesv[0:rows],
                                                               scalar1=flg[0:rows, vflag:vflag + 1]),
                      reads=[onesv_b, flg_b], writes=[vst_b])
        if n % 128 == 0:
            fw.dma(SP, c_vst, Vd[key0:key0 + n, :].rearrange("(tb p) c -> p tb c", p=128),
                   vst[:, 0:ntb].rearrange("p tb h j -> p tb (h j)"), reads=[vst_b], writes=[kb_])
        else:
            fw.dma(SP, c_vst, Vd[key0:key0 + n, :], vst[0:n, 0].rearrange("p h j -> p (h j)"), reads=[vst_b],
                   writes=[kb_])

    def attention(n, KTd, Vd, ctx, seqname):
        ld = [0]
        n_ctx = len(ctx)
        for hg_ in range(2):
            oacc = [(psum[4 + j], psum_b[4 + j]) for j in range(4)]
            first = [True] * 4
            slots = {}

            def load(ci):
                key0, nk, diag, tile_id = ctx[ci]
                s = ld[0] % NKS
                ld[0] += 1
                slots[ci] = s
                kb_ = kvbuf(seqname, tile_id)
                fw.dma(SP, c_ks[s], kslot[s][32:96, :, 0:nk],
                       KTd[0][2 * hg_:2 * hg_ + 2, :, key0:key0 + nk].rearrange("p (e j) s -> j (p e) s", e=2),
                       reads=[kb_], writes=[kslot_b[s]])
                fw.dma(SP, c_ks[s], kslot[s][0:32, :, 0:nk],
                       KTd[1][:, key0:key0 + nk].unsqueeze(1).broadcast_to([32, 4, nk]),
                       reads=[kb_], writes=[kslot_b[s]])
                nkb = (nk + 127) // 128
                if nk % 128 == 0:
                    fw.dma(SP, c_vs[s], vslot[s][:, 0:nkb].rearrange("p kb h j -> p kb (h j)"),
                           Vd[key0:key0 + nk, 512 * hg_:512 * hg_ + 512].rearrange("(kb p) c -> p kb c", p=128),
                           reads=[kb_], writes=[vslot_b[s]])
                else:
                    fw.dma(SP, c_vs[s], vslot[s][0:nk, 0].rearrange("p h j -> p (h j)"),
                           Vd[key0:key0 + nk, 512 * hg_:512 * hg_ + 512], reads=[kb_], writes=[vslot_b[s]])

            pend = []
            pt_i = [0]

            def do_pv(it):
                (s, kb, rows, hh, last, pt, ptb) = it
                oa, oab = oacc[hh]
                fw.op(PE, lambda: nc.tensor.matmul(oa[:, 0:n], lhsT=vslot[s][0:rows, kb, hh, :], rhs=pt[0:rows, 0:n],
                                                   start=first[hh], stop=last),
                      reads=[vslot_b[s], ptb], writes=[oab])
                first[hh] = False

            load(0)
            for ci, (key0, nk, diag, tile_id) in enumerate(ctx):
                s = slots[ci]
                nkb = (nk + 127) // 128
                idx = 0
                for kb in range(nkb):
                    rows = min(128, nk - kb * 128)
                    for hh in range(4):
                        last = (ci == n_ctx - 1) and (kb == nkb - 1)
                        h = 4 * hg_ + hh
                        sp_, spb = pget(0, 3)
                        fw.op(PE, lambda: nc.tensor.matmul(sp_[0:rows, 0:n],
                                                           lhsT=kslot[s][0:96, hh, kb * 128:kb * 128 + rows],
                                                           rhs=QT[0:96, h, 0:n], start=True, stop=True),
                              reads=[kslot_b[s], QT_b[h]], writes=[spb])
                        pi = pt_i[0] % NPT
                        pt_i[0] += 1
                        pt, ptb = PTs[pi], PT_b[pi]
                        if diag:
                            c0 = kb * 128
                            fw.op(ACT, lambda: nc.scalar.activation(out=pt[0:rows, c0:n], in_=sp_[0:rows, c0:n],
                                                                    func=AF.Exp), reads=[spb], writes=[ptb])
                            if c0 > 0:
                                fw.op(POOL, lambda: nc.gpsimd.memset(pt[0:rows, 0:c0], 0.0), writes=[ptb])
                            fw.op(POOL, lambda: nc.gpsimd.memset(pt[64:128, c0:c0 + 64], 0.0), writes=[ptb])
                        else:
                            fw.op(ACT, lambda: nc.scalar.activation(out=pt[0:rows, 0:n], in_=sp_[0:rows, 0:n],
                                                                    func=AF.Exp), reads=[spb], writes=[ptb])
                        pend.append((s, kb, rows, hh, last, pt, ptb))
                        if len(pend) > 2:
                            do_pv(pend.pop(0))
                        if idx == 2 and ci + 1 < n_ctx:
                            load(ci + 1)
                        idx += 1
            while pend:
                do_pv(pend.pop(0))
            for hh in range(4):
                h = 4 * hg_ + hh
                oa, oab = oacc[hh]
                if h % 2 == 0:
                    num, den = slice(0, 64), slice(64, 128)
                else:
                    num, den = slice(64, 128), slice(0, 64)
                fw.op(DVE, lambda: nc.vector.reciprocal(out=rec[num, 0:n], in_=oa[den, 0:n]), reads=[oab],
                      writes=[rec_b])
                fw.op(DVE, lambda: nc.vector.tensor_tensor(out=oT[num, h // 2, 0:n], in0=oa[num, 0:n],
                                                           in1=rec[num, 0:n], op=ALU.mult),
                      reads=[oab, rec_b], writes=[oT_b[h // 2]])

    def copy_act(out, in_, scale=None):
        if scale is None:
            return nc.scalar.activation(out=out, in_=in_, func=AF.Copy)
        return nc.scalar.activation(out=out, in_=in_, func=AF.Copy, scale=scale)

    def proj8(wv, col0, ncol, n, prow=128):
        ps, pb = pget()
        for k in range(8):
            fw.op(PE, lambda k=k: nc.tensor.matmul(ps[0:ncol, 0:n], lhsT=wv[:, k, col0:col0 + ncol],
                                                   rhs=uT[:, k, 0:n], start=(k == 0), stop=(k == 7)),
                  reads=[wv_b[0], uT_b[k]], writes=[pb])
        return ps, pb

    wv_b = [None]

    def rnn_s1(c, n, X, X_b, xc_, xc_b_):
        fw.op(POOL, lambda: nc.gpsimd.tensor_copy(out=X[:, 0:3], in_=halo[:, c, :]), reads=[halo_b[c]],
              writes=[X_b])
        fw.op(POOL, lambda: nc.gpsimd.tensor_copy(out=halo[:, c, :], in_=X[:, n:n + 3]), reads=[X_b],
              writes=[halo_b[c]])
        cw = PCW + 4 * c
        fw.op(DVE, lambda: nc.vector.tensor_scalar(out=xc_[:, 0:n], in0=X[:, 0:n], scalar1=par[:, cw:cw + 1],
                                                   scalar2=par[:, PCB + c:PCB + c + 1], op0=ALU.mult,
                                                   op1=ALU.add), reads=[X_b, par_b], writes=[xc_b_])
        for j in range(1, 4):
            fw.op(DVE, lambda j=j: nc.vector.scalar_tensor_tensor(out=xc_[:, 0:n], in0=X[:, j:j + n],
                                                                  scalar=par[:, cw + j:cw + j + 1],
                                                                  in1=xc_[:, 0:n], op0=ALU.mult, op1=ALU.add),
                  reads=[X_b, par_b, xc_b_], writes=[xc_b_])
        fw.op(POOL, lambda: nc.gpsimd.tensor_copy(out=xcb[:, 0:n], in_=xc_[:, 0:n]), reads=[xc_b_],
              writes=[xcb_b])

    def rnn_s2(seq, ti, c, n):
        pr, prb = pget()
        fw.op(PE, lambda: nc.tensor.matmul(pr[:, 0:n], lhsT=wgate[:, c, 0, :], rhs=xcb[:, 0:n], start=True,
                                           stop=True), reads=[wres_b, xcb_b], writes=[prb])
        pi_, pib = pget()
        fw.op(PE, lambda: nc.tensor.matmul(pi_[:, 0:n], lhsT=wgate[:, c, 1, :], rhs=xcb[:, 0:n], start=True,
                                           stop=True), reads=[wres_b, xcb_b], writes=[pib])
        fw.op(ACT, lambda: nc.scalar.activation(out=rbuf[:, 0:n], in_=pr[:, 0:n], func=AF.Sigmoid,
                                                bias=par[:, PBR + c:PBR + c + 1]),
              reads=[prb, par_b], writes=[rbuf_b])
        fw.op(ACT, lambda: nc.scalar.activation(out=igbuf[:, 0:n], in_=pi_[:, 0:n], func=AF.Sigmoid,
                                                bias=par[:, PBI + c:PBI + c + 1]),
              reads=[pib, par_b], writes=[igbuf_b])
        fw.op(ACT, lambda: nc.scalar.activation(out=a2buf[:, 0:n], in_=rbuf[:, 0:n], func=AF.Exp,
                                                scale=nsp[:, 8 + c:9 + c]), reads=[rbuf_b, nsp_b],
              writes=[a2buf_b])
        fw.op(ACT, lambda: nc.scalar.activation(out=rbuf[:, 0:n], in_=rbuf[:, 0:n], func=AF.Exp,
                                                scale=nsp[:, c:c + 1]), reads=[rbuf_b, nsp_b], writes=[rbuf_b])
        fw.op(DVE, lambda: nc.vector.tensor_scalar_min(out=a2buf[:, 0:n], in0=a2buf[:, 0:n], scalar1=1.0),
              reads=[a2buf_b], writes=[a2buf_b])
        fw.op(ACT, lambda: nc.scalar.activation(out=a2buf[:, 0:n], in_=a2buf[:, 0:n], func=AF.Sqrt, scale=-1.0,
                                                bias=1.0), reads=[a2buf_b], writes=[a2buf_b])
        if seq["prompt"] and ti < 4:
            fw.op(DVE, lambda: nc.vector.tensor_scalar_mul(out=rbuf[:, 0:1], in0=rbuf[:, 0:1],
                                                           scalar1=flg[:, ti:ti + 1]),
                  reads=[rbuf_b, flg_b], writes=[rbuf_b])
            fw.op(DVE, lambda: nc.vector.tensor_scalar(out=a2buf[:, 0:1], in0=a2buf[:, 0:1],
                                                       scalar1=flg[:, ti:ti + 1], scalar2=flg[:, 4 + ti:5 + ti],
                                                       op0=ALU.mult, op1=ALU.add),
                  reads=[a2buf_b, flg_b], writes=[a2buf_b])

    def rnn_s3(c, n, xc_, xc_b_):
        fw.op(DVE, lambda: nc.vector.tensor_tensor(out=igbuf[:, 0:n], in0=igbuf[:, 0:n], in1=xc_[:, 0:n],
                                                   op=ALU.mult), reads=[igbuf_b, xc_b_], writes=[igbuf_b])
        fw.op(DVE, lambda: nc.vector.tensor_tensor(out=igbuf[:, 0:n], in0=igbuf[:, 0:n], in1=a2buf[:, 0:n],
                                                   op=ALU.mult), reads=[igbuf_b, a2buf_b], writes=[igbuf_b])
        fw.op(DVE, lambda: nc.vector.tensor_tensor_scan(out=hbuf[:, 0:n], data0=rbuf[:, 0:n], data1=igbuf[:, 0:n],
                                                        initial=hst[:, c:c + 1], op0=ALU.mult, op1=ALU.add),
              reads=[rbuf_b, igbuf_b, hst_b[c]], writes=[hbuf_b])
        fw.op(POOL, lambda: nc.gpsimd.tensor_copy(out=hst[:, c:c + 1], in_=hbuf[:, n - 1:n]), reads=[hbuf_b],
              writes=[hst_b[c]])

    def rnn_chunk(seq, ti, c, n, psx, psx_b, psg, psg_b):
        X, X_b = Xb[c % 2], Xb_b[c % 2]
        fw.op(ACT, lambda: copy_act(X[:, 3:3 + n], psx[:, 0:n]), reads=[psx_b], writes=[X_b])
        rnn_s1(c, n, X, X_b, xc, xc_b)
        rnn_s2(seq, ti, c, n)
        rnn_s3(c, n, xc, xc_b)
        g_, g_b = gg[c % 2], gg_b[c % 2]
        fw.op(ACT, lambda: nc.scalar.activation(out=g_[:, 0:n], in_=psg[:, 0:n], func=AF.Gelu_apprx_tanh),
              reads=[psg_b], writes=[g_b])
        fw.op(DVE, lambda: nc.vector.tensor_tensor(out=hg[:, c, 0:n], in0=hbuf[:, 0:n], in1=g_[:, 0:n],
                                                   op=ALU.mult), reads=[hbuf_b, g_b], writes=[hg_b[c]])

    def rnn_chain_gen(seq, ti, n):
        for step in range(10):
            if 0 <= step - 2 < 8:
                c = step - 2
                rnn_s3(c, n, xcL[c % 2], xcL_b[c % 2])
            if 0 <= step - 1 < 8:
                rnn_s2(seq, ti, step - 1, n)
            if step < 8:
                c = step
                rnn_s1(c, n, xr_all[:, c, :], xr_b[c], xcL[c % 2], xcL_b[c % 2])
            yield

    c_xa = fw.dma_ctr("c_xa")

    def light_tile(seq, ti, tok0, n, key0, nxt_tok0):
        sname = seq["name"]
        pre1 = seq.pop("pre_norm1", None)
        if pre1 is None:
            fw.dma(SP, c_x, xT[:, :, 0:n], seq["x"].rearrange("(c p) s -> p c s", p=128)[:, :, tok0:tok0 + n],
                   writes=xT_b)
        fw.dma(SP, c_cs, cs[:, :, 0:n], seq["rope"][:, :, tok0:tok0 + n], writes=[cs_b])
        rmsnorm(xT, xT_b, 8, PG1, D, uT, uT_b, n, gT, gT_b, pre=pre1)
        ss = pget_reserve()
        ffn(w13a, w2a, n, side=seq.pop("pending", None), sumsq=ss)
        rmsnorm(xT, xT_b, 8, PGM, D, uT, uT_b, n, gT, gT_b, pre=ss)
        fw.dma(ACT, c_xa, xT[:, :, 0:n],
               seq["x"].rearrange("(c p) s -> p c s", p=128)[:, :, nxt_tok0:nxt_tok0 + n], writes=xT_b)
        wsl, wb = ws_next(win)
        wv = wsl.rearrange("p (k j) -> p k j", k=8)
        wv_b[0] = wb
        for e in range(2):
            ps, pb = proj8(wv, 384 + 32 * e, 32, n)
            fw.op(ACT, lambda: copy_act(krraw[:, e, 0:n], ps[0:32, 0:n]), reads=[pb], writes=[krraw_b[e]])
        wsl, wb = ws_next(win)
        wv = wsl.rearrange("p (k j) -> p k j", k=8)
        wv_b[0] = wb
        for c in range(2):
            ps, pb = proj8(wv, c * 128, 128, n)
            fw.op(ACT, lambda: copy_act(kvraw[:, c, 0:n], ps[:, 0:n]), reads=[pb], writes=[kvraw_b[c]])
        fence(alias_b)

        def rx_blocks(c2s):
            for c2 in c2s:
                wsl, wb = ws_next(win)
                wv = wsl.rearrange("p (k j) -> p k j", k=8)
                wv_b[0] = wb
                for e in range(2):
                    c = 2 * c2 + e
                    psx, psx_b = proj8(wv, e * 128, 128, n)
                    fw.op(ACT, lambda: copy_act(xr_all[:, c, 3:3 + n], psx[:, 0:n]), reads=[psx_b],
                          writes=[xr_b[c]])

        rx_blocks([0, 1])
        fw.op(DVE, lambda: nc.vector.tensor_tensor(out=rt[0][:, 0:n], in0=krraw[:, 0, 0:n], in1=cs[:, 0, 0:n],
                                                   op=ALU.mult), reads=[krraw_b[0], cs_b], writes=[rt_b[0]])
        fw.op(DVE, lambda: nc.vector.tensor_tensor(out=rt[1][:, 0:n], in0=krraw[:, 1, 0:n], in1=cs[:, 1, 0:n],
                                                   op=ALU.mult), reads=[krraw_b[1], cs_b], writes=[rt_b[1]])
        fw.op(DVE, lambda: nc.vector.tensor_tensor(out=krb[:, 0:n], in0=rt[0][:, 0:n], in1=rt[1][:, 0:n],
                                                   op=ALU.add), reads=[rt_b[0], rt_b[1]], writes=[krb_b])
        rmsnorm(kvraw, kvraw_b, 2, PGKV, 256, ckvb, ckvb_b, n, gT, gT_b)
        rx_blocks([2, 3])
        produce_kv(n, seq["KT"], seq["V"], key0, sname, ti, vflag=(8 + ti) if ti < 4 else None)
        seq["pre_norm1"] = norm_sums(xT, xT_b, 8, n, gT, gT_b, reserve=True)
        seq["pending"] = rnn_chain_gen(seq, ti, n)

    def tile(seq, ti, tok0, n, key0, ctx, light=False, out0=0):
        sname = seq["name"]
        pre1 = seq.pop("pre_norm1", None)
        if pre1 is None:
            fw.dma(SP, c_x, xT[:, :, 0:n], seq["x"].rearrange("(c p) s -> p c s", p=128)[:, :, tok0:tok0 + n],
                   writes=xT_b)
        fw.dma(SP, c_cs, cs[:, :, 0:n], seq["rope"][:, :, tok0:tok0 + n], writes=[cs_b])
        rmsnorm(xT, xT_b, 8, PG1, D, uT, uT_b, n, gT, gT_b, pre=pre1)
        ss = pget_reserve()
        ffn(w13a, w2a, n, side=seq.pop("pending", None), sumsq=ss)
        rmsnorm(xT, xT_b, 8, PGM, D, uT, uT_b, n, gT, gT_b, pre=ss)
        wsl, wb = ws_next(win)
        wv = wsl.rearrange("p (k j) -> p k j", k=8)
        wv_b[0] = wb
        for c in range(0 if light else 3):
            ps, pb = proj8(wv, c * 128, 128, n)
            fw.op(ACT, lambda: copy_act(zq[:, c, 0:n], ps[:, 0:n]), reads=[pb], writes=[zq_b[c]])
        for e in range(2):
            ps, pb = proj8(wv, 384 + 32 * e, 32, n)
            fw.op(ACT, lambda: copy_act(krraw[:, e, 0:n], ps[0:32, 0:n]), reads=[pb], writes=[krraw_b[e]])
        wsl, wb = ws_next(win)
        wv = wsl.rearrange("p (k j) -> p k j", k=8)
        wv_b[0] = wb
        for c in range(2):
            ps, pb = proj8(wv, c * 128, 128, n)
            fw.op(ACT, lambda: copy_act(kvraw[:, c, 0:n], ps[:, 0:n]), reads=[pb], writes=[kvraw_b[c]])
        fw.op(DVE, lambda: nc.vector.tensor_tensor(out=rt[0][:, 0:n], in0=krraw[:, 0, 0:n], in1=cs[:, 0, 0:n],
                                                   op=ALU.mult), reads=[krraw_b[0], cs_b], writes=[rt_b[0]])
        fw.op(DVE, lambda: nc.vector.tensor_tensor(out=rt[1][:, 0:n], in0=krraw[:, 1, 0:n], in1=cs[:, 1, 0:n],
                                                   op=ALU.mult), reads=[krraw_b[1], cs_b], writes=[rt_b[1]])
        fw.op(DVE, lambda: nc.vector.tensor_tensor(out=krf[:, 0:n], in0=rt[0][:, 0:n], in1=rt[1][:, 0:n],
                                                   op=ALU.add), reads=[rt_b[0], rt_b[1]], writes=[krf_b])
        fw.op(POOL, lambda: nc.gpsimd.tensor_copy(out=krb[:, 0:n], in_=krf[:, 0:n]), reads=[krf_b], writes=[krb_b])
        if not light:
            fw.dma(POOL, c_kr, seq["kr_out"][:, out0:out0 + n], krf[:, 0:n], reads=[krf_b])
        if not light:
            fence(alias_b)
            rmsnorm(zq, zq_b, 3, PGQ, 384, qn, qn_b, n, gT, gT_b)
        for h in range(0 if light else 8):
            ps, pb = pget()
            for kc in range(3):
                fw.op(PE, lambda kc=kc: nc.tensor.matmul(ps[:, 0:n], lhsT=wuq[:, kc, h, :], rhs=qn[:, kc, 0:n],
                                                         start=(kc == 0), stop=(kc == 2)),
                      reads=[wres_b, qn_b[kc]], writes=[pb])
            fw.op(ACT, lambda: copy_act(QT[0:96, h, 0:n], ps[0:96, 0:n], QSCALE), reads=[pb], writes=[QT_b[h]])
            fw.op(ACT, lambda: copy_act(krraw[:, 0, 0:n], ps[0:32, 0:n], QSCALE), reads=[pb], writes=[krraw_b[0]])
            fw.op(ACT, lambda: copy_act(krraw[:, 1, 0:n], ps[96:128, 0:n], QSCALE), reads=[pb], writes=[krraw_b[1]])
            fw.op(DVE, lambda: nc.vector.tensor_tensor(out=rt[0][:, 0:n], in0=krraw[:, 0, 0:n], in1=cs[:, 0, 0:n],
                                                       op=ALU.mult), reads=[krraw_b[0], cs_b], writes=[rt_b[0]])
            fw.op(DVE, lambda: nc.vector.tensor_tensor(out=rt[1][:, 0:n], in0=krraw[:, 1, 0:n], in1=cs[:, 1, 0:n],
                                                       op=ALU.mult), reads=[krraw_b[1], cs_b], writes=[rt_b[1]])
            fw.op(DVE, lambda: nc.vector.tensor_tensor(out=QT[0:32, h, 0:n], in0=rt[0][:, 0:n], in1=rt[1][:, 0:n],
                                                       op=ALU.add), reads=[rt_b[0], rt_b[1]], writes=[QT_b[h]])
        rmsnorm(kvraw, kvraw_b, 2, PGKV, 256, ckv, ckv_b, n, gT, gT_b)
        for c in range(2):
            fw.op(POOL, lambda c=c: nc.gpsimd.tensor_copy(out=ckvb[:, c, 0:n], in_=ckv[:, c, 0:n]),
                  reads=[ckv_b[c]], writes=[ckvb_b[c]])
        if not light:
            fw.dma(POOL, c_kv, seq["kv_out"].rearrange("(c p) s -> p c s", p=128)[:, :, out0:out0 + n],
                   ckv[:, :, 0:n], reads=ckv_b)
        produce_kv(n, seq["KT"], seq["V"], key0, sname, ctx[0][3],
                   vflag=(8 + ti) if (seq["prompt"] and ti < 4) else None)
        fence(alias_b)
        wvh = [None]
        for step in range(10):
            if 0 <= step - 2 < 8:
                c = step - 2
                rnn_s3(c, n, xcL[c % 2], xcL_b[c % 2])
                g_, g_b = gg[c % 2], gg_b[c % 2]
                fw.op(DVE, lambda: nc.vector.tensor_tensor(out=hg[:, c, 0:n], in0=hbuf[:, 0:n], in1=g_[:, 0:n],
                                                           op=ALU.mult), reads=[hbuf_b, g_b], writes=[hg_b[c]])
            if 0 <= step - 1 < 8:
                rnn_s2(seq, ti, step - 1, n)
            if step < 8:
                c = step
                e = c % 2
                if e == 0:
                    wsl, wb = ws_next(win)
                    wvh[0] = wsl.rearrange("p (k j) -> p k j", k=8)
                    wv_b[0] = wb
                wv = wvh[0]
                psx, psx_b = proj8(wv, e * 128, 128, n)
                psg, psg_b = proj8(wv, 256 + e * 128, 128, n)
                X, X_b = Xb[c % 2], Xb_b[c % 2]
                fw.op(ACT, lambda: copy_act(X[:, 3:3 + n], psx[:, 0:n]), reads=[psx_b], writes=[X_b])
                g_, g_b = gg[c % 2], gg_b[c % 2]
                fw.op(ACT, lambda: nc.scalar.activation(out=g_[:, 0:n], in_=psg[:, 0:n], func=AF.Gelu_apprx_tanh),
                      reads=[psg_b], writes=[g_b])
                rnn_s1(c, n, X, X_b, xcL[c % 2], xcL_b[c % 2])
        fence(alias_b)
        for c2 in range(4):
            wsl, wb = ws_next(win)
            wv = wsl.rearrange("p (k j) -> p k j", k=8)
            wv_b[0] = wb
            for e in range(2):
                c = 2 * c2 + e
                for gi in range(2):
                    ps, pb = proj8(wv, gi * 256 + e * 128, 128, n)
                    fw.op(ACT, lambda: nc.scalar.activation(out=gT[:, 8 * gi + c, 0:n], in_=ps[:, 0:n],
                                                            func=AF.Sigmoid), reads=[pb],
                          writes=[gT_b[8 * gi + c]])
        attention(n, seq["KT"], seq["V"], ctx, sname)
        woa_s, woa_b = ws_next(woa)
        woav = woa_s.rearrange("p (k j) -> p k j", k=4)
        wor_s = [ws_next(wor, 1), ws_next(wor, 2)]
        for d in range(8):
            ps, pb = pget()
            for kc in range(4):
                fw.op(PE, lambda kc=kc: nc.tensor.matmul(ps[:, 0:n], lhsT=woav[:, kc, d * 128:(d + 1) * 128],
                                                         rhs=oT[:, kc, 0:n], start=(kc == 0), stop=(kc == 3)),
                      reads=[woa_b, oT_b[kc]], writes=[pb])
            fw.op(DVE, lambda: nc.vector.tensor_tensor(out=ma[:, 0:n], in0=ps[:, 0:n], in1=gT[:, d, 0:n],
                                                       op=ALU.mult), reads=[pb, gT_b[d]], writes=[ma_b])
            wsl, wb = wor_s[d // 4]
            wv = wsl.rearrange("p (k j) -> p k j", k=8)
            ps2, pb2 = pget()
            for kc in range(8):
                fw.op(PE, lambda kc=kc: nc.tensor.matmul(ps2[:, 0:n], lhsT=wv[:, kc, (d % 4) * 128:(d % 4 + 1) * 128],
                                                         rhs=hg[:, kc, 0:n], start=(kc == 0), stop=(kc == 7)),
                      reads=[wb, hg_b[kc]], writes=[pb2])
            s_, s_b = sil[d % 2], sil_b[d % 2]
            fw.op(DVE, lambda: nc.vector.tensor_tensor(out=s_[:, 0:n], in0=ps2[:, 0:n], in1=gT[:, 8 + d, 0:n],
                                                       op=ALU.mult), reads=[pb2, gT_b[8 + d]], writes=[s_b])
            fw.op(POOL, lambda: nc.gpsimd.tensor_tensor(out=uT[:, d, 0:n], in0=s_[:, 0:n], in1=ma[:, 0:n],
                                                        op=ALU.add), reads=[s_b, ma_b], writes=[uT_b[d]])
        wo_s = [ws_next(wout), ws_next(wout, 1)]
        for d in range(8):
            wsl, wb = wo_s[d // 4]
            wv = wsl.rearrange("p (k j) -> p k j", k=8)
            ps, pb = pget()
            for kc in range(8):
                fw.op(PE, lambda kc=kc: nc.tensor.matmul(ps[:, 0:n], lhsT=wv[:, kc, (d % 4) * 128:(d % 4 + 1) * 128],
                                                         rhs=uT[:, kc, 0:n], start=(kc == 0), stop=(kc == 7)),
                      reads=[wb, uT_b[kc]], writes=[pb])
            fw.op(DVE, lambda: nc.vector.tensor_tensor(out=xT[:, d, 0:n], in0=ps[:, 0:n], in1=xT[:, d, 0:n],
                                                       op=ALU.add), reads=[pb, xT_b[d]], writes=[xT_b[d]])
        rmsnorm(xT, xT_b, 8, PG2, D, uT, uT_b, n, gT, gT_b)
        ss = pget_reserve()
        ffn(w13b, w2b, n, sumsq=ss)
        rmsnorm(xT, xT_b, 8, PGF, D, xT, xT_b, n, gT, gT_b, pre=ss)
        fw.dma(POOL, c_out, seq["y_out"].rearrange("(c p) s -> p c s", p=128)[:, :, out0:out0 + n], xT[:, :, 0:n],
               reads=xT_b)

    seq_p = dict(name="p", prompt=True, x=xp, rope=rope_p, kr_out=kr_p, kv_out=kvl_p, y_out=y_p, KT=(KN_p, KR_p), V=V_p)
    seq_s = dict(name="s", prompt=False, x=xs, rope=rope_s, kr_out=kr_s, kv_out=kvl_s, y_out=y_s, KT=(KN_s, KR_s), V=V_s)

    halo_flat = halo.rearrange("p c j -> p (c j)")
    for c in range(8):
        fw.op(POOL, lambda c=c: nc.gpsimd.memset(halo[:, c, :], 0.0), writes=[halo_b[c]])
        fw.op(POOL, lambda c=c: nc.gpsimd.memset(hst[:, c:c + 1], 0.0), writes=[hst_b[c]])
    for ti in range(NT):
        ctx = [(ti * TT, TT, True, ti)] + [(j * TT, TT, False, j) for j in range(ti)]
        if ti % 4 == 3:
            tile(seq_p, ti, ti * TT, TT, ti * TT, ctx, light=False, out0=(ti // 4) * TT)
        else:
            light_tile(seq_p, ti, ti * TT, TT, ti * TT, (ti + 1) * TT)
        if ti == 0:
            late_casts()
    assert "pending" not in seq_p and "pre_norm1" not in seq_p
    fw.dma(POOL, c_st, conv_p, halo_flat, reads=halo_b)
    fw.dma(POOL, c_st, h_p, hst, reads=hst_b)
    for j in range(PAST // TT):
        fw.dma(POOL, c_misc, ckvb[:, :, 0:TT],
               ckv_c.rearrange("(c p) s -> p c s", p=128)[:, :, j * TT:(j + 1) * TT], writes=ckvb_b)
        fw.dma(POOL, c_misc, krb[:, 0:TT], ckr_c[:, j * TT:(j + 1) * TT], writes=[krb_b])
        produce_kv(TT, (KN_s, KR_s), V_s, j * TT, "s", j)
    fw.dma(SP, c_msp, halo_flat, sconv, reads=[], writes=halo_b)
    fw.dma(SP, c_msp, hst, srg, reads=[], writes=hst_b)
    ctx = [(PAST, DEC, False, 2), (0, TT, False, 0), (TT, TT, False, 1)]
    tile(seq_s, 0, 0, DEC, PAST, ctx)
    fw.dma(POOL, c_st, conv_s, halo_flat, reads=halo_b)
    fw.dma(POOL, c_st, h_s, hst, reads=hst_b)
    fw.finish(SP)
    build_program.stats = dict(n_inst=fw.n_inst, n_wait=fw.n_wait, sbuf_left=nc.sbuf_bytes_remaining)
    return nc


_CACHE = {}


def kernel(x_prompt, x_sample, cache_kv_latent, cache_k_rope, state_conv, state_rglru,
           norm_ffn1, w1_ffn1, w3_ffn1, w2_ffn1, norm_mix, w_in,
           norm_q, w_uq, norm_kv, w_ukv, w_o_attn,
           conv_w, conv_b, w_rgate, b_rgate, w_igate, b_igate, lru_lambda, w_o_rnn,
           w_out, norm_ffn2, w1_ffn2, w3_ffn2, w2_ffn2, norm_final):
    f = lambda a: np.asarray(a, dtype=np.float32)
    x_prompt = f(x_prompt)
    x_sample = f(x_sample)
    Bp, S_P, _ = x_prompt.shape
    Bs = x_sample.shape[0]
    n_cores = 8
    assert Bs == n_cores and S_P % TT == 0
    wd = dict(norm_ffn1=f(norm_ffn1), norm_mix=f(norm_mix), norm_q=f(norm_q), norm_kv=f(norm_kv),
              norm_ffn2=f(norm_ffn2), norm_final=f(norm_final), conv_w=f(conv_w), conv_b=f(conv_b),
              b_rgate=f(b_rgate), b_igate=f(b_igate), lru_lambda=f(lru_lambda))
    shared = {
        "params": host_params(wd),
        "w13a": host_w13(f(w1_ffn1)[0], f(w3_ffn1)[0]),
        "w2a": host_w2(f(w2_ffn1)[0]),
        "w13b": host_w13(f(w1_ffn2)[0], f(w3_ffn2)[0]),
        "w2b": host_w2(f(w2_ffn2)[0]),
        "win": host_win(f(w_in)[0]),
        "wres": host_wres(f(w_uq)[0], f(w_ukv)[0], f(w_rgate)[0], f(w_igate)[0]),
        "woa": host_kc(f(w_o_attn)[0], 1024),
        "wor": host_kc(f(w_o_rnn)[0], 512),
        "wout": host_kc(f(w_out)[0], 512),
        "rope_s": rope_tables(PAST + np.arange(DEC)),
    }
    NT = S_P // TT
    assert NT % 4 == 0 and Bp * 4 == n_cores
    NF_ = NT // 4
    xpT = [np.ascontiguousarray(x_prompt[b].T) for b in range(Bp)]
    ckv = f(cache_kv_latent)[0]
    ckr = f(cache_k_rope)[0]
    sc = f(state_conv)[0]
    sh = f(state_rglru)[0]
    in_maps = []
    for c in range(n_cores):
        b, j = c // 4, c % 4
        m = dict(shared)
        xc_ = np.zeros((D, S_P), np.float32)
        pos = np.zeros((S_P,), np.int64)
        flags = np.zeros((128, 12), np.float32)
        for s_ in range(NT):
            g = s_ - (3 - j)
            if g >= 0:
                xc_[:, s_ * TT:(s_ + 1) * TT] = xpT[b][:, g * TT:(g + 1) * TT]
                pos[s_ * TT:(s_ + 1) * TT] = g * TT + np.arange(TT)
            if s_ < 4:
                flags[:, s_] = 0.0 if g == 0 else 1.0
                flags[:, 4 + s_] = 1.0 if g == 0 else 0.0
                flags[:, 8 + s_] = 1.0 if g >= 0 else 0.0
        m["xp"] = xc_
        m["rope_p"] = rope_tables(pos)
        m["flags"] = flags
        m["xs"] = np.ascontiguousarray(x_sample[c].T)
        m["ckv_c"] = np.ascontiguousarray(ckv[c].T)
        m["ckr_c"] = np.ascontiguousarray(ckr[c].T)
        m["sconv"] = np.ascontiguousarray(sc[c].reshape(3, 8, 128).transpose(2, 1, 0).reshape(128, 24))
        m["srg"] = np.ascontiguousarray(sh[c].reshape(8, 128).T)
        in_maps.append(m)
    if S_P not in _CACHE:
        _CACHE[S_P] = build_program(S_P)
    nc = _CACHE[S_P]
    res = run_bass_kernel_spmd(nc, in_maps, core_ids=list(range(n_cores)))
    R = res.results

    def unconv(a):
        return np.ascontiguousarray(a.reshape(128, 8, 3).transpose(2, 1, 0).reshape(3, 1024))

    def unh(a):
        return np.ascontiguousarray(a.T.reshape(1024))

    def gather(name, width):
        out = np.zeros((Bp, S_P, width), np.float32)
        for c in range(n_cores):
            b, j = c // 4, c % 4
            a = R[c][name]
            for k in range(NF_):
                g = 4 * k + j
                out[b, g * TT:(g + 1) * TT, :] = a[:, k * TT:(k + 1) * TT].T
        return out

    y_prompt = gather("y_p", D)
    y_sample = np.stack([np.ascontiguousarray(R[c]["y_s"].T) for c in range(n_cores)], 0)
    kvl_prompt = gather("kvl_p", 256)[None]
    kr_prompt = gather("kr_p", 32)[None]
    conv_prompt = np.stack([unconv(R[4 * b + 3]["conv_p"]) for b in range(Bp)], 0)[None]
    h_prompt = np.stack([unh(R[4 * b + 3]["h_p"]) for b in range(Bp)], 0)[None]
    kvl_sample = np.stack([np.ascontiguousarray(R[c]["kvl_s"].T) for c in range(n_cores)], 0)[None]
    kr_sample = np.stack([np.ascontiguousarray(R[c]["kr_s"].T) for c in range(n_cores)], 0)[None]
    conv_sample = np.stack([unconv(R[c]["conv_s"]) for c in range(n_cores)], 0)[None]
    h_sample = np.stack([unh(R[c]["h_s"]) for c in range(n_cores)], 0)[None]
    outs = (y_prompt, y_sample, kvl_prompt, kr_prompt, conv_prompt, h_prompt,
            kvl_sample, kr_sample, conv_sample, h_sample)
    return tuple(np.asarray(o, dtype=np.float32) for o in outs)
```

```python
import os
import bisect
import numpy as np
import concourse.bass as bass
import concourse.mybir as mybir
from concourse.bass_utils import run_bass_kernel_spmd

F32 = mybir.dt.float32
BF16 = mybir.dt.bfloat16
AF = mybir.ActivationFunctionType
ALU = mybir.AluOpType

D = 1024
DFF = 2816
NF = 22
EPS = 1e-6
TT = 512
PAST = 1024
DEC = 64
QSCALE = 96 ** -0.5
SAFE_DIST = 3


class StopBuild(Exception):
    pass


class Ctr:
    def __init__(self, fw, name):
        self.sem = fw.nc.alloc_semaphore(name)
        self.count = 0
        self.hist_t = []
        self.hist_k = []

    def snap(self, t, known):
        if self.hist_k and self.hist_k[-1] == known:
            return
        self.hist_t.append(t)
        self.hist_k.append(dict(known))

    def known_at(self, t):
        i = bisect.bisect_right(self.hist_t, t) - 1
        return self.hist_k[i] if i >= 0 else None


class Eng:
    def __init__(self, fw, name, eng):
        self.name = name
        self.eng = eng
        self.ctr = Ctr(fw, "c_" + name)
        self.known = {}
        self.n_issued = 0
        self.ticket_pos = {}


class Buf:
    __slots__ = ("name", "last_w", "reads")

    def __init__(self, name):
        self.name = name
        self.last_w = None
        self.reads = []


def _compress(reads):
    d = {}
    for c, t in reads:
        if d.get(c, 0) < t:
            d[c] = t
    return list(d.items())


class FW:
    def __init__(self, nc):
        self.nc = nc
        self.pe = Eng(self, "pe", nc.tensor)
        self.act = Eng(self, "act", nc.scalar)
        self.dve = Eng(self, "dve", nc.vector)
        self.pool = Eng(self, "pool", nc.gpsimd)
        self.sp = Eng(self, "sp", nc.sync)
        self.engs = [self.pe, self.act, self.dve, self.pool, self.sp]
        self.dma_ctrs = []
        self.dma_set = set()
        self.n_wait = 0
        self.n_inst = 0

    def dma_ctr(self, name):
        c = Ctr(self, name)
        self.dma_ctrs.append(c)
        self.dma_set.add(c)
        return c

    def _deps(self, reads, writes):
        deps = {}
        for b in reads:
            if b.last_w is not None:
                c, t = b.last_w
                if deps.get(c, 0) < t:
                    deps[c] = t
        for b in writes:
            if b.last_w is not None:
                c, t = b.last_w
                if deps.get(c, 0) < t:
                    deps[c] = t
            for c, t in b.reads:
                if deps.get(c, 0) < t:
                    deps[c] = t
        return deps

    def _wait(self, E, deps):
        for c, t in deps.items():
            if c in self.dma_set:
                t = c.count
            if c is E.ctr:
                if E is self.pe:
                    continue
            if E.known.get(c, 0) >= t:
                continue
            E.eng.wait_ge(c.sem, t)
            E.known[c] = t
            self.n_wait += 1
            k2 = c.known_at(t)
            if k2:
                for c2, t2 in k2.items():
                    if E.known.get(c2, 0) < t2:
                        E.known[c2] = t2

    def _record(self, ctr, t, reads, writes):
        for b in reads:
            b.reads.append((ctr, t))
            if len(b.reads) > 16:
                b.reads = _compress(b.reads)
        for b in writes:
            b.last_w = (ctr, t)
            b.reads = []

    def op(self, E, fn, reads=(), writes=()):
        self._wait(E, self._deps(reads, writes))
        inst = fn()
        E.ctr.count += 1
        t = E.ctr.count
        E.ctr.snap(t, E.known)
        inst.then_inc(E.ctr.sem, 1)
        E.ticket_pos[t] = E.n_issued
        E.n_issued += 1
        self.n_inst += 1
        if len(E.ticket_pos) > 64:
            for k in sorted(E.ticket_pos)[:32]:
                del E.ticket_pos[k]
        self._record(E.ctr, t, reads, writes)
        return inst

    def dma(self, Q, ctr, out, in_, reads=(), writes=(), **kw):
        self._wait(Q, self._deps(reads, writes))
        inst = Q.eng.dma_start(out=out, in_=in_, **kw)
        ctr.count += 16
        ctr.snap(ctr.count, Q.known)
        inst.then_inc(ctr.sem, 16)
        Q.n_issued += 1
        self.n_inst += 1
        self._record(ctr, ctr.count, reads, writes)
        return inst

    def finish(self, E):
        for F in self.engs:
            if F is not E and F.ctr.count > 0:
                E.eng.wait_ge(F.ctr.sem, F.ctr.count)
        for c in self.dma_ctrs:
            if c.count > 0:
                E.eng.wait_ge(c.sem, c.count)


NPAR = 101
PG1, PGM, PGQ, PGKV, PG2, PGF, PCW, PCB, PBR, PBI, PLAM = 0, 8, 16, 19, 21, 29, 37, 69, 77, 85, 93


def _pc(v, nchunk):
    return np.ascontiguousarray(np.asarray(v, np.float32).reshape(nchunk, 128).T)


def host_params(w):
    cols = [_pc(w["norm_ffn1"][0], 8), _pc(w["norm_mix"][0], 8), _pc(w["norm_q"][0], 3),
            _pc(w["norm_kv"][0], 2), _pc(w["norm_ffn2"][0], 8), _pc(w["norm_final"], 8)]
    cw = np.asarray(w["conv_w"][0], np.float32).reshape(4, 8, 128).transpose(2, 1, 0).reshape(128, 32)
    cols += [cw, _pc(w["conv_b"][0], 8), _pc(w["b_rgate"][0], 8), _pc(w["b_igate"][0], 8),
             _pc(w["lru_lambda"][0], 8)]
    p = np.concatenate(cols, axis=1)
    assert p.shape == (128, NPAR)
    return np.ascontiguousarray(p, dtype=np.float32)


def host_w13(w1, w3):
    a = np.asarray(w1, np.float32).reshape(8, 128, 11, 256)
    b = np.asarray(w3, np.float32).reshape(8, 128, 11, 256)
    s = np.stack([a, b], 0)
    return np.ascontiguousarray(s.transpose(3, 2, 0, 1, 4)).reshape(11 * 128, 4096)


def host_w2(w2):
    a = np.asarray(w2, np.float32).reshape(22, 128, 8, 128)
    return np.ascontiguousarray(a.transpose(2, 1, 0, 3)).reshape(8 * 128, 2816)


def _colblk(cols):
    n = cols.shape[1]
    out = np.zeros((128, 8, 512), np.float32)
    out[:, :, :n] = cols.reshape(8, 128, n).transpose(1, 0, 2)
    return out.reshape(128, 4096)


def host_win(w_in):
    w = np.asarray(w_in, np.float32)
    q = w[:, 0:384]
    kv = w[:, 384:640]
    kr = w[:, 640:672]
    krs = np.concatenate([kr[:, 16:32], kr[:, 0:16]], axis=1)
    rx = w[:, 672:1696]
    rg = w[:, 1696:2720]
    ga = w[:, 2720:3744]
    gb = w[:, 3744:4768]
    blks = [_colblk(np.concatenate([q, kr, krs], 1)), _colblk(kv)]
    for c2 in range(4):
        s = slice(c2 * 256, c2 * 256 + 256)
        blks.append(_colblk(np.concatenate([rx[:, s], rg[:, s]], 1)))
    for c2 in range(4):
        s = slice(c2 * 256, c2 * 256 + 256)
        blks.append(_colblk(np.concatenate([ga[:, s], gb[:, s]], 1)))
    return np.ascontiguousarray(np.stack(blks, 0)).reshape(10 * 128, 4096)


NRES = 3072 + 2048 + 2048


def host_wres(w_uq, w_ukv, w_rg, w_ig):
    uq = np.asarray(w_uq, np.float32).reshape(3, 128, 8, 96)
    nope = uq[..., 0:64]
    rope = uq[..., 64:96]
    sw = np.concatenate([rope[..., 16:32], rope[..., 0:16]], -1)
    a = np.concatenate([rope, nope, sw], -1).transpose(1, 0, 2, 3).reshape(128, 3072)
    ukv = np.asarray(w_ukv, np.float32).reshape(2, 128, 8, 128)
    kpart = ukv[..., 0:64].reshape(2, 128, 512)
    vpart = ukv[..., 64:128].reshape(2, 128, 512)
    b = np.stack([kpart, vpart], 0).transpose(2, 0, 1, 3).reshape(128, 2048)
    g = np.stack([np.asarray(w_rg, np.float32), np.asarray(w_ig, np.float32)], 0)
    c = g.transpose(2, 1, 0, 3).reshape(128, 2048)
    return np.ascontiguousarray(np.concatenate([a, b, c], 1))


def host_kc(wm, ncols_blk):
    wm = np.asarray(wm, np.float32)
    K, N = wm.shape
    a = wm.reshape(K // 128, 128, N // ncols_blk, ncols_blk)
    return np.ascontiguousarray(a.transpose(2, 1, 0, 3)).reshape((N // ncols_blk) * 128, (K // 128) * ncols_blk)


def rope_tables(pos):
    inv = (1.0 / (10000.0 ** (np.arange(0, 32, 2, dtype=np.float32) / np.float32(32)))).astype(np.float32)
    ang = pos.astype(np.float32)[:, None] * inv[None, :]
    c = np.cos(ang).astype(np.float32).T
    s = np.sin(ang).astype(np.float32).T
    cos2 = np.concatenate([c, c], 0)
    sins = np.concatenate([-s, s], 0)
    return np.ascontiguousarray(np.stack([cos2, sins], 1))


def build_program(S_P):
    NT = S_P // TT
    SK_S = PAST + DEC
    nc = bass.Bass("TRN2", target_bir_lowering=False)
    fw = FW(nc)
    PE, ACT, DVE, POOL, SP = fw.pe, fw.act, fw.dve, fw.pool, fw.sp

    def din(name, shape, dt=F32):
        return nc.dram_tensor(name, list(shape), dt, kind="ExternalInput").ap()

    def dout(name, shape):
        return nc.dram_tensor(name, list(shape), F32, kind="ExternalOutput").ap()

    def dscr(name, shape, dt=BF16):
        return nc.dram_tensor(name, list(shape), dt, kind="Internal").ap()

    xp = din("xp", [D, S_P])
    xs = din("xs", [D, DEC])
    ckv_c = din("ckv_c", [256, PAST])
    ckr_c = din("ckr_c", [32, PAST])
    sconv = din("sconv", [128, 24])
    srg = din("srg", [128, 8])
    rope_p = din("rope_p", [32, 2, S_P])
    rope_s = din("rope_s", [32, 2, DEC])
    params_d = din("params", [128, NPAR])
    w13a_f = din("w13a", [11 * 128, 4096])
    w2a_f = din("w2a", [8 * 128, 2816])
    w13b_f = din("w13b", [11 * 128, 4096])
    w2b_f = din("w2b", [8 * 128, 2816])
    win_f = din("win", [10 * 128, 4096])
    wres_f = din("wres", [128, NRES])
    woa_f = din("woa", [128, 4096])
    wor_f = din("wor", [2 * 128, 4096])
    wout_f = din("wout", [2 * 128, 4096])

    NF_ = NT // 4
    S_O = NF_ * TT
    flags_d = din("flags", [128, 12])
    y_p = dout("y_p", [D, S_O])
    kvl_p = dout("kvl_p", [256, S_O])
    kr_p = dout("kr_p", [32, S_O])
    conv_p = dout("conv_p", [128, 24])
    h_p = dout("h_p", [128, 8])
    y_s = dout("y_s", [D, DEC])
    kvl_s = dout("kvl_s", [256, DEC])
    kr_s = dout("kr_s", [32, DEC])
    conv_s = dout("conv_s", [128, 24])
    h_s = dout("h_s", [128, 8])

    w13a = dscr("w13a_b", [11 * 128, 4096])
    w2a = dscr("w2a_b", [8 * 128, 2816])
    w13b = dscr("w13b_b", [11 * 128, 4096])
    w2b = dscr("w2b_b", [8 * 128, 2816])
    win = dscr("win_b", [10 * 128, 4096])
    woa = dscr("woa_b", [128, 4096])
    wor = dscr("wor_b", [2 * 128, 4096])
    wout = dscr("wout_b", [2 * 128, 4096])
    KN_p = dscr("KN_p", [4, 128, S_P])
    KR_p = dscr("KR_p", [32, S_P])
    V_p = dscr("V_p", [S_P, 1024])
    KN_s = dscr("KN_s", [4, 128, SK_S])
    KR_s = dscr("KR_s", [32, SK_S])
    V_s = dscr("V_s", [SK_S, 1024])

    def sb(name, shape, dt=F32):
        return nc.alloc_sbuf_tensor("sb_" + name, list(shape), dt).ap()

    xT = sb("xT", [128, 8, TT])
    xT_b = [Buf("xT%d" % c) for c in range(8)]
    uT = sb("uT", [128, 8, TT], BF16)
    uT_b = [Buf("uT%d" % c) for c in range(8)]
    gT = sb("gT", [128, NF, TT], BF16)
    gT_b = [Buf("gT%d" % c) for c in range(NF)]
    rstd = sb("rstd", [128, TT])
    rstd_b = Buf("rstd")
    sil = [sb("sil%d" % i, [128, TT]) for i in range(2)]
    sil_b = [Buf("sil%d" % i) for i in range(2)]
    zq = sb("zq", [128, 3, TT])
    zq_b = [Buf("zq%d" % c) for c in range(3)]
    qn = sb("qn", [128, 3, TT], BF16)
    qn_b = [Buf("qn%d" % c) for c in range(3)]
    kvraw = sb("kvraw", [128, 2, TT])
    kvraw_b = [Buf("kvraw%d" % c) for c in range(2)]
    ckv = sb("ckv", [128, 2, TT])
    ckv_b = [Buf("ckv%d" % c) for c in range(2)]
    ckvb = sb("ckvb", [128, 2, TT], BF16)
    ckvb_b = [Buf("ckvb%d" % c) for c in range(2)]
    krraw = sb("krraw", [32, 2, TT])
    krraw_b = [Buf("krraw0"), Buf("krraw1")]
    krf = sb("krf", [32, TT])
    krf_b = Buf("krf")
    krb = sb("krb", [32, TT], BF16)
    krb_b = Buf("krb")
    cs = sb("cs", [32, 2, TT])
    cs_b = Buf("cs")
    rt = [sb("rt%d" % i, [32, TT]) for i in range(2)]
    rt_b = [Buf("rt0"), Buf("rt1")]
    Xb = [sb("Xb%d" % i, [128, TT + 3]) for i in range(2)]
    Xb_b = [Buf("Xb0"), Buf("Xb1")]
    xc = sb("xc", [128, TT])
    xc_b = Buf("xc")
    xcb = sb("xcb", [128, TT], BF16)
    xcb_b = Buf("xcb")
    rbuf = sb("rbuf", [128, TT])
    rbuf_b = Buf("rbuf")
    a2buf = sb("a2buf", [128, TT])
    a2buf_b = Buf("a2buf")
    igbuf = sb("igbuf", [128, TT])
    igbuf_b = Buf("igbuf")
    hbuf = sb("hbuf", [128, TT])
    hbuf_b = Buf("hbuf")
    gg = [sb("gg%d" % i, [128, TT], BF16) for i in range(2)]
    gg_b = [Buf("gg0"), Buf("gg1")]
    freg = sb("freg", [128, 12288], BF16)
    hg = freg[:, 0:4096].rearrange("p (c t) -> p c t", c=8)
    hg_b = [Buf("hg%d" % c) for c in range(8)]
    QT = freg[:, 4096:8192].rearrange("p (c t) -> p c t", c=8)
    QT_b = [Buf("QT%d" % c) for c in range(8)]
    oT = freg[:, 8192:10240].rearrange("p (c t) -> p c t", c=4)
    oT_b = [Buf("oT%d" % c) for c in range(4)]
    ma = freg[:, 10240:11264].bitcast(F32)
    ma_b = Buf("ma")
    rec = freg[:, 11264:12288].bitcast(F32)
    rec_b = Buf("rec")
    xr_all = freg[:, 0:8240].bitcast(F32).rearrange("p (c t) -> p c t", c=8)
    xr_b = [Buf("xr%d" % c) for c in range(8)]
    xcL = [freg[:, 8256:9280].bitcast(F32), freg[:, 9280:10304].bitcast(F32), freg[:, 10304:11328].bitcast(F32)]
    xcL_b = [Buf("xcL0"), Buf("xcL1"), Buf("xcL2")]
    xcb2 = sb("xcb2", [128, TT], BF16)
    xcb2_b = Buf("xcb2")
    alias_b = hg_b + QT_b + oT_b + [ma_b, rec_b] + xr_b + xcL_b
    fdummy = sb("fdummy", [128, 8])
    knT = sb("knT", [128, 4, TT], BF16)
    knT_b = [Buf("knT%d" % c) for c in range(4)]
    vst = sb("vst", [128, 4, 8, 128], BF16)
    vst_b = Buf("vst")
    NKS = 2
    kslot = [sb("kslot%d" % i, [96, 4, TT], BF16) for i in range(NKS)]
    kslot_b = [Buf("kslot%d" % i) for i in range(NKS)]
    vslot = [sb("vslot%d" % i, [128, 4, 4, 128], BF16) for i in range(NKS)]
    vslot_b = [Buf("vslot%d" % i) for i in range(NKS)]
    NPT = 4
    PTs = [sb("PT%d" % i, [128, TT], BF16) for i in range(NPT)]
    PT_b = [Buf("PT%d" % i) for i in range(NPT)]
    NWS = 4
    wslot = [sb("wslot%d" % i, [128, 4096], BF16) for i in range(NWS)]
    wslot_b = [Buf("wslot%d" % i) for i in range(NWS)]
    wslot_c = [fw.dma_ctr("wsl%d" % i) for i in range(NWS)]
    wres = sb("wres", [128, NRES], BF16)
    wres_b = Buf("wres")
    ones = sb("ones", [128, 128], BF16)
    ones_b = Buf("ones")
    par = sb("par", [128, NPAR])
    par_b = Buf("par")
    flg = sb("flg", [128, 12])
    flg_b = Buf("flg")
    onesv = sb("onesv", [128, 4, 64], BF16)
    onesv_b = Buf("onesv")
    nsp = sb("nsp", [128, 16])
    nsp_b = Buf("nsp")
    halo = sb("halo", [128, 8, 3])
    halo_b = [Buf("halo%d" % c) for c in range(8)]
    hst = sb("hst", [128, 8])
    hst_b = [Buf("hst%d" % c) for c in range(8)]

    wuq = wres[:, 0:3072].rearrange("p (k h j) -> p k h j", k=3, h=8)
    wukv = wres[:, 3072:5120].rearrange("p (e k j) -> p e k j", e=2, k=2)
    wgate = wres[:, 5120:7168].rearrange("p (n e j) -> p n e j", n=8, e=2)

    psum = [nc.alloc_psum_tensor("ps%d" % i, [128, TT], F32).ap() for i in range(8)]
    psum_b = [Buf("ps%d" % i) for i in range(8)]
    prr = [0]

    reserved = set()

    def pget(lo=0, hi=8):
        while True:
            i = lo + prr[0] % (hi - lo)
            prr[0] += 1
            if i not in reserved:
                return psum[i], psum_b[i]

    def pget_reserve():
        ps, pb = pget()
        reserved.add(psum_b.index(pb))
        return ps, pb

    c_x = fw.dma_ctr("c_x")
    c_misc = fw.dma_ctr("c_misc")
    c_msp = fw.dma_ctr("c_msp")
    c_cs = fw.dma_ctr("c_cs")
    c_out = fw.dma_ctr("c_out")
    c_kv = fw.dma_ctr("c_kv")
    c_kr = fw.dma_ctr("c_kr")
    c_kn = fw.dma_ctr("c_kn")
    c_krs = fw.dma_ctr("c_krs")
    c_vst = fw.dma_ctr("c_vst")
    c_ks = [fw.dma_ctr("c_ks%d" % i) for i in range(NKS)]
    c_vs = [fw.dma_ctr("c_vs%d" % i) for i in range(NKS)]
    c_cast = fw.dma_ctr("c_cast")
    c_st = fw.dma_ctr("c_st")

    wdram_b = Buf("wdram")
    KV_b = {}

    def kvbuf(seqname, tile):
        k = (seqname, tile)
        if k not in KV_b:
            KV_b[k] = Buf("kv_%s_%d" % k)
        return KV_b[k]

    wdram2_b = Buf("wdram2")
    c_cast2 = fw.dma_ctr("c_cast2")
    cast_ctx = {"ctr": c_cast, "buf": wdram_b}

    def cast_copy(dst, src, rows, cols):
        step = max(1, (1 << 20) // cols)
        r = 0
        cc, bb = cast_ctx["ctr"], cast_ctx["buf"]
        while r < rows:
            rr = min(step, rows - r)
            if cc.count >= 32:
                POOL.eng.wait_ge(cc.sem, cc.count - 16)
            fw.dma(POOL, cc, dst[r:r + rr, :], src[r:r + rr, :], writes=[bb])
            r += rr

    cast_copy(w13a, w13a_f, 11 * 128, 4096)
    cast_copy(w2a, w2a_f, 8 * 128, 2816)
    cast_copy(win, win_f, 10 * 128, 4096)

    def late_casts():
        cast_ctx["ctr"], cast_ctx["buf"] = c_cast2, wdram2_b
        cast_copy(woa, woa_f, 128, 4096)
        cast_copy(wor, wor_f, 256, 4096)
        cast_copy(wout, wout_f, 256, 4096)
        cast_copy(w13b, w13b_f, 11 * 128, 4096)
        cast_copy(w2b, w2b_f, 8 * 128, 2816)

    fw.dma(POOL, c_misc, wres, wres_f, writes=[wres_b])
    fw.dma(SP, c_msp, par, params_d, writes=[par_b])
    fw.dma(SP, c_msp, flg, flags_d, writes=[flg_b])
    fw.op(DVE, lambda: nc.vector.memset(onesv, 1.0), writes=[onesv_b])
    fw.op(DVE, lambda: nc.vector.memset(ones, 1.0), writes=[ones_b])
    fw.op(DVE, lambda: nc.vector.memset(vst, 1.0), writes=[vst_b])
    fw.op(ACT, lambda: nc.scalar.activation(out=nsp[:, 0:8], in_=par[:, PLAM:PLAM + 8], func=AF.Exp, scale=-1.0),
          reads=[par_b], writes=[nsp_b])
    fw.op(ACT, lambda: nc.scalar.activation(out=nsp[:, 0:8], in_=nsp[:, 0:8], func=AF.Ln, bias=1.0, scale=1.0),
          reads=[nsp_b], writes=[nsp_b])
    fw.op(DVE, lambda: nc.vector.tensor_scalar(out=nsp[:, 8:16], in0=nsp[:, 0:8], scalar1=-16.0, scalar2=None,
                                               op0=ALU.mult), reads=[nsp_b], writes=[nsp_b])
    fw.op(DVE, lambda: nc.vector.tensor_scalar(out=nsp[:, 0:8], in0=nsp[:, 0:8], scalar1=-8.0, scalar2=None,
                                               op0=ALU.mult), reads=[nsp_b], writes=[nsp_b])

    blocks_L = [(w13a, g, 4096) for g in range(11)] + [(w2a, d, 2816) for d in range(8)] + \
               [(win, i, 4096) for i in range(6)]
    blocks_F = [(w13a, g, 4096) for g in range(11)] + [(w2a, d, 2816) for d in range(8)] + \
               [(win, i, 4096) for i in range(10)] + [(woa, 0, 4096), (wor, 0, 4096), (wor, 1, 4096),
                                                       (wout, 0, 4096), (wout, 1, 4096)] + \
               [(w13b, g, 4096) for g in range(11)] + [(w2b, d, 2816) for d in range(8)]
    tile_blocks = []
    for k in range(NF_):
        tile_blocks += blocks_L * 3 + blocks_F
    tile_blocks += blocks_F
    total_blocks = len(tile_blocks)
    NBLK = total_blocks
    ws = {"issued": 0, "used": 0}

    def ws_issue():
        i = ws["issued"]
        if i >= total_blocks:
            return
        t, idx, E = tile_blocks[i % NBLK]
        s = i % NWS
        src_b = wdram_b if (t is w13a or t is w2a or t is win) else wdram2_b
        fw.dma(SP, wslot_c[s], wslot[s][:, 0:E], t[idx * 128:(idx + 1) * 128, 0:E], reads=[src_b],
               writes=[wslot_b[s]])
        ws["issued"] += 1

    def ws_next(expect, hold=0):
        i = ws["used"]
        assert tile_blocks[i % NBLK][0] is expect, "weight stream order mismatch"
        while ws["issued"] < min(total_blocks, i + NWS - hold):
            ws_issue()
        ws["used"] += 1
        s = i % NWS
        return wslot[s], wslot_b[s]

    def norm_sums(src, src_b, nch, n, sqbuf, sqbuf_b, reserve=False):
        ps, pb = pget_reserve() if reserve else pget()
        for c in range(nch):
            if c % 2 == 0:
                fw.op(POOL, lambda c=c: nc.gpsimd.tensor_tensor(out=sqbuf[:, c, 0:n], in0=src[:, c, 0:n],
                                                                in1=src[:, c, 0:n], op=ALU.mult),
                      reads=[src_b[c]], writes=[sqbuf_b[c]])
            else:
                fw.op(DVE, lambda c=c: nc.vector.tensor_tensor(out=sqbuf[:, c, 0:n], in0=src[:, c, 0:n],
                                                               in1=src[:, c, 0:n], op=ALU.mult),
                      reads=[src_b[c]], writes=[sqbuf_b[c]])
            fw.op(PE, lambda c=c: nc.tensor.matmul(ps[:, 0:n], lhsT=ones, rhs=sqbuf[:, c, 0:n], start=(c == 0),
                                                   stop=(c == nch - 1)),
                  reads=[sqbuf_b[c], ones_b], writes=[pb])
        return ps, pb

    def rmsnorm(src, src_b, nch, gcol, dim, out, out_b, n, sqbuf, sqbuf_b, pre=None):
        if pre is not None:
            ps, pb = pre
            reserved.discard(psum_b.index(pb))
        else:
            ps, pb = norm_sums(src, src_b, nch, n, sqbuf, sqbuf_b)
        fw.op(ACT, lambda: nc.scalar.activation(out=rstd[:, 0:n], in_=ps[:, 0:n], func=AF.Sqrt, scale=1.0 / dim,
                                                bias=EPS), reads=[pb], writes=[rstd_b])
        fw.op(DVE, lambda: nc.vector.reciprocal(out=rstd[:, 0:n], in_=rstd[:, 0:n]), reads=[rstd_b],
              writes=[rstd_b])
        for c in range(nch):
            fw.op(DVE, lambda c=c: nc.vector.scalar_tensor_tensor(out=out[:, c, 0:n], in0=src[:, c, 0:n],
                                                                  scalar=par[:, gcol + c:gcol + c + 1],
                                                                  in1=rstd[:, 0:n], op0=ALU.mult, op1=ALU.mult),
                  reads=[src_b[c], rstd_b, par_b], writes=[out_b[c]])

    def ffn_gen(w13t, w2t, n, sumsq=None):
        si = 0
        for g in range(11):
            wsl, wb = ws_next(w13t)
            wv = wsl.rearrange("p (e k j) -> p e k j", e=2, k=8)
            for e in range(2):
                f = 2 * g + e
                pa, pab = pget()
                pb_, pbb = pget()
                for k in range(8):
                    fw.op(PE, lambda k=k: nc.tensor.matmul(pa[:, 0:n], lhsT=wv[:, 0, k, e * 128:(e + 1) * 128],
                                                           rhs=uT[:, k, 0:n], start=(k == 0), stop=(k == 7)),
                          reads=[wb, uT_b[k]], writes=[pab])
                for k in range(8):
                    fw.op(PE, lambda k=k: nc.tensor.matmul(pb_[:, 0:n], lhsT=wv[:, 1, k, e * 128:(e + 1) * 128],
                                                           rhs=uT[:, k, 0:n], start=(k == 0), stop=(k == 7)),
                          reads=[wb, uT_b[k]], writes=[pbb])
                s = sil[si % 2]
                s_b = sil_b[si % 2]
                si += 1
                fw.op(ACT, lambda: nc.scalar.activation(out=s[:, 0:n], in_=pa[:, 0:n], func=AF.Silu),
                      reads=[pab], writes=[s_b])
                fw.op(DVE, lambda: nc.vector.tensor_tensor(out=gT[:, f, 0:n], in0=pb_[:, 0:n], in1=s[:, 0:n],
                                                           op=ALU.mult), reads=[pbb, s_b], writes=[gT_b[f]])
            yield
        for d in range(8):
            wsl, wb = ws_next(w2t)
            wv = wsl[:, 0:2816].rearrange("p (f j) -> p f j", f=NF)
            pd, pdb = pget()
            for f in range(NF):
                fw.op(PE, lambda f=f: nc.tensor.matmul(pd[:, 0:n], lhsT=wv[:, f, :], rhs=gT[:, f, 0:n],
                                                       start=(f == 0), stop=(f == NF - 1)),
                      reads=[wb, gT_b[f]], writes=[pdb])
            fw.op(DVE, lambda: nc.vector.scalar_tensor_tensor(out=xT[:, d, 0:n], in0=pd[:, 0:n], scalar=0.5,
                                                              in1=xT[:, d, 0:n], op0=ALU.mult, op1=ALU.add),
                  reads=[pdb, xT_b[d]], writes=[xT_b[d]])
            if sumsq is not None:
                sps, spb = sumsq
                if d > 0:
                    fw.op(PE, lambda: nc.tensor.matmul(sps[:, 0:n], lhsT=ones, rhs=sqv[(d - 1) % 2][:, 0:n],
                                                       start=(d == 1), stop=False),
                          reads=[sil_b[(d - 1) % 2], ones_b], writes=[spb])
                fw.op(ACT, lambda: nc.scalar.activation(out=sqv[d % 2][:, 0:n], in_=xT[:, d, 0:n], func=AF.Square),
                      reads=[xT_b[d]], writes=[sil_b[d % 2]])
                if d == 7:
                    fw.op(PE, lambda: nc.tensor.matmul(sps[:, 0:n], lhsT=ones, rhs=sqv[1][:, 0:n], start=False,
                                                       stop=True), reads=[sil_b[1], ones_b], writes=[spb])
            yield

    sqv = [sil[0].bitcast(BF16), sil[1].bitcast(BF16)]

    def ffn(w13t, w2t, n, side=None, sumsq=None):
        for _ in ffn_gen(w13t, w2t, n, sumsq):
            if side is not None:
                try:
                    next(side)
                except StopIteration:
                    side = None
        if side is not None:
            for _ in side:
                pass

    def fence(bufs):
        fw.op(POOL, lambda: nc.gpsimd.memset(fdummy[:, 0:1], 0.0), writes=bufs)

    def produce_kv(n, KTd, Vd, key0, seqname, tile_id, vflag=None):
        kb_ = kvbuf(seqname, tile_id)
        for p in range(4):
            ps, pb = pget()
            for kc in range(2):
                fw.op(PE, lambda kc=kc: nc.tensor.matmul(ps[:, 0:n], lhsT=wukv[:, 0, kc, p * 128:(p + 1) * 128],
                                                         rhs=ckvb[:, kc, 0:n], start=(kc == 0), stop=(kc == 1)),
                      reads=[wres_b, ckvb_b[kc]], writes=[pb])
            fw.op(ACT, lambda: nc.scalar.copy(out=knT[:, p, 0:n], in_=ps[:, 0:n]), reads=[pb], writes=[knT_b[p]])
            fw.dma(ACT, c_kn, KTd[0][p, :, key0:key0 + n], knT[:, p, 0:n], reads=[knT_b[p]], writes=[kb_])
        fw.dma(SP, c_krs, KTd[1][:, key0:key0 + n], krb[:, 0:n], reads=[krb_b], writes=[kb_])
        ntb = (n + 127) // 128
        for tb in range(ntb):
            rows = min(128, n - tb * 128)
            ps, pb = pget()
            for kc in range(2):
                fw.op(PE, lambda kc=kc: nc.tensor.matmul(ps[0:rows, 0:512],
                                                         lhsT=ckvb[:, kc, tb * 128:tb * 128 + rows],
                                                         rhs=wukv[:, 1, kc, :], start=(kc == 0), stop=(kc == 1)),
                      reads=[wres_b, ckvb_b[kc]], writes=[pb])
            psv = ps[0:rows, 0:512].rearrange("p (h e j) -> p h e j", h=4, e=2)
            vv = vst[0:rows, tb].rearrange("p (h e) j -> p h e j", e=2)
            fw.op(DVE, lambda: nc.vector.tensor_copy(out=vv[:, :, 0, 0:64], in_=psv[:, :, 0, :]), reads=[pb],
                  writes=[vst_b])
            fw.op(ACT, lambda: nc.scalar.copy(out=vv[:, :, 1, 64:128], in_=psv[:, :, 1, :]), reads=[pb],
                  writes=[vst_b])
            if vflag is not None:
                fw.op(DVE, lambda: nc.vector.tensor_scalar_mul(out=vv[:, :, 0, 64:128], in0=onesv[0:rows],
                                                               scalar1=flg[0:rows, vflag:vflag + 1]),
                      reads=[onesv_b, flg_b], writes=[vst_b])
                fw.op(DVE, lambda: nc.vector.tensor_scalar_mul(out=vv[:, :, 1, 0:64], in0=onesv[0:rows],
                                                               scalar1=flg[0:rows, vflag:vflag + 1]),
                      reads=[onesv_b, flg_b], writes=[vst_b])
        if n % 128 == 0:
            fw.dma(SP, c_vst, Vd[key0:key0 + n, :].rearrange("(tb p) c -> p tb c", p=128),
                   vst[:, 0:ntb].rearrange("p tb h j -> p tb (h j)"), reads=[vst_b], writes=[kb_])
        else:
            fw.dma(SP, c_vst, Vd[key0:key0 + n, :], vst[0:n, 0].rearrange("p h j -> p (h j)"), reads=[vst_b],
                   writes=[kb_])

    def attention(n, KTd, Vd, ctx, seqname):
        ld = [0]
        n_ctx = len(ctx)
        for hg_ in range(2):
            oacc = [(psum[4 + j], psum_b[4 + j]) for j in range(4)]
            first = [True] * 4
            slots = {}

            def load(ci):
                key0, nk, diag, tile_id = ctx[ci]
                s = ld[0] % NKS
                ld[0] += 1
                slots[ci] = s
                kb_ = kvbuf(seqname, tile_id)
                fw.dma(SP, c_ks[s], kslot[s][32:96, :, 0:nk],
                       KTd[0][2 * hg_:2 * hg_ + 2, :, key0:key0 + nk].rearrange("p (e j) s -> j (p e) s", e=2),
                       reads=[kb_], writes=[kslot_b[s]])
                fw.dma(SP, c_ks[s], kslot[s][0:32, :, 0:nk],
                       KTd[1][:, key0:key0 + nk].unsqueeze(1).broadcast_to([32, 4, nk]),
                       reads=[kb_], writes=[kslot_b[s]])
                nkb = (nk + 127) // 128
                if nk % 128 == 0:
                    fw.dma(SP, c_vs[s], vslot[s][:, 0:nkb].rearrange("p kb h j -> p kb (h j)"),
                           Vd[key0:key0 + nk, 512 * hg_:512 * hg_ + 512].rearrange("(kb p) c -> p kb c", p=128),
                           reads=[kb_], writes=[vslot_b[s]])
                else:
                    fw.dma(SP, c_vs[s], vslot[s][0:nk, 0].rearrange("p h j -> p (h j)"),
                           Vd[key0:key0 + nk, 512 * hg_:512 * hg_ + 512], reads=[kb_], writes=[vslot_b[s]])

            pend = []
            pt_i = [0]

            def do_pv(it):
                (s, kb, rows, hh, last, pt, ptb) = it
                oa, oab = oacc[hh]
                fw.op(PE, lambda: nc.tensor.matmul(oa[:, 0:n], lhsT=vslot[s][0:rows, kb, hh, :], rhs=pt[0:rows, 0:n],
                                                   start=first[hh], stop=last),
                      reads=[vslot_b[s], ptb], writes=[oab])
                first[hh] = False

            load(0)
            for ci, (key0, nk, diag, tile_id) in enumerate(ctx):
                s = slots[ci]
                nkb = (nk + 127) // 128
                idx = 0
                for kb in range(nkb):
                    rows = min(128, nk - kb * 128)
                    for hh in range(4):
                        last = (ci == n_ctx - 1) and (kb == nkb - 1)
                        h = 4 * hg_ + hh
                        sp_, spb = pget(0, 3)
                        fw.op(PE, lambda: nc.tensor.matmul(sp_[0:rows, 0:n],
                                                           lhsT=kslot[s][0:96, hh, kb * 128:kb * 128 + rows],
                                                           rhs=QT[0:96, h, 0:n], start=True, stop=True),
                              reads=[kslot_b[s], QT_b[h]], writes=[spb])
                        pi = pt_i[0] % NPT
                        pt_i[0] += 1
                        pt, ptb = PTs[pi], PT_b[pi]
                        if diag:
                            c0 = kb * 128
                            fw.op(ACT, lambda: nc.scalar.activation(out=pt[0:rows, c0:n], in_=sp_[0:rows, c0:n],
                                                                    func=AF.Exp), reads=[spb], writes=[ptb])
                            if c0 > 0:
                                fw.op(POOL, lambda: nc.gpsimd.memset(pt[0:rows, 0:c0], 0.0), writes=[ptb])
                            fw.op(POOL, lambda: nc.gpsimd.memset(pt[64:128, c0:c0 + 64], 0.0), writes=[ptb])
                        else:
                            fw.op(ACT, lambda: nc.scalar.activation(out=pt[0:rows, 0:n], in_=sp_[0:rows, 0:n],
                                                                    func=AF.Exp), reads=[spb], writes=[ptb])
                        pend.append((s, kb, rows, hh, last, pt, ptb))
                        if len(pend) > 2:
                            do_pv(pend.pop(0))
                        if idx == 2 and ci + 1 < n_ctx:
                            load(ci + 1)
                        idx += 1
            while pend:
                do_pv(pend.pop(0))
            for hh in range(4):
                h = 4 * hg_ + hh
                oa, oab = oacc[hh]
                if h % 2 == 0:
                    num, den = slice(0, 64), slice(64, 128)
                else:
                    num, den = slice(64, 128), slice(0, 64)
                fw.op(DVE, lambda: nc.vector.reciprocal(out=rec[num, 0:n], in_=oa[den, 0:n]), reads=[oab],
                      writes=[rec_b])
                fw.op(DVE, lambda: nc.vector.tensor_tensor(out=oT[num, h // 2, 0:n], in0=oa[num, 0:n],
                                                           in1=rec[num, 0:n], op=ALU.mult),
                      reads=[oab, rec_b], writes=[oT_b[h // 2]])

    def copy_act(out, in_, scale=None):
        if scale is None:
            return nc.scalar.activation(out=out, in_=in_, func=AF.Copy)
        return nc.scalar.activation(out=out, in_=in_, func=AF.Copy, scale=scale)

    def proj8(wv, col0, ncol, n, prow=128):
        ps, pb = pget()
        for k in range(8):
            fw.op(PE, lambda k=k: nc.tensor.matmul(ps[0:ncol, 0:n], lhsT=wv[:, k, col0:col0 + ncol],
                                                   rhs=uT[:, k, 0:n], start=(k == 0), stop=(k == 7)),
                  reads=[wv_b[0], uT_b[k]], writes=[pb])
        return ps, pb

    wv_b = [None]

    def rnn_s1(c, n, X, X_b, xc_, xc_b_, xcb_=None, xcb_b_=None):
        if xcb_ is None:
            xcb_, xcb_b_ = xcb, xcb_b
        fw.op(POOL, lambda: nc.gpsimd.tensor_copy(out=X[:, 0:3], in_=halo[:, c, :]), reads=[halo_b[c]],
              writes=[X_b])
        fw.op(POOL, lambda: nc.gpsimd.tensor_copy(out=halo[:, c, :], in_=X[:, n:n + 3]), reads=[X_b],
              writes=[halo_b[c]])
        cw = PCW + 4 * c
        fw.op(DVE, lambda: nc.vector.tensor_scalar(out=xc_[:, 0:n], in0=X[:, 0:n], scalar1=par[:, cw:cw + 1],
                                                   scalar2=par[:, PCB + c:PCB + c + 1], op0=ALU.mult,
                                                   op1=ALU.add), reads=[X_b, par_b], writes=[xc_b_])
        for j in range(1, 4):
            fw.op(DVE, lambda j=j: nc.vector.scalar_tensor_tensor(out=xc_[:, 0:n], in0=X[:, j:j + n],
                                                                  scalar=par[:, cw + j:cw + j + 1],
                                                                  in1=xc_[:, 0:n], op0=ALU.mult, op1=ALU.add),
                  reads=[X_b, par_b, xc_b_], writes=[xc_b_])
        fw.op(POOL, lambda: nc.gpsimd.tensor_copy(out=xcb_[:, 0:n], in_=xc_[:, 0:n]), reads=[xc_b_],
              writes=[xcb_b_])

    def rnn_s2(seq, ti, c, n, xcb_=None, xcb_b_=None):
        if xcb_ is None:
            xcb_, xcb_b_ = xcb, xcb_b
        pr, prb = pget()
        fw.op(PE, lambda: nc.tensor.matmul(pr[:, 0:n], lhsT=wgate[:, c, 0, :], rhs=xcb_[:, 0:n], start=True,
                                           stop=True), reads=[wres_b, xcb_b_], writes=[prb])
        pi_, pib = pget()
        fw.op(PE, lambda: nc.tensor.matmul(pi_[:, 0:n], lhsT=wgate[:, c, 1, :], rhs=xcb_[:, 0:n], start=True,
                                           stop=True), reads=[wres_b, xcb_b_], writes=[pib])
        fw.op(ACT, lambda: nc.scalar.activation(out=rbuf[:, 0:n], in_=pr[:, 0:n], func=AF.Sigmoid,
                                                bias=par[:, PBR + c:PBR + c + 1]),
              reads=[prb, par_b], writes=[rbuf_b])
        fw.op(ACT, lambda: nc.scalar.activation(out=igbuf[:, 0:n], in_=pi_[:, 0:n], func=AF.Sigmoid,
                                                bias=par[:, PBI + c:PBI + c + 1]),
              reads=[pib, par_b], writes=[igbuf_b])
        fw.op(ACT, lambda: nc.scalar.activation(out=a2buf[:, 0:n], in_=rbuf[:, 0:n], func=AF.Exp,
                                                scale=nsp[:, 8 + c:9 + c]), reads=[rbuf_b, nsp_b],
              writes=[a2buf_b])
        fw.op(ACT, lambda: nc.scalar.activation(out=rbuf[:, 0:n], in_=rbuf[:, 0:n], func=AF.Exp,
                                                scale=nsp[:, c:c + 1]), reads=[rbuf_b, nsp_b], writes=[rbuf_b])
        fw.op(DVE, lambda: nc.vector.tensor_scalar_min(out=a2buf[:, 0:n], in0=a2buf[:, 0:n], scalar1=1.0),
              reads=[a2buf_b], writes=[a2buf_b])
        fw.op(ACT, lambda: nc.scalar.activation(out=a2buf[:, 0:n], in_=a2buf[:, 0:n], func=AF.Sqrt, scale=-1.0,
                                                bias=1.0), reads=[a2buf_b], writes=[a2buf_b])
        if seq["prompt"] and ti < 4:
            fw.op(DVE, lambda: nc.vector.tensor_scalar_mul(out=rbuf[:, 0:1], in0=rbuf[:, 0:1],
                                                           scalar1=flg[:, ti:ti + 1]),
                  reads=[rbuf_b, flg_b], writes=[rbuf_b])
            fw.op(DVE, lambda: nc.vector.tensor_scalar(out=a2buf[:, 0:1], in0=a2buf[:, 0:1],
                                                       scalar1=flg[:, ti:ti + 1], scalar2=flg[:, 4 + ti:5 + ti],
                                                       op0=ALU.mult, op1=ALU.add),
                  reads=[a2buf_b, flg_b], writes=[a2buf_b])

    def rnn_s3(c, n, xc_, xc_b_):
        fw.op(DVE, lambda: nc.vector.tensor_tensor(out=igbuf[:, 0:n], in0=igbuf[:, 0:n], in1=xc_[:, 0:n],
                                                   op=ALU.mult), reads=[igbuf_b, xc_b_], writes=[igbuf_b])
        fw.op(DVE, lambda: nc.vector.tensor_tensor(out=igbuf[:, 0:n], in0=igbuf[:, 0:n], in1=a2buf[:, 0:n],
                                                   op=ALU.mult), reads=[igbuf_b, a2buf_b], writes=[igbuf_b])
        fw.op(DVE, lambda: nc.vector.tensor_tensor_scan(out=hbuf[:, 0:n], data0=rbuf[:, 0:n], data1=igbuf[:, 0:n],
                                                        initial=hst[:, c:c + 1], op0=ALU.mult, op1=ALU.add),
              reads=[rbuf_b, igbuf_b, hst_b[c]], writes=[hbuf_b])
        fw.op(POOL, lambda: nc.gpsimd.tensor_copy(out=hst[:, c:c + 1], in_=hbuf[:, n - 1:n]), reads=[hbuf_b],
              writes=[hst_b[c]])

    def rnn_chunk(seq, ti, c, n, psx, psx_b, psg, psg_b):
        X, X_b = Xb[c % 2], Xb_b[c % 2]
        fw.op(ACT, lambda: copy_act(X[:, 3:3 + n], psx[:, 0:n]), reads=[psx_b], writes=[X_b])
        rnn_s1(c, n, X, X_b, xc, xc_b)
        rnn_s2(seq, ti, c, n)
        rnn_s3(c, n, xc, xc_b)
        g_, g_b = gg[c % 2], gg_b[c % 2]
        fw.op(ACT, lambda: nc.scalar.activation(out=g_[:, 0:n], in_=psg[:, 0:n], func=AF.Gelu_apprx_tanh),
              reads=[psg_b], writes=[g_b])
        fw.op(DVE, lambda: nc.vector.tensor_tensor(out=hg[:, c, 0:n], in0=hbuf[:, 0:n], in1=g_[:, 0:n],
                                                   op=ALU.mult), reads=[hbuf_b, g_b], writes=[hg_b[c]])

    def rnn_chain_gen(seq, ti, n):
        xcbs = [(xcb, xcb_b), (xcb2, xcb2_b)]
        for step in range(11):
            if 0 <= step - 3 < 8:
                c = step - 3
                rnn_s3(c, n, xcL[c % 3], xcL_b[c % 3])
            if 0 <= step - 2 < 8:
                c = step - 2
                rnn_s2(seq, ti, c, n, *xcbs[c % 2])
            if step < 8:
                c = step
                rnn_s1(c, n, xr_all[:, c, :], xr_b[c], xcL[c % 3], xcL_b[c % 3], *xcbs[c % 2])
            yield

    c_xa = fw.dma_ctr("c_xa")

    def light_tile(seq, ti, tok0, n, key0, nxt_tok0):
        sname = seq["name"]
        pre1 = seq.pop("pre_norm1", None)
        if pre1 is None:
            fw.dma(SP, c_x, xT[:, :, 0:n], seq["x"].rearrange("(c p) s -> p c s", p=128)[:, :, tok0:tok0 + n],
                   writes=xT_b)
        fw.dma(SP, c_cs, cs[:, :, 0:n], seq["rope"][:, :, tok0:tok0 + n], writes=[cs_b])
        rmsnorm(xT, xT_b, 8, PG1, D, uT, uT_b, n, gT, gT_b, pre=pre1)
        ss = pget_reserve()
        ffn(w13a, w2a, n, side=seq.pop("pending", None), sumsq=ss)
        rmsnorm(xT, xT_b, 8, PGM, D, uT, uT_b, n, gT, gT_b, pre=ss)
        fw.dma(ACT, c_xa, xT[:, :, 0:n],
               seq["x"].rearrange("(c p) s -> p c s", p=128)[:, :, nxt_tok0:nxt_tok0 + n], writes=xT_b)
        wsl, wb = ws_next(win)
        wv = wsl.rearrange("p (k j) -> p k j", k=8)
        wv_b[0] = wb
        for e in range(2):
            ps, pb = proj8(wv, 384 + 32 * e, 32, n)
            fw.op(ACT, lambda: copy_act(krraw[:, e, 0:n], ps[0:32, 0:n]), reads=[pb], writes=[krraw_b[e]])
        wsl, wb = ws_next(win)
        wv = wsl.rearrange("p (k j) -> p k j", k=8)
        wv_b[0] = wb
        for c in range(2):
            ps, pb = proj8(wv, c * 128, 128, n)
            fw.op(ACT, lambda: copy_act(kvraw[:, c, 0:n], ps[:, 0:n]), reads=[pb], writes=[kvraw_b[c]])
        fence(alias_b)

        def rx_blocks(c2s):
            for c2 in c2s:
                wsl, wb = ws_next(win)
                wv = wsl.rearrange("p (k j) -> p k j", k=8)
                wv_b[0] = wb
                for e in range(2):
                    c = 2 * c2 + e
                    psx, psx_b = proj8(wv, e * 128, 128, n)
                    fw.op(ACT, lambda: copy_act(xr_all[:, c, 3:3 + n], psx[:, 0:n]), reads=[psx_b],
                          writes=[xr_b[c]])

        rx_blocks([0, 1])
        fw.op(DVE, lambda: nc.vector.tensor_tensor(out=rt[0][:, 0:n], in0=krraw[:, 0, 0:n], in1=cs[:, 0, 0:n],
                                                   op=ALU.mult), reads=[krraw_b[0], cs_b], writes=[rt_b[0]])
        fw.op(DVE, lambda: nc.vector.tensor_tensor(out=rt[1][:, 0:n], in0=krraw[:, 1, 0:n], in1=cs[:, 1, 0:n],
                                                   op=ALU.mult), reads=[krraw_b[1], cs_b], writes=[rt_b[1]])
        fw.op(DVE, lambda: nc.vector.tensor_tensor(out=krb[:, 0:n], in0=rt[0][:, 0:n], in1=rt[1][:, 0:n],
                                                   op=ALU.add), reads=[rt_b[0], rt_b[1]], writes=[krb_b])
        rmsnorm(kvraw, kvraw_b, 2, PGKV, 256, ckvb, ckvb_b, n, gT, gT_b)
        rx_blocks([2, 3])
        produce_kv(n, seq["KT"], seq["V"], key0, sname, ti, vflag=(8 + ti) if ti < 4 else None)
        seq["pre_norm1"] = norm_sums(xT, xT_b, 8, n, gT, gT_b, reserve=True)
        seq["pending"] = rnn_chain_gen(seq, ti, n)

    def tile(seq, ti, tok0, n, key0, ctx, light=False, out0=0):
        sname = seq["name"]
        pre1 = seq.pop("pre_norm1", None)
        if pre1 is None:
            fw.dma(SP, c_x, xT[:, :, 0:n], seq["x"].rearrange("(c p) s -> p c s", p=128)[:, :, tok0:tok0 + n],
                   writes=xT_b)
        fw.dma(SP, c_cs, cs[:, :, 0:n], seq["rope"][:, :, tok0:tok0 + n], writes=[cs_b])
        rmsnorm(xT, xT_b, 8, PG1, D, uT, uT_b, n, gT, gT_b, pre=pre1)
        ss = pget_reserve()
        ffn(w13a, w2a, n, side=seq.pop("pending", None), sumsq=ss)
        rmsnorm(xT, xT_b, 8, PGM, D, uT, uT_b, n, gT, gT_b, pre=ss)
        wsl, wb = ws_next(win)
        wv = wsl.rearrange("p (k j) -> p k j", k=8)
        wv_b[0] = wb
        for c in range(0 if light else 3):
            ps, pb = proj8(wv, c * 128, 128, n)
            fw.op(ACT, lambda: copy_act(zq[:, c, 0:n], ps[:, 0:n]), reads=[pb], writes=[zq_b[c]])
        for e in range(2):
            ps, pb = proj8(wv, 384 + 32 * e, 32, n)
            fw.op(ACT, lambda: copy_act(krraw[:, e, 0:n], ps[0:32, 0:n]), reads=[pb], writes=[krraw_b[e]])
        wsl, wb = ws_next(win)
        wv = wsl.rearrange("p (k j) -> p k j", k=8)
        wv_b[0] = wb
        for c in range(2):
            ps, pb = proj8(wv, c * 128, 128, n)
            fw.op(ACT, lambda: copy_act(kvraw[:, c, 0:n], ps[:, 0:n]), reads=[pb], writes=[kvraw_b[c]])
        fw.op(DVE, lambda: nc.vector.tensor_tensor(out=rt[0][:, 0:n], in0=krraw[:, 0, 0:n], in1=cs[:, 0, 0:n],
                                                   op=ALU.mult), reads=[krraw_b[0], cs_b], writes=[rt_b[0]])
        fw.op(DVE, lambda: nc.vector.tensor_tensor(out=rt[1][:, 0:n], in0=krraw[:, 1, 0:n], in1=cs[:, 1, 0:n],
                                                   op=ALU.mult), reads=[krraw_b[1], cs_b], writes=[rt_b[1]])
        fw.op(DVE, lambda: nc.vector.tensor_tensor(out=krf[:, 0:n], in0=rt[0][:, 0:n], in1=rt[1][:, 0:n],
                                                   op=ALU.add), reads=[rt_b[0], rt_b[1]], writes=[krf_b])
        fw.op(POOL, lambda: nc.gpsimd.tensor_copy(out=krb[:, 0:n], in_=krf[:, 0:n]), reads=[krf_b], writes=[krb_b])
        if not light:
            fw.dma(POOL, c_kr, seq["kr_out"][:, out0:out0 + n], krf[:, 0:n], reads=[krf_b])
        if not light:
            fence(alias_b)
            rmsnorm(zq, zq_b, 3, PGQ, 384, qn, qn_b, n, gT, gT_b)
        for h in range(0 if light else 8):
            ps, pb = pget()
            for kc in range(3):
                fw.op(PE, lambda kc=kc: nc.tensor.matmul(ps[:, 0:n], lhsT=wuq[:, kc, h, :], rhs=qn[:, kc, 0:n],
                                                         start=(kc == 0), stop=(kc == 2)),
                      reads=[wres_b, qn_b[kc]], writes=[pb])
            fw.op(ACT, lambda: copy_act(QT[0:96, h, 0:n], ps[0:96, 0:n], QSCALE), reads=[pb], writes=[QT_b[h]])
            fw.op(ACT, lambda: copy_act(krraw[:, 0, 0:n], ps[0:32, 0:n], QSCALE), reads=[pb], writes=[krraw_b[0]])
            fw.op(ACT, lambda: copy_act(krraw[:, 1, 0:n], ps[96:128, 0:n], QSCALE), reads=[pb], writes=[krraw_b[1]])
            fw.op(DVE, lambda: nc.vector.tensor_tensor(out=rt[0][:, 0:n], in0=krraw[:, 0, 0:n], in1=cs[:, 0, 0:n],
                                                       op=ALU.mult), reads=[krraw_b[0], cs_b], writes=[rt_b[0]])
            fw.op(DVE, lambda: nc.vector.tensor_tensor(out=rt[1][:, 0:n], in0=krraw[:, 1, 0:n], in1=cs[:, 1, 0:n],
                                                       op=ALU.mult), reads=[krraw_b[1], cs_b], writes=[rt_b[1]])
            fw.op(DVE, lambda: nc.vector.tensor_tensor(out=QT[0:32, h, 0:n], in0=rt[0][:, 0:n], in1=rt[1][:, 0:n],
                                                       op=ALU.add), reads=[rt_b[0], rt_b[1]], writes=[QT_b[h]])
        rmsnorm(kvraw, kvraw_b, 2, PGKV, 256, ckv, ckv_b, n, gT, gT_b)
        for c in range(2):
            fw.op(POOL, lambda c=c: nc.gpsimd.tensor_copy(out=ckvb[:, c, 0:n], in_=ckv[:, c, 0:n]),
                  reads=[ckv_b[c]], writes=[ckvb_b[c]])
        if not light:
            fw.dma(POOL, c_kv, seq["kv_out"].rearrange("(c p) s -> p c s", p=128)[:, :, out0:out0 + n],
                   ckv[:, :, 0:n], reads=ckv_b)
        produce_kv(n, seq["KT"], seq["V"], key0, sname, ctx[0][3],
                   vflag=(8 + ti) if (seq["prompt"] and ti < 4) else None)
        fence(alias_b)
        wvh = [None]
        for step in range(10):
            if 0 <= step - 2 < 8:
                c = step - 2
                rnn_s3(c, n, xcL[c % 2], xcL_b[c % 2])
                g_, g_b = gg[c % 2], gg_b[c % 2]
                fw.op(DVE, lambda: nc.vector.tensor_tensor(out=hg[:, c, 0:n], in0=hbuf[:, 0:n], in1=g_[:, 0:n],
                                                           op=ALU.mult), reads=[hbuf_b, g_b], writes=[hg_b[c]])
            if 0 <= step - 1 < 8:
                rnn_s2(seq, ti, step - 1, n)
            if step < 8:
                c = step
                e = c % 2
                if e == 0:
                    wsl, wb = ws_next(win)
                    wvh[0] = wsl.rearrange("p (k j) -> p k j", k=8)
                    wv_b[0] = wb
                wv = wvh[0]
                psx, psx_b = proj8(wv, e * 128, 128, n)
                psg, psg_b = proj8(wv, 256 + e * 128, 128, n)
                X, X_b = Xb[c % 2], Xb_b[c % 2]
                fw.op(ACT, lambda: copy_act(X[:, 3:3 + n], psx[:, 0:n]), reads=[psx_b], writes=[X_b])
                g_, g_b = gg[c % 2], gg_b[c % 2]
                fw.op(ACT, lambda: nc.scalar.activation(out=g_[:, 0:n], in_=psg[:, 0:n], func=AF.Gelu_apprx_tanh),
                      reads=[psg_b], writes=[g_b])
                rnn_s1(c, n, X, X_b, xcL[c % 2], xcL_b[c % 2])
        fence(alias_b)
        for c2 in range(4):
            wsl, wb = ws_next(win)
            wv = wsl.rearrange("p (k j) -> p k j", k=8)
            wv_b[0] = wb
            for e in range(2):
                c = 2 * c2 + e
                for gi in range(2):
                    ps, pb = proj8(wv, gi * 256 + e * 128, 128, n)
                    fw.op(ACT, lambda: nc.scalar.activation(out=gT[:, 8 * gi + c, 0:n], in_=ps[:, 0:n],
                                                            func=AF.Sigmoid), reads=[pb],
                          writes=[gT_b[8 * gi + c]])
        attention(n, seq["KT"], seq["V"], ctx, sname)
        woa_s, woa_b = ws_next(woa)
        woav = woa_s.rearrange("p (k j) -> p k j", k=4)
        wor_s = [ws_next(wor, 1), ws_next(wor, 2)]
        for d in range(8):
            ps, pb = pget()
            for kc in range(4):
                fw.op(PE, lambda kc=kc: nc.tensor.matmul(ps[:, 0:n], lhsT=woav[:, kc, d * 128:(d + 1) * 128],
                                                         rhs=oT[:, kc, 0:n], start=(kc == 0), stop=(kc == 3)),
                      reads=[woa_b, oT_b[kc]], writes=[pb])
            fw.op(DVE, lambda: nc.vector.tensor_tensor(out=ma[:, 0:n], in0=ps[:, 0:n], in1=gT[:, d, 0:n],
                                                       op=ALU.mult), reads=[pb, gT_b[d]], writes=[ma_b])
            wsl, wb = wor_s[d // 4]
            wv = wsl.rearrange("p (k j) -> p k j", k=8)
            ps2, pb2 = pget()
            for kc in range(8):
                fw.op(PE, lambda kc=kc: nc.tensor.matmul(ps2[:, 0:n], lhsT=wv[:, kc, (d % 4) * 128:(d % 4 + 1) * 128],
                                                         rhs=hg[:, kc, 0:n], start=(kc == 0), stop=(kc == 7)),
                      reads=[wb, hg_b[kc]], writes=[pb2])
            s_, s_b = sil[d % 2], sil_b[d % 2]
            fw.op(DVE, lambda: nc.vector.tensor_tensor(out=s_[:, 0:n], in0=ps2[:, 0:n], in1=gT[:, 8 + d, 0:n],
                                                       op=ALU.mult), reads=[pb2, gT_b[8 + d]], writes=[s_b])
            fw.op(POOL, lambda: nc.gpsimd.tensor_tensor(out=uT[:, d, 0:n], in0=s_[:, 0:n], in1=ma[:, 0:n],
                                                        op=ALU.add), reads=[s_b, ma_b], writes=[uT_b[d]])
        wo_s = [ws_next(wout), ws_next(wout, 1)]
        for d in range(8):
            wsl, wb = wo_s[d // 4]
            wv = wsl.rearrange("p (k j) -> p k j", k=8)
            ps, pb = pget()
            for kc in range(8):
                fw.op(PE, lambda kc=kc: nc.tensor.matmul(ps[:, 0:n], lhsT=wv[:, kc, (d % 4) * 128:(d % 4 + 1) * 128],
                                                         rhs=uT[:, kc, 0:n], start=(kc == 0), stop=(kc == 7)),
                      reads=[wb, uT_b[kc]], writes=[pb])
            fw.op(DVE, lambda: nc.vector.tensor_tensor(out=xT[:, d, 0:n], in0=ps[:, 0:n], in1=xT[:, d, 0:n],
                                                       op=ALU.add), reads=[pb, xT_b[d]], writes=[xT_b[d]])
        rmsnorm(xT, xT_b, 8, PG2, D, uT, uT_b, n, gT, gT_b)
        ss = pget_reserve()
        ffn(w13b, w2b, n, sumsq=ss)
        rmsnorm(xT, xT_b, 8, PGF, D, xT, xT_b, n, gT, gT_b, pre=ss)
        fw.dma(POOL, c_out, seq["y_out"].rearrange("(c p) s -> p c s", p=128)[:, :, out0:out0 + n], xT[:, :, 0:n],
               reads=xT_b)

    seq_p = dict(name="p", prompt=True, x=xp, rope=rope_p, kr_out=kr_p, kv_out=kvl_p, y_out=y_p, KT=(KN_p, KR_p), V=V_p)
    seq_s = dict(name="s", prompt=False, x=xs, rope=rope_s, kr_out=kr_s, kv_out=kvl_s, y_out=y_s, KT=(KN_s, KR_s), V=V_s)

    halo_flat = halo.rearrange("p c j -> p (c j)")
    for c in range(8):
        fw.op(POOL, lambda c=c: nc.gpsimd.memset(halo[:, c, :], 0.0), writes=[halo_b[c]])
        fw.op(POOL, lambda c=c: nc.gpsimd.memset(hst[:, c:c + 1], 0.0), writes=[hst_b[c]])
    for ti in range(NT):
        ctx = [(ti * TT, TT, True, ti)] + [(j * TT, TT, False, j) for j in range(ti)]
        if ti % 4 == 3:
            tile(seq_p, ti, ti * TT, TT, ti * TT, ctx, light=False, out0=(ti // 4) * TT)
        else:
            light_tile(seq_p, ti, ti * TT, TT, ti * TT, (ti + 1) * TT)
        if ti == 0:
            late_casts()
    assert "pending" not in seq_p and "pre_norm1" not in seq_p
    fw.dma(POOL, c_st, conv_p, halo_flat, reads=halo_b)
    fw.dma(POOL, c_st, h_p, hst, reads=hst_b)
    for j in range(PAST // TT):
        fw.dma(POOL, c_misc, ckvb[:, :, 0:TT],
               ckv_c.rearrange("(c p) s -> p c s", p=128)[:, :, j * TT:(j + 1) * TT], writes=ckvb_b)
        fw.dma(POOL, c_misc, krb[:, 0:TT], ckr_c[:, j * TT:(j + 1) * TT], writes=[krb_b])
        produce_kv(TT, (KN_s, KR_s), V_s, j * TT, "s", j)
    fw.dma(SP, c_msp, halo_flat, sconv, reads=[], writes=halo_b)
    fw.dma(SP, c_msp, hst, srg, reads=[], writes=hst_b)
    ctx = [(PAST, DEC, False, 2), (0, TT, False, 0), (TT, TT, False, 1)]
    tile(seq_s, 0, 0, DEC, PAST, ctx)
    fw.dma(POOL, c_st, conv_s, halo_flat, reads=halo_b)
    fw.dma(POOL, c_st, h_s, hst, reads=hst_b)
    fw.finish(SP)
    build_program.stats = dict(n_inst=fw.n_inst, n_wait=fw.n_wait, sbuf_left=nc.sbuf_bytes_remaining)
    return nc


_CACHE = {}


def kernel(x_prompt, x_sample, cache_kv_latent, cache_k_rope, state_conv, state_rglru,
           norm_ffn1, w1_ffn1, w3_ffn1, w2_ffn1, norm_mix, w_in,
           norm_q, w_uq, norm_kv, w_ukv, w_o_attn,
           conv_w, conv_b, w_rgate, b_rgate, w_igate, b_igate, lru_lambda, w_o_rnn,
           w_out, norm_ffn2, w1_ffn2, w3_ffn2, w2_ffn2, norm_final):
    f = lambda a: np.asarray(a, dtype=np.float32)
    x_prompt = f(x_prompt)
    x_sample = f(x_sample)
    Bp, S_P, _ = x_prompt.shape
    Bs = x_sample.shape[0]
    n_cores = 8
    assert Bs == n_cores and S_P % TT == 0
    wd = dict(norm_ffn1=f(norm_ffn1), norm_mix=f(norm_mix), norm_q=f(norm_q), norm_kv=f(norm_kv),
              norm_ffn2=f(norm_ffn2), norm_final=f(norm_final), conv_w=f(conv_w), conv_b=f(conv_b),
              b_rgate=f(b_rgate), b_igate=f(b_igate), lru_lambda=f(lru_lambda))
    shared = {
        "params": host_params(wd),
        "w13a": host_w13(f(w1_ffn1)[0], f(w3_ffn1)[0]),
        "w2a": host_w2(f(w2_ffn1)[0]),
        "w13b": host_w13(f(w1_ffn2)[0], f(w3_ffn2)[0]),
        "w2b": host_w2(f(w2_ffn2)[0]),
        "win": host_win(f(w_in)[0]),
        "wres": host_wres(f(w_uq)[0], f(w_ukv)[0], f(w_rgate)[0], f(w_igate)[0]),
        "woa": host_kc(f(w_o_attn)[0], 1024),
        "wor": host_kc(f(w_o_rnn)[0], 512),
        "wout": host_kc(f(w_out)[0], 512),
        "rope_s": rope_tables(PAST + np.arange(DEC)),
    }
    NT = S_P // TT
    assert NT % 4 == 0 and Bp * 4 == n_cores
    NF_ = NT // 4
    xpT = [np.ascontiguousarray(x_prompt[b].T) for b in range(Bp)]
    ckv = f(cache_kv_latent)[0]
    ckr = f(cache_k_rope)[0]
    sc = f(state_conv)[0]
    sh = f(state_rglru)[0]
    in_maps = []
    for c in range(n_cores):
        b, j = c // 4, c % 4
        m = dict(shared)
        xc_ = np.zeros((D, S_P), np.float32)
        pos = np.zeros((S_P,), np.int64)
        flags = np.zeros((128, 12), np.float32)
        for s_ in range(NT):
            g = s_ - (3 - j)
            if g >= 0:
                xc_[:, s_ * TT:(s_ + 1) * TT] = xpT[b][:, g * TT:(g + 1) * TT]
                pos[s_ * TT:(s_ + 1) * TT] = g * TT + np.arange(TT)
            if s_ < 4:
                flags[:, s_] = 0.0 if g == 0 else 1.0
                flags[:, 4 + s_] = 1.0 if g == 0 else 0.0
                flags[:, 8 + s_] = 1.0 if g >= 0 else 0.0
        m["xp"] = xc_
        m["rope_p"] = rope_tables(pos)
        m["flags"] = flags
        m["xs"] = np.ascontiguousarray(x_sample[c].T)
        m["ckv_c"] = np.ascontiguousarray(ckv[c].T)
        m["ckr_c"] = np.ascontiguousarray(ckr[c].T)
        m["sconv"] = np.ascontiguousarray(sc[c].reshape(3, 8, 128).transpose(2, 1, 0).reshape(128, 24))
        m["srg"] = np.ascontiguousarray(sh[c].reshape(8, 128).T)
        in_maps.append(m)
    if S_P not in _CACHE:
        _CACHE[S_P] = build_program(S_P)
    nc = _CACHE[S_P]
    res = run_bass_kernel_spmd(nc, in_maps, core_ids=list(range(n_cores)))
    R = res.results

    def unconv(a):
        return np.ascontiguousarray(a.reshape(128, 8, 3).transpose(2, 1, 0).reshape(3, 1024))

    def unh(a):
        return np.ascontiguousarray(a.T.reshape(1024))

    def gather(name, width):
        out = np.zeros((Bp, S_P, width), np.float32)
        for c in range(n_cores):
            b, j = c // 4, c % 4
            a = R[c][name]
            for k in range(NF_):
                g = 4 * k + j
                out[b, g * TT:(g + 1) * TT, :] = a[:, k * TT:(k + 1) * TT].T
        return out

    y_prompt = gather("y_p", D)
    y_sample = np.stack([np.ascontiguousarray(R[c]["y_s"].T) for c in range(n_cores)], 0)
    kvl_prompt = gather("kvl_p", 256)[None]
    kr_prompt = gather("kr_p", 32)[None]
    conv_prompt = np.stack([unconv(R[4 * b + 3]["conv_p"]) for b in range(Bp)], 0)[None]
    h_prompt = np.stack([unh(R[4 * b + 3]["h_p"]) for b in range(Bp)], 0)[None]
    kvl_sample = np.stack([np.ascontiguousarray(R[c]["kvl_s"].T) for c in range(n_cores)], 0)[None]
    kr_sample = np.stack([np.ascontiguousarray(R[c]["kr_s"].T) for c in range(n_cores)], 0)[None]
    conv_sample = np.stack([unconv(R[c]["conv_s"]) for c in range(n_cores)], 0)[None]
    h_sample = np.stack([unh(R[c]["h_s"]) for c in range(n_cores)], 0)[None]
    outs = (y_prompt, y_sample, kvl_prompt, kr_prompt, conv_prompt, h_prompt,
            kvl_sample, kr_sample, conv_sample, h_sample)
    return tuple(np.asarray(o, dtype=np.float32) for o in outs)
```

```python
import os
import bisect
import numpy as np
import concourse.bass as bass
import concourse.mybir as mybir
from concourse.bass_utils import run_bass_kernel_spmd

F32 = mybir.dt.float32
BF16 = mybir.dt.bfloat16
AF = mybir.ActivationFunctionType
ALU = mybir.AluOpType

D = 1024
DFF = 2816
NF = 22
EPS = 1e-6
TT = 512
PAST = 1024
DEC = 64
QSCALE = 96 ** -0.5
SAFE_DIST = 3


class StopBuild(Exception):
    pass


class Ctr:
    def __init__(self, fw, name):
        self.sem = fw.nc.alloc_semaphore(name)
        self.count = 0
        self.hist_t = []
        self.hist_k = []

    def snap(self, t, known):
        if self.hist_k and self.hist_k[-1] == known:
            return
        self.hist_t.append(t)
        self.hist_k.append(dict(known))

    def known_at(self, t):
        i = bisect.bisect_right(self.hist_t, t) - 1
        return self.hist_k[i] if i >= 0 else None


class Eng:
    def __init__(self, fw, name, eng):
        self.name = name
        self.eng = eng
        self.ctr = Ctr(fw, "c_" + name)
        self.known = {}
        self.n_issued = 0
        self.ticket_pos = {}


class Buf:
    __slots__ = ("name", "last_w", "reads")

    def __init__(self, name):
        self.name = name
        self.last_w = None
        self.reads = []


def _compress(reads):
    d = {}
    for c, t in reads:
        if d.get(c, 0) < t:
            d[c] = t
    return list(d.items())


class FW:
    def __init__(self, nc):
        self.nc = nc
        self.pe = Eng(self, "pe", nc.tensor)
        self.act = Eng(self, "act", nc.scalar)
        self.dve = Eng(self, "dve", nc.vector)
        self.pool = Eng(self, "pool", nc.gpsimd)
        self.sp = Eng(self, "sp", nc.sync)
        self.engs = [self.pe, self.act, self.dve, self.pool, self.sp]
        self.dma_ctrs = []
        self.dma_set = set()
        self.n_wait = 0
        self.n_inst = 0

    def dma_ctr(self, name):
        c = Ctr(self, name)
        self.dma_ctrs.append(c)
        self.dma_set.add(c)
        return c

    def _deps(self, reads, writes):
        deps = {}
        for b in reads:
            if b.last_w is not None:
                c, t = b.last_w
                if deps.get(c, 0) < t:
                    deps[c] = t
        for b in writes:
            if b.last_w is not None:
                c, t = b.last_w
                if deps.get(c, 0) < t:
                    deps[c] = t
            for c, t in b.reads:
                if deps.get(c, 0) < t:
                    deps[c] = t
        return deps

    def _wait(self, E, deps):
        for c, t in deps.items():
            if c in self.dma_set:
                t = c.count
            if c is E.ctr:
                if E is self.pe:
                    continue
            if E.known.get(c, 0) >= t:
                continue
            E.eng.wait_ge(c.sem, t)
            E.known[c] = t
            self.n_wait += 1
            k2 = c.known_at(t)
            if k2:
                for c2, t2 in k2.items():
                    if E.known.get(c2, 0) < t2:
                        E.known[c2] = t2

    def _record(self, ctr, t, reads, writes):
        for b in reads:
            b.reads.append((ctr, t))
            if len(b.reads) > 16:
                b.reads = _compress(b.reads)
        for b in writes:
            b.last_w = (ctr, t)
            b.reads = []

    def op(self, E, fn, reads=(), writes=()):
        self._wait(E, self._deps(reads, writes))
        inst = fn()
        E.ctr.count += 1
        t = E.ctr.count
        E.ctr.snap(t, E.known)
        inst.then_inc(E.ctr.sem, 1)
        E.ticket_pos[t] = E.n_issued
        E.n_issued += 1
        self.n_inst += 1
        if len(E.ticket_pos) > 64:
            for k in sorted(E.ticket_pos)[:32]:
                del E.ticket_pos[k]
        self._record(E.ctr, t, reads, writes)
        return inst

    def dma(self, Q, ctr, out, in_, reads=(), writes=(), **kw):
        self._wait(Q, self._deps(reads, writes))
        inst = Q.eng.dma_start(out=out, in_=in_, **kw)
        ctr.count += 16
        ctr.snap(ctr.count, Q.known)
        inst.then_inc(ctr.sem, 16)
        Q.n_issued += 1
        self.n_inst += 1
        self._record(ctr, ctr.count, reads, writes)
        return inst

    def finish(self, E):
        for F in self.engs:
            if F is not E and F.ctr.count > 0:
                E.eng.wait_ge(F.ctr.sem, F.ctr.count)
        for c in self.dma_ctrs:
            if c.count > 0:
                E.eng.wait_ge(c.sem, c.count)


NPAR = 101
PG1, PGM, PGQ, PGKV, PG2, PGF, PCW, PCB, PBR, PBI, PLAM = 0, 8, 16, 19, 21, 29, 37, 69, 77, 85, 93


def _pc(v, nchunk):
    return np.ascontiguousarray(np.asarray(v, np.float32).reshape(nchunk, 128).T)


def host_params(w):
    cols = [_pc(w["norm_ffn1"][0], 8), _pc(w["norm_mix"][0], 8), _pc(w["norm_q"][0], 3),
            _pc(w["norm_kv"][0], 2), _pc(w["norm_ffn2"][0], 8), _pc(w["norm_final"], 8)]
    cw = np.asarray(w["conv_w"][0], np.float32).reshape(4, 8, 128).transpose(2, 1, 0).reshape(128, 32)
    cols += [cw, _pc(w["conv_b"][0], 8), _pc(w["b_rgate"][0], 8), _pc(w["b_igate"][0], 8),
             _pc(w["lru_lambda"][0], 8)]
    p = np.concatenate(cols, axis=1)
    assert p.shape == (128, NPAR)
    return np.ascontiguousarray(p, dtype=np.float32)


def host_w13(w1, w3):
    a = np.asarray(w1, np.float32).reshape(8, 128, 11, 256)
    b = np.asarray(w3, np.float32).reshape(8, 128, 11, 256)
    s = np.stack([a, b], 0)
    return np.ascontiguousarray(s.transpose(3, 2, 0, 1, 4)).reshape(11 * 128, 4096)


def host_w2(w2):
    a = np.asarray(w2, np.float32).reshape(22, 128, 8, 128)
    return np.ascontiguousarray(a.transpose(2, 1, 0, 3)).reshape(8 * 128, 2816)


def _colblk(cols):
    n = cols.shape[1]
    out = np.zeros((128, 8, 512), np.float32)
    out[:, :, :n] = cols.reshape(8, 128, n).transpose(1, 0, 2)
    return out.reshape(128, 4096)


def host_win(w_in):
    w = np.asarray(w_in, np.float32)
    q = w[:, 0:384]
    kv = w[:, 384:640]
    kr = w[:, 640:672]
    krs = np.concatenate([kr[:, 16:32], kr[:, 0:16]], axis=1)
    rx = w[:, 672:1696]
    rg = w[:, 1696:2720]
    ga = w[:, 2720:3744]
    gb = w[:, 3744:4768]
    blks = [_colblk(np.concatenate([q, kr, krs], 1)), _colblk(kv)]
    for c2 in range(4):
        s = slice(c2 * 256, c2 * 256 + 256)
        blks.append(_colblk(np.concatenate([rx[:, s], rg[:, s]], 1)))
    for c2 in range(4):
        s = slice(c2 * 256, c2 * 256 + 256)
        blks.append(_colblk(np.concatenate([ga[:, s], gb[:, s]], 1)))
    return np.ascontiguousarray(np.stack(blks, 0)).reshape(10 * 128, 4096)


NRES = 3072 + 2048 + 2048


def host_wres(w_uq, w_ukv, w_rg, w_ig):
    uq = np.asarray(w_uq, np.float32).reshape(3, 128, 8, 96)
    nope = uq[..., 0:64]
    rope = uq[..., 64:96]
    sw = np.concatenate([rope[..., 16:32], rope[..., 0:16]], -1)
    a = np.concatenate([rope, nope, sw], -1).transpose(1, 0, 2, 3).reshape(128, 3072)
    ukv = np.asarray(w_ukv, np.float32).reshape(2, 128, 8, 128)
    kpart = ukv[..., 0:64].reshape(2, 128, 512)
    vpart = ukv[..., 64:128].reshape(2, 128, 512)
    b = np.stack([kpart, vpart], 0).transpose(2, 0, 1, 3).reshape(128, 2048)
    g = np.stack([np.asarray(w_rg, np.float32), np.asarray(w_ig, np.float32)], 0)
    c = g.transpose(2, 1, 0, 3).reshape(128, 2048)
    return np.ascontiguousarray(np.concatenate([a, b, c], 1))


def host_kc(wm, ncols_blk):
    wm = np.asarray(wm, np.float32)
    K, N = wm.shape
    a = wm.reshape(K // 128, 128, N // ncols_blk, ncols_blk)
    return np.ascontiguousarray(a.transpose(2, 1, 0, 3)).reshape((N // ncols_blk) * 128, (K // 128) * ncols_blk)


def rope_tables(pos):
    inv = (1.0 / (10000.0 ** (np.arange(0, 32, 2, dtype=np.float32) / np.float32(32)))).astype(np.float32)
    ang = pos.astype(np.float32)[:, None] * inv[None, :]
    c = np.cos(ang).astype(np.float32).T
    s = np.sin(ang).astype(np.float32).T
    cos2 = np.concatenate([c, c], 0)
    sins = np.concatenate([-s, s], 0)
    return np.ascontiguousarray(np.stack([cos2, sins], 1))


def build_program(S_P):
    NT = S_P // TT
    SK_S = PAST + DEC
    nc = bass.Bass("TRN2", target_bir_lowering=False)
    fw = FW(nc)
    PE, ACT, DVE, POOL, SP = fw.pe, fw.act, fw.dve, fw.pool, fw.sp

    def din(name, shape, dt=F32):
        return nc.dram_tensor(name, list(shape), dt, kind="ExternalInput").ap()

    def dout(name, shape):
        return nc.dram_tensor(name, list(shape), F32, kind="ExternalOutput").ap()

    def dscr(name, shape, dt=BF16):
        return nc.dram_tensor(name, list(shape), dt, kind="Internal").ap()

    xp = din("xp", [D, S_P])
    xs = din("xs", [D, DEC])
    ckv_c = din("ckv_c", [256, PAST])
    ckr_c = din("ckr_c", [32, PAST])
    sconv = din("sconv", [128, 24])
    srg = din("srg", [128, 8])
    rope_p = din("rope_p", [32, 2, S_P])
    rope_s = din("rope_s", [32, 2, DEC])
    params_d = din("params", [128, NPAR])
    w13a_f = din("w13a", [11 * 128, 4096])
    w2a_f = din("w2a", [8 * 128, 2816])
    w13b_f = din("w13b", [11 * 128, 4096])
    w2b_f = din("w2b", [8 * 128, 2816])
    win_f = din("win", [10 * 128, 4096])
    wres_f = din("wres", [128, NRES])
    woa_f = din("woa", [128, 4096])
    wor_f = din("wor", [2 * 128, 4096])
    wout_f = din("wout", [2 * 128, 4096])

    NF_ = NT // 4
    S_O = NF_ * TT
    flags_d = din("flags", [128, 12])
    y_p = dout("y_p", [D, S_O])
    kvl_p = dout("kvl_p", [256, S_O])
    kr_p = dout("kr_p", [32, S_O])
    conv_p = dout("conv_p", [128, 24])
    h_p = dout("h_p", [128, 8])
    y_s = dout("y_s", [D, DEC])
    kvl_s = dout("kvl_s", [256, DEC])
    kr_s = dout("kr_s", [32, DEC])
    conv_s = dout("conv_s", [128, 24])
    h_s = dout("h_s", [128, 8])

    w13a = dscr("w13a_b", [11 * 128, 4096])
    w2a = dscr("w2a_b", [8 * 128, 2816])
    w13b = dscr("w13b_b", [11 * 128, 4096])
    w2b = dscr("w2b_b", [8 * 128, 2816])
    win = dscr("win_b", [10 * 128, 4096])
    woa = dscr("woa_b", [128, 4096])
    wor = dscr("wor_b", [2 * 128, 4096])
    wout = dscr("wout_b", [2 * 128, 4096])
    KN_p = dscr("KN_p", [4, 128, S_P])
    KR_p = dscr("KR_p", [32, S_P])
    V_p = dscr("V_p", [S_P, 1024])
    KN_s = dscr("KN_s", [4, 128, SK_S])
    KR_s = dscr("KR_s", [32, SK_S])
    V_s = dscr("V_s", [SK_S, 1024])

    def sb(name, shape, dt=F32):
        return nc.alloc_sbuf_tensor("sb_" + name, list(shape), dt).ap()

    xT = sb("xT", [128, 8, TT])
    xT_b = [Buf("xT%d" % c) for c in range(8)]
    uT = sb("uT", [128, 8, TT], BF16)
    uT_b = [Buf("uT%d" % c) for c in range(8)]
    gT = sb("gT", [128, NF, TT], BF16)
    gT_b = [Buf("gT%d" % c) for c in range(NF)]
    rstd = sb("rstd", [128, TT])
    rstd_b = Buf("rstd")
    sil = [sb("sil%d" % i, [128, TT]) for i in range(2)]
    sil_b = [Buf("sil%d" % i) for i in range(2)]
    zq = sb("zq", [128, 3, TT])
    zq_b = [Buf("zq%d" % c) for c in range(3)]
    qn = sb("qn", [128, 3, TT], BF16)
    qn_b = [Buf("qn%d" % c) for c in range(3)]
    kvraw = sb("kvraw", [128, 2, TT])
    kvraw_b = [Buf("kvraw%d" % c) for c in range(2)]
    ckv = sb("ckv", [128, 2, TT])
    ckv_b = [Buf("ckv%d" % c) for c in range(2)]
    ckvb = sb("ckvb", [128, 2, TT], BF16)
    ckvb_b = [Buf("ckvb%d" % c) for c in range(2)]
    krraw = sb("krraw", [32, 2, TT])
    krraw_b = [Buf("krraw0"), Buf("krraw1")]
    krf = sb("krf", [32, TT])
    krf_b = Buf("krf")
    krb = sb("krb", [32, TT], BF16)
    krb_b = Buf("krb")
    cs = sb("cs", [32, 2, TT])
    cs_b = Buf("cs")
    rt = [sb("rt%d" % i, [32, TT]) for i in range(2)]
    rt_b = [Buf("rt0"), Buf("rt1")]
    Xb = [sb("Xb%d" % i, [128, TT + 3]) for i in range(2)]
    Xb_b = [Buf("Xb0"), Buf("Xb1")]
    xc = sb("xc", [128, TT])
    xc_b = Buf("xc")
    xcb = sb("xcb", [128, TT], BF16)
    xcb_b = Buf("xcb")
    rbuf = sb("rbuf", [128, TT])
    rbuf_b = Buf("rbuf")
    a2buf = sb("a2buf", [128, TT])
    a2buf_b = Buf("a2buf")
    igbuf = sb("igbuf", [128, TT])
    igbuf_b = Buf("igbuf")
    hbuf = sb("hbuf", [128, TT])
    hbuf_b = Buf("hbuf")
    gg = [sb("gg%d" % i, [128, TT], BF16) for i in range(2)]
    gg_b = [Buf("gg0"), Buf("gg1")]
    freg = sb("freg", [128, 12288], BF16)
    hg = freg[:, 0:4096].rearrange("p (c t) -> p c t", c=8)
    hg_b = [Buf("hg%d" % c) for c in range(8)]
    QT = freg[:, 4096:8192].rearrange("p (c t) -> p c t", c=8)
    QT_b = [Buf("QT%d" % c) for c in range(8)]
    oT = freg[:, 8192:10240].rearrange("p (c t) -> p c t", c=4)
    oT_b = [Buf("oT%d" % c) for c in range(4)]
    ma = freg[:, 10240:11264].bitcast(F32)
    ma_b = Buf("ma")
    rec = freg[:, 11264:12288].bitcast(F32)
    rec_b = Buf("rec")
    xr_all = freg[:, 0:8240].bitcast(F32).rearrange("p (c t) -> p c t", c=8)
    xr_b = [Buf("xr%d" % c) for c in range(8)]
    xcL = [freg[:, 8256:9280].bitcast(F32), freg[:, 9280:10304].bitcast(F32)]
    xcL_b = [Buf("xcL0"), Buf("xcL1")]
    alias_b = hg_b + QT_b + oT_b + [ma_b, rec_b] + xr_b + xcL_b
    fdummy = sb("fdummy", [128, 8])
    knT = sb("knT", [128, 4, TT], BF16)
    knT_b = [Buf("knT%d" % c) for c in range(4)]
    vst = sb("vst", [128, 4, 8, 128], BF16)
    vst_b = Buf("vst")
    NKS = 2
    kslot = [sb("kslot%d" % i, [96, 4, TT], BF16) for i in range(NKS)]
    kslot_b = [Buf("kslot%d" % i) for i in range(NKS)]
    vslot = [sb("vslot%d" % i, [128, 4, 4, 128], BF16) for i in range(NKS)]
    vslot_b = [Buf("vslot%d" % i) for i in range(NKS)]
    NPT = 4
    PTs = [sb("PT%d" % i, [128, TT], BF16) for i in range(NPT)]
    PT_b = [Buf("PT%d" % i) for i in range(NPT)]
    NWS = 4
    wslot = [sb("wslot%d" % i, [128, 4096], BF16) for i in range(NWS)]
    wslot_b = [Buf("wslot%d" % i) for i in range(NWS)]
    wslot_c = [fw.dma_ctr("wsl%d" % i) for i in range(NWS)]
    wres = sb("wres", [128, NRES], BF16)
    wres_b = Buf("wres")
    ones = sb("ones", [128, 128], BF16)
    ones_b = Buf("ones")
    par = sb("par", [128, NPAR])
    par_b = Buf("par")
    hb = sb("hb", [128, 16])
    hb_b = Buf("hb")
    flg = sb("flg", [128, 12])
    flg_b = Buf("flg")
    onesv = sb("onesv", [128, 4, 64], BF16)
    onesv_b = Buf("onesv")
    nsp = sb("nsp", [128, 16])
    nsp_b = Buf("nsp")
    halo = sb("halo", [128, 8, 3])
    halo_b = [Buf("halo%d" % c) for c in range(8)]
    hst = sb("hst", [128, 8])
    hst_b = [Buf("hst%d" % c) for c in range(8)]

    wuq = wres[:, 0:3072].rearrange("p (k h j) -> p k h j", k=3, h=8)
    wukv = wres[:, 3072:5120].rearrange("p (e k j) -> p e k j", e=2, k=2)
    wgate = wres[:, 5120:7168].rearrange("p (n e j) -> p n e j", n=8, e=2)

    psum = [nc.alloc_psum_tensor("ps%d" % i, [128, TT], F32).ap() for i in range(8)]
    psum_b = [Buf("ps%d" % i) for i in range(8)]
    prr = [0]

    reserved = set()

    def pget(lo=0, hi=8):
        while True:
            i = lo + prr[0] % (hi - lo)
            prr[0] += 1
            if i not in reserved:
                return psum[i], psum_b[i]

    def pget_reserve():
        ps, pb = pget()
        reserved.add(psum_b.index(pb))
        return ps, pb

    c_x = fw.dma_ctr("c_x")
    c_misc = fw.dma_ctr("c_misc")
    c_msp = fw.dma_ctr("c_msp")
    c_cs = fw.dma_ctr("c_cs")
    c_out = fw.dma_ctr("c_out")
    c_kv = fw.dma_ctr("c_kv")
    c_kr = fw.dma_ctr("c_kr")
    c_kn = fw.dma_ctr("c_kn")
    c_krs = fw.dma_ctr("c_krs")
    c_vst = fw.dma_ctr("c_vst")
    c_ks = [fw.dma_ctr("c_ks%d" % i) for i in range(NKS)]
    c_vs = [fw.dma_ctr("c_vs%d" % i) for i in range(NKS)]
    c_cast = fw.dma_ctr("c_cast")
    c_st = fw.dma_ctr("c_st")

    wdram_b = Buf("wdram")
    KV_b = {}

    def kvbuf(seqname, tile):
        k = (seqname, tile)
        if k not in KV_b:
            KV_b[k] = Buf("kv_%s_%d" % k)
        return KV_b[k]

    wdram2_b = Buf("wdram2")
    c_cast2 = fw.dma_ctr("c_cast2")
    cast_ctx = {"ctr": c_cast, "buf": wdram_b}

    def cast_copy(dst, src, rows, cols):
        step = max(1, (1 << 20) // cols)
        r = 0
        cc, bb = cast_ctx["ctr"], cast_ctx["buf"]
        while r < rows:
            rr = min(step, rows - r)
            if cc.count >= 32:
                POOL.eng.wait_ge(cc.sem, cc.count - 16)
            fw.dma(POOL, cc, dst[r:r + rr, :], src[r:r + rr, :], writes=[bb])
            r += rr

    cast_copy(w13a, w13a_f, 11 * 128, 4096)
    cast_copy(w2a, w2a_f, 8 * 128, 2816)
    cast_copy(win, win_f, 10 * 128, 4096)

    def late_casts():
        cast_ctx["ctr"], cast_ctx["buf"] = c_cast2, wdram2_b
        cast_copy(woa, woa_f, 128, 4096)
        cast_copy(wor, wor_f, 256, 4096)
        cast_copy(wout, wout_f, 256, 4096)
        cast_copy(w13b, w13b_f, 11 * 128, 4096)
        cast_copy(w2b, w2b_f, 8 * 128, 2816)

    fw.dma(POOL, c_misc, wres, wres_f, writes=[wres_b])
    fw.dma(SP, c_msp, par, params_d, writes=[par_b])
    fw.dma(SP, c_msp, flg, flags_d, writes=[flg_b])
    fw.op(DVE, lambda: nc.vector.tensor_scalar(out=hb, in0=par[:, PBR:PBR + 16], scalar1=0.5, scalar2=None,
                                               op0=ALU.mult), reads=[par_b], writes=[hb_b])
    fw.op(DVE, lambda: nc.vector.memset(onesv, 1.0), writes=[onesv_b])
    fw.op(DVE, lambda: nc.vector.memset(ones, 1.0), writes=[ones_b])
    fw.op(DVE, lambda: nc.vector.memset(vst, 1.0), writes=[vst_b])
    fw.op(ACT, lambda: nc.scalar.activation(out=nsp[:, 0:8], in_=par[:, PLAM:PLAM + 8], func=AF.Exp, scale=-1.0),
          reads=[par_b], writes=[nsp_b])
    fw.op(ACT, lambda: nc.scalar.activation(out=nsp[:, 0:8], in_=nsp[:, 0:8], func=AF.Ln, bias=1.0, scale=1.0),
          reads=[nsp_b], writes=[nsp_b])
    fw.op(DVE, lambda: nc.vector.tensor_scalar(out=nsp[:, 8:16], in0=nsp[:, 0:8], scalar1=-16.0, scalar2=None,
                                               op0=ALU.mult), reads=[nsp_b], writes=[nsp_b])
    fw.op(DVE, lambda: nc.vector.tensor_scalar(out=nsp[:, 0:8], in0=nsp[:, 0:8], scalar1=-8.0, scalar2=None,
                                               op0=ALU.mult), reads=[nsp_b], writes=[nsp_b])

    blocks_L = [(w13a, g, 4096) for g in range(11)] + [(w2a, d, 2816) for d in range(8)] + \
               [(win, i, 4096) for i in range(6)]
    blocks_F = [(w13a, g, 4096) for g in range(11)] + [(w2a, d, 2816) for d in range(8)] + \
               [(win, i, 4096) for i in range(10)] + [(woa, 0, 4096), (wor, 0, 4096), (wor, 1, 4096),
                                                       (wout, 0, 4096), (wout, 1, 4096)] + \
               [(w13b, g, 4096) for g in range(11)] + [(w2b, d, 2816) for d in range(8)]
    tile_blocks = []
    for k in range(NF_):
        tile_blocks += blocks_L * 3 + blocks_F
    tile_blocks += blocks_F
    total_blocks = len(tile_blocks)
    NBLK = total_blocks
    ws = {"issued": 0, "used": 0}

    def ws_issue():
        i = ws["issued"]
        if i >= total_blocks:
            return
        t, idx, E = tile_blocks[i % NBLK]
        s = i % NWS
        src_b = wdram_b if (t is w13a or t is w2a or t is win) else wdram2_b
        fw.dma(SP, wslot_c[s], wslot[s][:, 0:E], t[idx * 128:(idx + 1) * 128, 0:E], reads=[src_b],
               writes=[wslot_b[s]])
        ws["issued"] += 1

    def ws_next(expect, hold=0):
        i = ws["used"]
        assert tile_blocks[i % NBLK][0] is expect, "weight stream order mismatch"
        while ws["issued"] < min(total_blocks, i + NWS - hold):
            ws_issue()
        ws["used"] += 1
        s = i % NWS
        return wslot[s], wslot_b[s]

    def norm_sums(src, src_b, nch, n, sqbuf, sqbuf_b, reserve=False):
        ps, pb = pget_reserve() if reserve else pget()
        for c in range(nch):
            if c % 2 == 0:
                fw.op(POOL, lambda c=c: nc.gpsimd.tensor_tensor(out=sqbuf[:, c, 0:n], in0=src[:, c, 0:n],
                                                                in1=src[:, c, 0:n], op=ALU.mult),
                      reads=[src_b[c]], writes=[sqbuf_b[c]])
            else:
                fw.op(DVE, lambda c=c: nc.vector.tensor_tensor(out=sqbuf[:, c, 0:n], in0=src[:, c, 0:n],
                                                               in1=src[:, c, 0:n], op=ALU.mult),
                      reads=[src_b[c]], writes=[sqbuf_b[c]])
            fw.op(PE, lambda c=c: nc.tensor.matmul(ps[:, 0:n], lhsT=ones, rhs=sqbuf[:, c, 0:n], start=(c == 0),
                                                   stop=(c == nch - 1)),
                  reads=[sqbuf_b[c], ones_b], writes=[pb])
        return ps, pb

    def rmsnorm(src, src_b, nch, gcol, dim, out, out_b, n, sqbuf, sqbuf_b, pre=None):
        if pre is not None:
            ps, pb = pre
            reserved.discard(psum_b.index(pb))
        else:
            ps, pb = norm_sums(src, src_b, nch, n, sqbuf, sqbuf_b)
        fw.op(ACT, lambda: nc.scalar.activation(out=rstd[:, 0:n], in_=ps[:, 0:n], func=AF.Sqrt, scale=1.0 / dim,
                                                bias=EPS), reads=[pb], writes=[rstd_b])
        fw.op(DVE, lambda: nc.vector.reciprocal(out=rstd[:, 0:n], in_=rstd[:, 0:n]), reads=[rstd_b],
              writes=[rstd_b])
        for c in range(nch):
            fw.op(DVE, lambda c=c: nc.vector.scalar_tensor_tensor(out=out[:, c, 0:n], in0=src[:, c, 0:n],
                                                                  scalar=par[:, gcol + c:gcol + c + 1],
                                                                  in1=rstd[:, 0:n], op0=ALU.mult, op1=ALU.mult),
                  reads=[src_b[c], rstd_b, par_b], writes=[out_b[c]])

    def ffn_gen(w13t, w2t, n, sumsq=None):
        si = 0
        for g in range(11):
            wsl, wb = ws_next(w13t)
            wv = wsl.rearrange("p (e k j) -> p e k j", e=2, k=8)
            for e in range(2):
                f = 2 * g + e
                pa, pab = pget()
                pb_, pbb = pget()
                for k in range(8):
                    fw.op(PE, lambda k=k: nc.tensor.matmul(pa[:, 0:n], lhsT=wv[:, 0, k, e * 128:(e + 1) * 128],
                                                           rhs=uT[:, k, 0:n], start=(k == 0), stop=(k == 7)),
                          reads=[wb, uT_b[k]], writes=[pab])
                for k in range(8):
                    fw.op(PE, lambda k=k: nc.tensor.matmul(pb_[:, 0:n], lhsT=wv[:, 1, k, e * 128:(e + 1) * 128],
                                                           rhs=uT[:, k, 0:n], start=(k == 0), stop=(k == 7)),
                          reads=[wb, uT_b[k]], writes=[pbb])
                s = sil[si % 2]
                s_b = sil_b[si % 2]
                si += 1
                fw.op(ACT, lambda: nc.scalar.activation(out=s[:, 0:n], in_=pa[:, 0:n], func=AF.Silu),
                      reads=[pab], writes=[s_b])
                fw.op(DVE, lambda: nc.vector.tensor_tensor(out=gT[:, f, 0:n], in0=pb_[:, 0:n], in1=s[:, 0:n],
                                                           op=ALU.mult), reads=[pbb, s_b], writes=[gT_b[f]])
            yield
        for d in range(8):
            wsl, wb = ws_next(w2t)
            wv = wsl[:, 0:2816].rearrange("p (f j) -> p f j", f=NF)
            pd, pdb = pget()
            for f in range(NF):
                fw.op(PE, lambda f=f: nc.tensor.matmul(pd[:, 0:n], lhsT=wv[:, f, :], rhs=gT[:, f, 0:n],
                                                       start=(f == 0), stop=(f == NF - 1)),
                      reads=[wb, gT_b[f]], writes=[pdb])
            fw.op(DVE, lambda: nc.vector.scalar_tensor_tensor(out=xT[:, d, 0:n], in0=pd[:, 0:n], scalar=0.5,
                                                              in1=xT[:, d, 0:n], op0=ALU.mult, op1=ALU.add),
                  reads=[pdb, xT_b[d]], writes=[xT_b[d]])
            if sumsq is not None:
                sps, spb = sumsq
                if d > 0:
                    fw.op(PE, lambda: nc.tensor.matmul(sps[:, 0:n], lhsT=ones, rhs=sqv[(d - 1) % 2][:, 0:n],
                                                       start=(d == 1), stop=False),
                          reads=[sil_b[(d - 1) % 2], ones_b], writes=[spb])
                fw.op(ACT, lambda: nc.scalar.activation(out=sqv[d % 2][:, 0:n], in_=xT[:, d, 0:n], func=AF.Square),
                      reads=[xT_b[d]], writes=[sil_b[d % 2]])
                if d == 7:
                    fw.op(PE, lambda: nc.tensor.matmul(sps[:, 0:n], lhsT=ones, rhs=sqv[1][:, 0:n], start=False,
                                                       stop=True), reads=[sil_b[1], ones_b], writes=[spb])
            yield

    sqv = [sil[0].bitcast(BF16), sil[1].bitcast(BF16)]

    def ffn(w13t, w2t, n, side=None, sumsq=None):
        for _ in ffn_gen(w13t, w2t, n, sumsq):
            if side is not None:
                try:
                    next(side)
                except StopIteration:
                    side = None
        if side is not None:
            for _ in side:
                pass

    def fence(bufs):
        fw.op(POOL, lambda: nc.gpsimd.memset(fdummy[:, 0:1], 0.0), writes=bufs)

    def produce_kv(n, KTd, Vd, key0, seqname, tile_id, vflag=None):
        kb_ = kvbuf(seqname, tile_id)
        for p in range(4):
            ps, pb = pget()
            for kc in range(2):
                fw.op(PE, lambda kc=kc: nc.tensor.matmul(ps[:, 0:n], lhsT=wukv[:, 0, kc, p * 128:(p + 1) * 128],
                                                         rhs=ckvb[:, kc, 0:n], start=(kc == 0), stop=(kc == 1)),
                      reads=[wres_b, ckvb_b[kc]], writes=[pb])
            fw.op(ACT, lambda: nc.scalar.copy(out=knT[:, p, 0:n], in_=ps[:, 0:n]), reads=[pb], writes=[knT_b[p]])
            fw.dma(ACT, c_kn, KTd[0][p, :, key0:key0 + n], knT[:, p, 0:n], reads=[knT_b[p]], writes=[kb_])
        fw.dma(SP, c_krs, KTd[1][:, key0:key0 + n], krb[:, 0:n], reads=[krb_b], writes=[kb_])
        ntb = (n + 127) // 128
        for tb in range(ntb):
            rows = min(128, n - tb * 128)
            ps, pb = pget()
            for kc in range(2):
                fw.op(PE, lambda kc=kc: nc.tensor.matmul(ps[0:rows, 0:512],
                                                         lhsT=ckvb[:, kc, tb * 128:tb * 128 + rows],
                                                         rhs=wukv[:, 1, kc, :], start=(kc == 0), stop=(kc == 1)),
                      reads=[wres_b, ckvb_b[kc]], writes=[pb])
            psv = ps[0:rows, 0:512].rearrange("p (h e j) -> p h e j", h=4, e=2)
            vv = vst[0:rows, tb].rearrange("p (h e) j -> p h e j", e=2)
            fw.op(DVE, lambda: nc.vector.tensor_copy(out=vv[:, :, 0, 0:64], in_=psv[:, :, 0, :]), reads=[pb],
                  writes=[vst_b])
            fw.op(ACT, lambda: nc.scalar.copy(out=vv[:, :, 1, 64:128], in_=psv[:, :, 1, :]), reads=[pb],
                  writes=[vst_b])
            if vflag is not None:
                fw.op(DVE, lambda: nc.vector.tensor_scalar_mul(out=vv[:, :, 0, 64:128], in0=onesv[0:rows],
                                                               scalar1=flg[0:rows, vflag:vflag + 1]),
                      reads=[onesv_b, flg_b], writes=[vst_b])
                fw.op(DVE, lambda: nc.vector.tensor_scalar_mul(out=vv[:, :, 1, 0:64], in0=onesv[0:rows],
                                                               scalar1=flg[0:rows, vflag:vflag + 1]),
                      reads=[onesv_b, flg_b], writes=[vst_b])
        if n % 128 == 0:
            fw.dma(SP, c_vst, Vd[key0:key0 + n, :].rearrange("(tb p) c -> p tb c", p=128),
                   vst[:, 0:ntb].rearrange("p tb h j -> p tb (h j)"), reads=[vst_b], writes=[kb_])
        else:
            fw.dma(SP, c_vst, Vd[key0:key0 + n, :], vst[0:n, 0].rearrange("p h j -> p (h j)"), reads=[vst_b],
                   writes=[kb_])

    def attention(n, KTd, Vd, ctx, seqname):
        ld = [0]
        n_ctx = len(ctx)
        for hg_ in range(2):
            oacc = [(psum[4 + j], psum_b[4 + j]) for j in range(4)]
            first = [True] * 4
            slots = {}

            def load(ci):
                key0, nk, diag, tile_id = ctx[ci]
                s = ld[0] % NKS
                ld[0] += 1
                slots[ci] = s
                kb_ = kvbuf(seqname, tile_id)
                fw.dma(SP, c_ks[s], kslot[s][32:96, :, 0:nk],
                       KTd[0][2 * hg_:2 * hg_ + 2, :, key0:key0 + nk].rearrange("p (e j) s -> j (p e) s", e=2),
                       reads=[kb_], writes=[kslot_b[s]])
                fw.dma(SP, c_ks[s], kslot[s][0:32, :, 0:nk],
                       KTd[1][:, key0:key0 + nk].unsqueeze(1).broadcast_to([32, 4, nk]),
                       reads=[kb_], writes=[kslot_b[s]])
                nkb = (nk + 127) // 128
                if nk % 128 == 0:
                    fw.dma(SP, c_vs[s], vslot[s][:, 0:nkb].rearrange("p kb h j -> p kb (h j)"),
                           Vd[key0:key0 + nk, 512 * hg_:512 * hg_ + 512].rearrange("(kb p) c -> p kb c", p=128),
                           reads=[kb_], writes=[vslot_b[s]])
                else:
                    fw.dma(SP, c_vs[s], vslot[s][0:nk, 0].rearrange("p h j -> p (h j)"),
                           Vd[key0:key0 + nk, 512 * hg_:512 * hg_ + 512], reads=[kb_], writes=[vslot_b[s]])

            pend = []
            pt_i = [0]

            def do_pv(it):
                (s, kb, rows, hh, last, pt, ptb) = it
                oa, oab = oacc[hh]
                fw.op(PE, lambda: nc.tensor.matmul(oa[:, 0:n], lhsT=vslot[s][0:rows, kb, hh, :], rhs=pt[0:rows, 0:n],
                                                   start=first[hh], stop=last),
                      reads=[vslot_b[s], ptb], writes=[oab])
                first[hh] = False

            load(0)
            for ci, (key0, nk, diag, tile_id) in enumerate(ctx):
                s = slots[ci]
                nkb = (nk + 127) // 128
                idx = 0
                for kb in range(nkb):
                    rows = min(128, nk - kb * 128)
                    for hh in range(4):
                        last = (ci == n_ctx - 1) and (kb == nkb - 1)
                        h = 4 * hg_ + hh
                        sp_, spb = pget(0, 3)
                        fw.op(PE, lambda: nc.tensor.matmul(sp_[0:rows, 0:n],
                                                           lhsT=kslot[s][0:96, hh, kb * 128:kb * 128 + rows],
                                                           rhs=QT[0:96, h, 0:n], start=True, stop=True),
                              reads=[kslot_b[s], QT_b[h]], writes=[spb])
                        pi = pt_i[0] % NPT
                        pt_i[0] += 1
                        pt, ptb = PTs[pi], PT_b[pi]
                        if diag:
                            c0 = kb * 128
                            fw.op(ACT, lambda: nc.scalar.activation(out=pt[0:rows, c0:n], in_=sp_[0:rows, c0:n],
                                                                    func=AF.Exp), reads=[spb], writes=[ptb])
                            if c0 > 0:
                                fw.op(POOL, lambda: nc.gpsimd.memset(pt[0:rows, 0:c0], 0.0), writes=[ptb])
                            fw.op(POOL, lambda: nc.gpsimd.memset(pt[64:128, c0:c0 + 64], 0.0), writes=[ptb])
                        else:
                            fw.op(ACT, lambda: nc.scalar.activation(out=pt[0:rows, 0:n], in_=sp_[0:rows, 0:n],
                                                                    func=AF.Exp), reads=[spb], writes=[ptb])
                        pend.append((s, kb, rows, hh, last, pt, ptb))
                        if len(pend) > 2:
                            do_pv(pend.pop(0))
                        if idx == 2 and ci + 1 < n_ctx:
                            load(ci + 1)
                        idx += 1
            while pend:
                do_pv(pend.pop(0))
            for hh in range(4):
                h = 4 * hg_ + hh
                oa, oab = oacc[hh]
                if h % 2 == 0:
                    num, den = slice(0, 64), slice(64, 128)
                else:
                    num, den = slice(64, 128), slice(0, 64)
                fw.op(DVE, lambda: nc.vector.reciprocal(out=rec[num, 0:n], in_=oa[den, 0:n]), reads=[oab],
                      writes=[rec_b])
                fw.op(DVE, lambda: nc.vector.tensor_tensor(out=oT[num, h // 2, 0:n], in0=oa[num, 0:n],
                                                           in1=rec[num, 0:n], op=ALU.mult),
                      reads=[oab, rec_b], writes=[oT_b[h // 2]])

    def copy_act(out, in_, scale=None):
        if scale is None:
            return nc.scalar.activation(out=out, in_=in_, func=AF.Copy)
        return nc.scalar.activation(out=out, in_=in_, func=AF.Copy, scale=scale)

    def proj8(wv, col0, ncol, n, prow=128):
        ps, pb = pget()
        for k in range(8):
            fw.op(PE, lambda k=k: nc.tensor.matmul(ps[0:ncol, 0:n], lhsT=wv[:, k, col0:col0 + ncol],
                                                   rhs=uT[:, k, 0:n], start=(k == 0), stop=(k == 7)),
                  reads=[wv_b[0], uT_b[k]], writes=[pb])
        return ps, pb

    wv_b = [None]

    def rnn_s1(c, n, X, X_b, xc_, xc_b_):
        fw.op(POOL, lambda: nc.gpsimd.tensor_copy(out=X[:, 0:3], in_=halo[:, c, :]), reads=[halo_b[c]],
              writes=[X_b])
        fw.op(POOL, lambda: nc.gpsimd.tensor_copy(out=halo[:, c, :], in_=X[:, n:n + 3]), reads=[X_b],
              writes=[halo_b[c]])
        cw = PCW + 4 * c
        fw.op(DVE, lambda: nc.vector.tensor_scalar(out=xc_[:, 0:n], in0=X[:, 0:n], scalar1=par[:, cw:cw + 1],
                                                   scalar2=par[:, PCB + c:PCB + c + 1], op0=ALU.mult,
                                                   op1=ALU.add), reads=[X_b, par_b], writes=[xc_b_])
        for j in range(1, 4):
            fw.op(DVE, lambda j=j: nc.vector.scalar_tensor_tensor(out=xc_[:, 0:n], in0=X[:, j:j + n],
                                                                  scalar=par[:, cw + j:cw + j + 1],
                                                                  in1=xc_[:, 0:n], op0=ALU.mult, op1=ALU.add),
                  reads=[X_b, par_b, xc_b_], writes=[xc_b_])
        fw.op(POOL, lambda: nc.gpsimd.tensor_copy(out=xcb[:, 0:n], in_=xc_[:, 0:n]), reads=[xc_b_],
              writes=[xcb_b])

    def rnn_s2(seq, ti, c, n):
        pr, prb = pget()
        fw.op(PE, lambda: nc.tensor.matmul(pr[:, 0:n], lhsT=wgate[:, c, 0, :], rhs=xcb[:, 0:n], start=True,
                                           stop=True), reads=[wres_b, xcb_b], writes=[prb])
        pi_, pib = pget()
        fw.op(PE, lambda: nc.tensor.matmul(pi_[:, 0:n], lhsT=wgate[:, c, 1, :], rhs=xcb[:, 0:n], start=True,
                                           stop=True), reads=[wres_b, xcb_b], writes=[pib])
        fw.op(ACT, lambda: nc.scalar.activation(out=rbuf[:, 0:n], in_=pr[:, 0:n], func=AF.Tanh, scale=0.5,
                                                bias=hb[:, c:c + 1]),
              reads=[prb, hb_b], writes=[rbuf_b])
        fw.op(ACT, lambda: nc.scalar.activation(out=igbuf[:, 0:n], in_=pi_[:, 0:n], func=AF.Tanh, scale=0.5,
                                                bias=hb[:, 8 + c:9 + c]),
              reads=[pib, hb_b], writes=[igbuf_b])
        fw.op(DVE, lambda: nc.vector.tensor_scalar(out=rbuf[:, 0:n], in0=rbuf[:, 0:n], scalar1=0.5, scalar2=0.5,
                                                   op0=ALU.mult, op1=ALU.add), reads=[rbuf_b], writes=[rbuf_b])
        fw.op(DVE, lambda: nc.vector.tensor_scalar(out=igbuf[:, 0:n], in0=igbuf[:, 0:n], scalar1=0.5, scalar2=0.5,
                                                   op0=ALU.mult, op1=ALU.add), reads=[igbuf_b], writes=[igbuf_b])
        fw.op(ACT, lambda: nc.scalar.activation(out=a2buf[:, 0:n], in_=rbuf[:, 0:n], func=AF.Exp,
                                                scale=nsp[:, 8 + c:9 + c]), reads=[rbuf_b, nsp_b],
              writes=[a2buf_b])
        fw.op(ACT, lambda: nc.scalar.activation(out=rbuf[:, 0:n], in_=rbuf[:, 0:n], func=AF.Exp,
                                                scale=nsp[:, c:c + 1]), reads=[rbuf_b, nsp_b], writes=[rbuf_b])
        fw.op(DVE, lambda: nc.vector.tensor_scalar(out=a2buf[:, 0:n], in0=a2buf[:, 0:n], scalar1=-1.0, scalar2=1.0,
                                                   op0=ALU.mult, op1=ALU.add), reads=[a2buf_b], writes=[a2buf_b])
        fw.op(DVE, lambda: nc.vector.tensor_scalar_max(out=a2buf[:, 0:n], in0=a2buf[:, 0:n], scalar1=1e-30),
              reads=[a2buf_b], writes=[a2buf_b])
        fw.op(ACT, lambda: nc.scalar.activation(out=a2buf[:, 0:n], in_=a2buf[:, 0:n], func=AF.Ln),
              reads=[a2buf_b], writes=[a2buf_b])
        fw.op(ACT, lambda: nc.scalar.activation(out=a2buf[:, 0:n], in_=a2buf[:, 0:n], func=AF.Exp, scale=0.5),
              reads=[a2buf_b], writes=[a2buf_b])
        if seq["prompt"] and ti < 4:
            fw.op(DVE, lambda: nc.vector.tensor_scalar_mul(out=rbuf[:, 0:1], in0=rbuf[:, 0:1],
                                                           scalar1=flg[:, ti:ti + 1]),
                  reads=[rbuf_b, flg_b], writes=[rbuf_b])
            fw.op(DVE, lambda: nc.vector.tensor_scalar(out=a2buf[:, 0:1], in0=a2buf[:, 0:1],
                                                       scalar1=flg[:, ti:ti + 1], scalar2=flg[:, 4 + ti:5 + ti],
                                                       op0=ALU.mult, op1=ALU.add),
                  reads=[a2buf_b, flg_b], writes=[a2buf_b])

    def rnn_s3(c, n, xc_, xc_b_):
        fw.op(DVE, lambda: nc.vector.tensor_tensor(out=igbuf[:, 0:n], in0=igbuf[:, 0:n], in1=xc_[:, 0:n],
                                                   op=ALU.mult), reads=[igbuf_b, xc_b_], writes=[igbuf_b])
        fw.op(DVE, lambda: nc.vector.tensor_tensor(out=igbuf[:, 0:n], in0=igbuf[:, 0:n], in1=a2buf[:, 0:n],
                                                   op=ALU.mult), reads=[igbuf_b, a2buf_b], writes=[igbuf_b])
        fw.op(DVE, lambda: nc.vector.tensor_tensor_scan(out=hbuf[:, 0:n], data0=rbuf[:, 0:n], data1=igbuf[:, 0:n],
                                                        initial=hst[:, c:c + 1], op0=ALU.mult, op1=ALU.add),
              reads=[rbuf_b, igbuf_b, hst_b[c]], writes=[hbuf_b])
        fw.op(POOL, lambda: nc.gpsimd.tensor_copy(out=hst[:, c:c + 1], in_=hbuf[:, n - 1:n]), reads=[hbuf_b],
              writes=[hst_b[c]])

    def rnn_chunk(seq, ti, c, n, psx, psx_b, psg, psg_b):
        X, X_b = Xb[c % 2], Xb_b[c % 2]
        fw.op(ACT, lambda: copy_act(X[:, 3:3 + n], psx[:, 0:n]), reads=[psx_b], writes=[X_b])
        rnn_s1(c, n, X, X_b, xc, xc_b)
        rnn_s2(seq, ti, c, n)
        rnn_s3(c, n, xc, xc_b)
        g_, g_b = gg[c % 2], gg_b[c % 2]
        fw.op(ACT, lambda: nc.scalar.activation(out=g_[:, 0:n], in_=psg[:, 0:n], func=AF.Gelu_apprx_tanh),
              reads=[psg_b], writes=[g_b])
        fw.op(DVE, lambda: nc.vector.tensor_tensor(out=hg[:, c, 0:n], in0=hbuf[:, 0:n], in1=g_[:, 0:n],
                                                   op=ALU.mult), reads=[hbuf_b, g_b], writes=[hg_b[c]])

    def rnn_chain_gen(seq, ti, n):
        for step in range(10):
            if 0 <= step - 2 < 8:
                c = step - 2
                rnn_s3(c, n, xcL[c % 2], xcL_b[c % 2])
            if 0 <= step - 1 < 8:
                rnn_s2(seq, ti, step - 1, n)
            if step < 8:
                c = step
                rnn_s1(c, n, xr_all[:, c, :], xr_b[c], xcL[c % 2], xcL_b[c % 2])
            yield

    c_xa = fw.dma_ctr("c_xa")

    def light_tile(seq, ti, tok0, n, key0, nxt_tok0):
        sname = seq["name"]
        pre1 = seq.pop("pre_norm1", None)
        if pre1 is None:
            fw.dma(SP, c_x, xT[:, :, 0:n], seq["x"].rearrange("(c p) s -> p c s", p=128)[:, :, tok0:tok0 + n],
                   writes=xT_b)
        fw.dma(SP, c_cs, cs[:, :, 0:n], seq["rope"][:, :, tok0:tok0 + n], writes=[cs_b])
        rmsnorm(xT, xT_b, 8, PG1, D, uT, uT_b, n, gT, gT_b, pre=pre1)
        ss = pget_reserve()
        ffn(w13a, w2a, n, side=seq.pop("pending", None), sumsq=ss)
        rmsnorm(xT, xT_b, 8, PGM, D, uT, uT_b, n, gT, gT_b, pre=ss)
        fw.dma(ACT, c_xa, xT[:, :, 0:n],
               seq["x"].rearrange("(c p) s -> p c s", p=128)[:, :, nxt_tok0:nxt_tok0 + n], writes=xT_b)
        wsl, wb = ws_next(win)
        wv = wsl.rearrange("p (k j) -> p k j", k=8)
        wv_b[0] = wb
        for e in range(2):
            ps, pb = proj8(wv, 384 + 32 * e, 32, n)
            fw.op(ACT, lambda: copy_act(krraw[:, e, 0:n], ps[0:32, 0:n]), reads=[pb], writes=[krraw_b[e]])
        wsl, wb = ws_next(win)
        wv = wsl.rearrange("p (k j) -> p k j", k=8)
        wv_b[0] = wb
        for c in range(2):
            ps, pb = proj8(wv, c * 128, 128, n)
            fw.op(ACT, lambda: copy_act(kvraw[:, c, 0:n], ps[:, 0:n]), reads=[pb], writes=[kvraw_b[c]])
        fence(alias_b)

        def rx_blocks(c2s):
            for c2 in c2s:
                wsl, wb = ws_next(win)
                wv = wsl.rearrange("p (k j) -> p k j", k=8)
                wv_b[0] = wb
                for e in range(2):
                    c = 2 * c2 + e
                    psx, psx_b = proj8(wv, e * 128, 128, n)
                    fw.op(ACT, lambda: copy_act(xr_all[:, c, 3:3 + n], psx[:, 0:n]), reads=[psx_b],
                          writes=[xr_b[c]])

        rx_blocks([0, 1])
        fw.op(DVE, lambda: nc.vector.tensor_tensor(out=rt[0][:, 0:n], in0=krraw[:, 0, 0:n], in1=cs[:, 0, 0:n],
                                                   op=ALU.mult), reads=[krraw_b[0], cs_b], writes=[rt_b[0]])
        fw.op(DVE, lambda: nc.vector.tensor_tensor(out=rt[1][:, 0:n], in0=krraw[:, 1, 0:n], in1=cs[:, 1, 0:n],
                                                   op=ALU.mult), reads=[krraw_b[1], cs_b], writes=[rt_b[1]])
        fw.op(DVE, lambda: nc.vector.tensor_tensor(out=krb[:, 0:n], in0=rt[0][:, 0:n], in1=rt[1][:, 0:n],
                                                   op=ALU.add), reads=[rt_b[0], rt_b[1]], writes=[krb_b])
        rmsnorm(kvraw, kvraw_b, 2, PGKV, 256, ckvb, ckvb_b, n, gT, gT_b)
        rx_blocks([2, 3])
        produce_kv(n, seq["KT"], seq["V"], key0, sname, ti, vflag=(8 + ti) if ti < 4 else None)
        seq["pre_norm1"] = norm_sums(xT, xT_b, 8, n, gT, gT_b, reserve=True)
        seq["pending"] = rnn_chain_gen(seq, ti, n)

    def tile(seq, ti, tok0, n, key0, ctx, light=False, out0=0):
        sname = seq["name"]
        pre1 = seq.pop("pre_norm1", None)
        if pre1 is None:
            fw.dma(SP, c_x, xT[:, :, 0:n], seq["x"].rearrange("(c p) s -> p c s", p=128)[:, :, tok0:tok0 + n],
                   writes=xT_b)
        fw.dma(SP, c_cs, cs[:, :, 0:n], seq["rope"][:, :, tok0:tok0 + n], writes=[cs_b])
        rmsnorm(xT, xT_b, 8, PG1, D, uT, uT_b, n, gT, gT_b, pre=pre1)
        ss = pget_reserve()
        ffn(w13a, w2a, n, side=seq.pop("pending", None), sumsq=ss)
        rmsnorm(xT, xT_b, 8, PGM, D, uT, uT_b, n, gT, gT_b, pre=ss)
        wsl, wb = ws_next(win)
        wv = wsl.rearrange("p (k j) -> p k j", k=8)
        wv_b[0] = wb
        for c in range(0 if light else 3):
            ps, pb = proj8(wv, c * 128, 128, n)
            fw.op(ACT, lambda: copy_act(zq[:, c, 0:n], ps[:, 0:n]), reads=[pb], writes=[zq_b[c]])
        for e in range(2):
            ps, pb = proj8(wv, 384 + 32 * e, 32, n)
            fw.op(ACT, lambda: copy_act(krraw[:, e, 0:n], ps[0:32, 0:n]), reads=[pb], writes=[krraw_b[e]])
        wsl, wb = ws_next(win)
        wv = wsl.rearrange("p (k j) -> p k j", k=8)
        wv_b[0] = wb
        for c in range(2):
            ps, pb = proj8(wv, c * 128, 128, n)
            fw.op(ACT, lambda: copy_act(kvraw[:, c, 0:n], ps[:, 0:n]), reads=[pb], writes=[kvraw_b[c]])
        fw.op(DVE, lambda: nc.vector.tensor_tensor(out=rt[0][:, 0:n], in0=krraw[:, 0, 0:n], in1=cs[:, 0, 0:n],
                                                   op=ALU.mult), reads=[krraw_b[0], cs_b], writes=[rt_b[0]])
        fw.op(DVE, lambda: nc.vector.tensor_tensor(out=rt[1][:, 0:n], in0=krraw[:, 1, 0:n], in1=cs[:, 1, 0:n],
                                                   op=ALU.mult), reads=[krraw_b[1], cs_b], writes=[rt_b[1]])
        fw.op(DVE, lambda: nc.vector.tensor_tensor(out=krf[:, 0:n], in0=rt[0][:, 0:n], in1=rt[1][:, 0:n],
                                                   op=ALU.add), reads=[rt_b[0], rt_b[1]], writes=[krf_b])
        fw.op(POOL, lambda: nc.gpsimd.tensor_copy(out=krb[:, 0:n], in_=krf[:, 0:n]), reads=[krf_b], writes=[krb_b])
        if not light:
            fw.dma(POOL, c_kr, seq["kr_out"][:, out0:out0 + n], krf[:, 0:n], reads=[krf_b])
        if not light:
            fence(alias_b)
            rmsnorm(zq, zq_b, 3, PGQ, 384, qn, qn_b, n, gT, gT_b)
        for h in range(0 if light else 8):
            ps, pb = pget()
            for kc in range(3):
                fw.op(PE, lambda kc=kc: nc.tensor.matmul(ps[:, 0:n], lhsT=wuq[:, kc, h, :], rhs=qn[:, kc, 0:n],
                                                         start=(kc == 0), stop=(kc == 2)),
                      reads=[wres_b, qn_b[kc]], writes=[pb])
            fw.op(ACT, lambda: copy_act(QT[0:96, h, 0:n], ps[0:96, 0:n], QSCALE), reads=[pb], writes=[QT_b[h]])
            fw.op(ACT, lambda: copy_act(krraw[:, 0, 0:n], ps[0:32, 0:n], QSCALE), reads=[pb], writes=[krraw_b[0]])
            fw.op(ACT, lambda: copy_act(krraw[:, 1, 0:n], ps[96:128, 0:n], QSCALE), reads=[pb], writes=[krraw_b[1]])
            fw.op(DVE, lambda: nc.vector.tensor_tensor(out=rt[0][:, 0:n], in0=krraw[:, 0, 0:n], in1=cs[:, 0, 0:n],
                                                       op=ALU.mult), reads=[krraw_b[0], cs_b], writes=[rt_b[0]])
            fw.op(DVE, lambda: nc.vector.tensor_tensor(out=rt[1][:, 0:n], in0=krraw[:, 1, 0:n], in1=cs[:, 1, 0:n],
                                                       op=ALU.mult), reads=[krraw_b[1], cs_b], writes=[rt_b[1]])
            fw.op(DVE, lambda: nc.vector.tensor_tensor(out=QT[0:32, h, 0:n], in0=rt[0][:, 0:n], in1=rt[1][:, 0:n],
                                                       op=ALU.add), reads=[rt_b[0], rt_b[1]], writes=[QT_b[h]])
        rmsnorm(kvraw, kvraw_b, 2, PGKV, 256, ckv, ckv_b, n, gT, gT_b)
        for c in range(2):
            fw.op(POOL, lambda c=c: nc.gpsimd.tensor_copy(out=ckvb[:, c, 0:n], in_=ckv[:, c, 0:n]),
                  reads=[ckv_b[c]], writes=[ckvb_b[c]])
        if not light:
            fw.dma(POOL, c_kv, seq["kv_out"].rearrange("(c p) s -> p c s", p=128)[:, :, out0:out0 + n],
                   ckv[:, :, 0:n], reads=ckv_b)
        produce_kv(n, seq["KT"], seq["V"], key0, sname, ctx[0][3],
                   vflag=(8 + ti) if (seq["prompt"] and ti < 4) else None)
        fence(alias_b)
        wvh = [None]
        for step in range(10):
            if 0 <= step - 2 < 8:
                c = step - 2
                rnn_s3(c, n, xcL[c % 2], xcL_b[c % 2])
                g_, g_b = gg[c % 2], gg_b[c % 2]
                fw.op(DVE, lambda: nc.vector.tensor_tensor(out=hg[:, c, 0:n], in0=hbuf[:, 0:n], in1=g_[:, 0:n],
                                                           op=ALU.mult), reads=[hbuf_b, g_b], writes=[hg_b[c]])
            if 0 <= step - 1 < 8:
                rnn_s2(seq, ti, step - 1, n)
            if step < 8:
                c = step
                e = c % 2
                if e == 0:
                    wsl, wb = ws_next(win)
                    wvh[0] = wsl.rearrange("p (k j) -> p k j", k=8)
                    wv_b[0] = wb
                wv = wvh[0]
                psx, psx_b = proj8(wv, e * 128, 128, n)
                psg, psg_b = proj8(wv, 256 + e * 128, 128, n)
                X, X_b = Xb[c % 2], Xb_b[c % 2]
                fw.op(ACT, lambda: copy_act(X[:, 3:3 + n], psx[:, 0:n]), reads=[psx_b], writes=[X_b])
                g_, g_b = gg[c % 2], gg_b[c % 2]
                fw.op(ACT, lambda: nc.scalar.activation(out=g_[:, 0:n], in_=psg[:, 0:n], func=AF.Gelu_apprx_tanh),
                      reads=[psg_b], writes=[g_b])
                rnn_s1(c, n, X, X_b, xcL[c % 2], xcL_b[c % 2])
        fence(alias_b)
        for c2 in range(4):
            wsl, wb = ws_next(win)
            wv = wsl.rearrange("p (k j) -> p k j", k=8)
            wv_b[0] = wb
            for e in range(2):
                c = 2 * c2 + e
                for gi in range(2):
                    ps, pb = proj8(wv, gi * 256 + e * 128, 128, n)
                    fw.op(ACT, lambda: nc.scalar.activation(out=gT[:, 8 * gi + c, 0:n], in_=ps[:, 0:n],
                                                            func=AF.Sigmoid), reads=[pb],
                          writes=[gT_b[8 * gi + c]])
        attention(n, seq["KT"], seq["V"], ctx, sname)
        woa_s, woa_b = ws_next(woa)
        woav = woa_s.rearrange("p (k j) -> p k j", k=4)
        wor_s = [ws_next(wor, 1), ws_next(wor, 2)]
        for d in range(8):
            ps, pb = pget()
            for kc in range(4):
                fw.op(PE, lambda kc=kc: nc.tensor.matmul(ps[:, 0:n], lhsT=woav[:, kc, d * 128:(d + 1) * 128],
                                                         rhs=oT[:, kc, 0:n], start=(kc == 0), stop=(kc == 3)),
                      reads=[woa_b, oT_b[kc]], writes=[pb])
            fw.op(DVE, lambda: nc.vector.tensor_tensor(out=ma[:, 0:n], in0=ps[:, 0:n], in1=gT[:, d, 0:n],
                                                       op=ALU.mult), reads=[pb, gT_b[d]], writes=[ma_b])
            wsl, wb = wor_s[d // 4]
            wv = wsl.rearrange("p (k j) -> p k j", k=8)
            ps2, pb2 = pget()
            for kc in range(8):
                fw.op(PE, lambda kc=kc: nc.tensor.matmul(ps2[:, 0:n], lhsT=wv[:, kc, (d % 4) * 128:(d % 4 + 1) * 128],
                                                         rhs=hg[:, kc, 0:n], start=(kc == 0), stop=(kc == 7)),
                      reads=[wb, hg_b[kc]], writes=[pb2])
            s_, s_b = sil[d % 2], sil_b[d % 2]
            fw.op(DVE, lambda: nc.vector.tensor_tensor(out=s_[:, 0:n], in0=ps2[:, 0:n], in1=gT[:, 8 + d, 0:n],
                                                       op=ALU.mult), reads=[pb2, gT_b[8 + d]], writes=[s_b])
            fw.op(POOL, lambda: nc.gpsimd.tensor_tensor(out=uT[:, d, 0:n], in0=s_[:, 0:n], in1=ma[:, 0:n],
                                                        op=ALU.add), reads=[s_b, ma_b], writes=[uT_b[d]])
        wo_s = [ws_next(wout), ws_next(wout, 1)]
        for d in range(8):
            wsl, wb = wo_s[d // 4]
            wv = wsl.rearrange("p (k j) -> p k j", k=8)
            ps, pb = pget()
            for kc in range(8):
                fw.op(PE, lambda kc=kc: nc.tensor.matmul(ps[:, 0:n], lhsT=wv[:, kc, (d % 4) * 128:(d % 4 + 1) * 128],
                                                         rhs=uT[:, kc, 0:n], start=(kc == 0), stop=(kc == 7)),
                      reads=[wb, uT_b[kc]], writes=[pb])
            fw.op(DVE, lambda: nc.vector.tensor_tensor(out=xT[:, d, 0:n], in0=ps[:, 0:n], in1=xT[:, d, 0:n],
                                                       op=ALU.add), reads=[pb, xT_b[d]], writes=[xT_b[d]])
        rmsnorm(xT, xT_b, 8, PG2, D, uT, uT_b, n, gT, gT_b)
        ss = pget_reserve()
        ffn(w13b, w2b, n, sumsq=ss)
        rmsnorm(xT, xT_b, 8, PGF, D, xT, xT_b, n, gT, gT_b, pre=ss)
        fw.dma(POOL, c_out, seq["y_out"].rearrange("(c p) s -> p c s", p=128)[:, :, out0:out0 + n], xT[:, :, 0:n],
               reads=xT_b)

    seq_p = dict(name="p", prompt=True, x=xp, rope=rope_p, kr_out=kr_p, kv_out=kvl_p, y_out=y_p, KT=(KN_p, KR_p), V=V_p)
    seq_s = dict(name="s", prompt=False, x=xs, rope=rope_s, kr_out=kr_s, kv_out=kvl_s, y_out=y_s, KT=(KN_s, KR_s), V=V_s)

    halo_flat = halo.rearrange("p c j -> p (c j)")
    for c in range(8):
        fw.op(POOL, lambda c=c: nc.gpsimd.memset(halo[:, c, :], 0.0), writes=[halo_b[c]])
        fw.op(POOL, lambda c=c: nc.gpsimd.memset(hst[:, c:c + 1], 0.0), writes=[hst_b[c]])
    for ti in range(NT):
        ctx = [(ti * TT, TT, True, ti)] + [(j * TT, TT, False, j) for j in range(ti)]
        if ti % 4 == 3:
            tile(seq_p, ti, ti * TT, TT, ti * TT, ctx, light=False, out0=(ti // 4) * TT)
        else:
            light_tile(seq_p, ti, ti * TT, TT, ti * TT, (ti + 1) * TT)
        if ti == 0:
            late_casts()
    assert "pending" not in seq_p and "pre_norm1" not in seq_p
    fw.dma(POOL, c_st, conv_p, halo_flat, reads=halo_b)
    fw.dma(POOL, c_st, h_p, hst, reads=hst_b)
    for j in range(PAST // TT):
        fw.dma(POOL, c_misc, ckvb[:, :, 0:TT],
               ckv_c.rearrange("(c p) s -> p c s", p=128)[:, :, j * TT:(j + 1) * TT], writes=ckvb_b)
        fw.dma(POOL, c_misc, krb[:, 0:TT], ckr_c[:, j * TT:(j + 1) * TT], writes=[krb_b])
        produce_kv(TT, (KN_s, KR_s), V_s, j * TT, "s", j)
    fw.dma(SP, c_msp, halo_flat, sconv, reads=[], writes=halo_b)
    fw.dma(SP, c_msp, hst, srg, reads=[], writes=hst_b)
    ctx = [(PAST, DEC, False, 2), (0, TT, False, 0), (TT, TT, False, 1)]
    tile(seq_s, 0, 0, DEC, PAST, ctx)
    fw.dma(POOL, c_st, conv_s, halo_flat, reads=halo_b)
    fw.dma(POOL, c_st, h_s, hst, reads=hst_b)
    fw.finish(SP)
    build_program.stats = dict(n_inst=fw.n_inst, n_wait=fw.n_wait, sbuf_left=nc.sbuf_bytes_remaining)
    return nc


_CACHE = {}


def kernel(x_prompt, x_sample, cache_kv_latent, cache_k_rope, state_conv, state_rglru,
           norm_ffn1, w1_ffn1, w3_ffn1, w2_ffn1, norm_mix, w_in,
           norm_q, w_uq, norm_kv, w_ukv, w_o_attn,
           conv_w, conv_b, w_rgate, b_rgate, w_igate, b_igate, lru_lambda, w_o_rnn,
           w_out, norm_ffn2, w1_ffn2, w3_ffn2, w2_ffn2, norm_final):
    f = lambda a: np.asarray(a, dtype=np.float32)
    x_prompt = f(x_prompt)
    x_sample = f(x_sample)
    Bp, S_P, _ = x_prompt.shape
    Bs = x_sample.shape[0]
    n_cores = 8
    assert Bs == n_cores and S_P % TT == 0
    wd = dict(norm_ffn1=f(norm_ffn1), norm_mix=f(norm_mix), norm_q=f(norm_q), norm_kv=f(norm_kv),
              norm_ffn2=f(norm_ffn2), norm_final=f(norm_final), conv_w=f(conv_w), conv_b=f(conv_b),
              b_rgate=f(b_rgate), b_igate=f(b_igate), lru_lambda=f(lru_lambda))
    shared = {
        "params": host_params(wd),
        "w13a": host_w13(f(w1_ffn1)[0], f(w3_ffn1)[0]),
        "w2a": host_w2(f(w2_ffn1)[0]),
        "w13b": host_w13(f(w1_ffn2)[0], f(w3_ffn2)[0]),
        "w2b": host_w2(f(w2_ffn2)[0]),
        "win": host_win(f(w_in)[0]),
        "wres": host_wres(f(w_uq)[0], f(w_ukv)[0], f(w_rgate)[0], f(w_igate)[0]),
        "woa": host_kc(f(w_o_attn)[0], 1024),
        "wor": host_kc(f(w_o_rnn)[0], 512),
        "wout": host_kc(f(w_out)[0], 512),
        "rope_s": rope_tables(PAST + np.arange(DEC)),
    }
    NT = S_P // TT
    assert NT % 4 == 0 and Bp * 4 == n_cores
    NF_ = NT // 4
    xpT = [np.ascontiguousarray(x_prompt[b].T) for b in range(Bp)]
    ckv = f(cache_kv_latent)[0]
    ckr = f(cache_k_rope)[0]
    sc = f(state_conv)[0]
    sh = f(state_rglru)[0]
    in_maps = []
    for c in range(n_cores):
        b, j = c // 4, c % 4
        m = dict(shared)
        xc_ = np.zeros((D, S_P), np.float32)
        pos = np.zeros((S_P,), np.int64)
        flags = np.zeros((128, 12), np.float32)
        for s_ in range(NT):
            g = s_ - (3 - j)
            if g >= 0:
                xc_[:, s_ * TT:(s_ + 1) * TT] = xpT[b][:, g * TT:(g + 1) * TT]
                pos[s_ * TT:(s_ + 1) * TT] = g * TT + np.arange(TT)
            if s_ < 4:
                flags[:, s_] = 0.0 if g == 0 else 1.0
                flags[:, 4 + s_] = 1.0 if g == 0 else 0.0
                flags[:, 8 + s_] = 1.0 if g >= 0 else 0.0
        m["xp"] = xc_
        m["rope_p"] = rope_tables(pos)
        m["flags"] = flags
        m["xs"] = np.ascontiguousarray(x_sample[c].T)
        m["ckv_c"] = np.ascontiguousarray(ckv[c].T)
        m["ckr_c"] = np.ascontiguousarray(ckr[c].T)
        m["sconv"] = np.ascontiguousarray(sc[c].reshape(3, 8, 128).transpose(2, 1, 0).reshape(128, 24))
        m["srg"] = np.ascontiguousarray(sh[c].reshape(8, 128).T)
        in_maps.append(m)
    if S_P not in _CACHE:
        _CACHE[S_P] = build_program(S_P)
    nc = _CACHE[S_P]
    res = run_bass_kernel_spmd(nc, in_maps, core_ids=list(range(n_cores)))
    R = res.results

    def unconv(a):
        return np.ascontiguousarray(a.reshape(128, 8, 3).transpose(2, 1, 0).reshape(3, 1024))

    def unh(a):
        return np.ascontiguousarray(a.T.reshape(1024))

    def gather(name, width):
        out = np.zeros((Bp, S_P, width), np.float32)
        for c in range(n_cores):
            b, j = c // 4, c % 4
            a = R[c][name]
            for k in range(NF_):
                g = 4 * k + j
                out[b, g * TT:(g + 1) * TT, :] = a[:, k * TT:(k + 1) * TT].T
        return out

    y_prompt = gather("y_p", D)
    y_sample = np.stack([np.ascontiguousarray(R[c]["y_s"].T) for c in range(n_cores)], 0)
    kvl_prompt = gather("kvl_p", 256)[None]
    kr_prompt = gather("kr_p", 32)[None]
    conv_prompt = np.stack([unconv(R[4 * b + 3]["conv_p"]) for b in range(Bp)], 0)[None]
    h_prompt = np.stack([unh(R[4 * b + 3]["h_p"]) for b in range(Bp)], 0)[None]
    kvl_sample = np.stack([np.ascontiguousarray(R[c]["kvl_s"].T) for c in range(n_cores)], 0)[None]
    kr_sample = np.stack([np.ascontiguousarray(R[c]["kr_s"].T) for c in range(n_cores)], 0)[None]
    conv_sample = np.stack([unconv(R[c]["conv_s"]) for c in range(n_cores)], 0)[None]
    h_sample = np.stack([unh(R[c]["h_s"]) for c in range(n_cores)], 0)[None]
    outs = (y_prompt, y_sample, kvl_prompt, kr_prompt, conv_prompt, h_prompt,
            kvl_sample, kr_sample, conv_sample, h_sample)
    return tuple(np.asarray(o, dtype=np.float32) for o in outs)
```
